# Optimizing a Trainium2 kernel written in Bass

```python
import math
import jax, jax.numpy as jnp
from jax import lax
import numpy as np


D_MODEL = 2048
BATCH = 4
SEQ = 8192
DEPTH = 2

EPS = 1e-6
ROPE_THETA = 10000.0

SSD_HEADS = 16
SSD_HEAD_DIM = 64
SSD_WIDTH = SSD_HEADS * SSD_HEAD_DIM
SSD_GROUPS = 2
SSD_STATE = 128
SSD_CONV = 4
SSD_CHUNK = 128
SSD_CONV_CH = SSD_WIDTH + 2 * SSD_GROUPS * SSD_STATE

ATT_HEADS = 8
ATT_HEAD_DIM = 64
ATT_WIDTH = ATT_HEADS * ATT_HEAD_DIM
DILATED_PAIRS = ((128, 1), (512, 4), (2048, 16))
ATT_BLOCK = 128

RET_HEADS = 4
RET_QK_DIM = 64
RET_V_DIM = 128
RET_QK_WIDTH = RET_HEADS * RET_QK_DIM
RET_V_WIDTH = RET_HEADS * RET_V_DIM
RET_CHUNK = 128

MIX_WIDTH = SSD_WIDTH + ATT_WIDTH + RET_V_WIDTH
IN_SPLITS = (SSD_WIDTH, SSD_CONV_CH, SSD_HEADS,
             ATT_WIDTH, ATT_WIDTH, ATT_WIDTH,
             RET_QK_WIDTH, RET_QK_WIDTH, RET_V_WIDTH, RET_V_WIDTH)
IN_WIDTH = sum(IN_SPLITS)
D_FF = ((8 * D_MODEL // 3 + 255) // 256) * 256

kernel_name = 'hybrid_ssd_dilated_retention_block'


def rms_norm(x, g):
    xf = x.astype(jnp.float32)
    y = xf * lax.rsqrt(jnp.mean(xf * xf, axis=-1, keepdims=True) + EPS)
    return (y * g.astype(jnp.float32)).astype(x.dtype)


def grouped_rms(x, groups, gain):
    b, s, w = x.shape
    xg = x.astype(jnp.float32).reshape(b, s, groups, w // groups)
    xg = xg * lax.rsqrt(jnp.mean(xg * xg, axis=-1, keepdims=True) + EPS)
    return xg.reshape(b, s, w) * gain.astype(jnp.float32)


def rope_tables(seq, dim):
    pos = jnp.arange(seq, dtype=jnp.float32)
    inv = ROPE_THETA ** (-jnp.arange(0, dim, 2, dtype=jnp.float32) / dim)
    ang = pos[:, None] * inv[None, :]
    return jnp.cos(ang), jnp.sin(ang)


def apply_rope(t, cos, sin):
    half = t.shape[-1] // 2
    t1, t2 = t[..., :half], t[..., half:]
    c = cos[None, :, None, :]
    s = sin[None, :, None, :]
    return jnp.concatenate([t1 * c - t2 * s, t1 * s + t2 * c], axis=-1)


def causal_dwconv(u, w, b):
    k = w.shape[0]
    s = u.shape[1]
    up = jnp.pad(u, ((0, 0), (k - 1, 0), (0, 0)))
    return b + sum(up[:, i:i + s] * w[i] for i in range(k))


def ssd_chunked(x, dt, a_neg, bm, cm):
    bsz, s, h, p = x.shape
    g, n = bm.shape[2], bm.shape[3]
    j = h // g
    l = SSD_CHUNK
    c = s // l
    xg = (x * dt[..., None]).reshape(bsz, c, l, g, j, p)
    da = (dt * a_neg).reshape(bsz, c, l, g, j).transpose(0, 1, 3, 4, 2)
    acum = jnp.cumsum(da, axis=-1)
    bc = bm.reshape(bsz, c, l, g, n)
    cc = cm.reshape(bsz, c, l, g, n)
    causal = jnp.tril(jnp.ones((l, l), dtype=bool))
    seg = acum[..., :, None] - acum[..., None, :]
    decay_in = jnp.exp(jnp.where(causal, seg, -jnp.inf))
    cb = jnp.einsum('bclgn,bcsgn->bcgls', cc, bc)
    y_diag = jnp.einsum('bcgjls,bcsgjp->bclgjp', cb[:, :, :, None] * decay_in, xg)
    decay_to_end = jnp.exp(acum[..., -1:] - acum)
    xw = xg * decay_to_end.transpose(0, 1, 4, 2, 3)[..., None]
    chunk_states = jnp.einsum('bclgn,bclgjp->bcgjpn', bc, xw)
    chunk_decay = jnp.exp(acum[..., -1])

    def step(state, inp):
        st_c, dec_c = inp
        return state * dec_c[..., None, None] + st_c, state

    init = jnp.zeros((bsz, g, j, p, n), jnp.float32)
    _, prev = lax.scan(step, init, (chunk_states.transpose(1, 0, 2, 3, 4, 5),
                                    chunk_decay.transpose(1, 0, 2, 3)))
    prev = prev.transpose(1, 0, 2, 3, 4, 5)
    y_off = jnp.einsum('bclgn,bcgjpn->bclgjp', cc, prev) * \
        jnp.exp(acum).transpose(0, 1, 4, 2, 3)[..., None]
    return (y_diag + y_off).reshape(bsz, s, h, p)


def dilated_branch(q, k, v, window, dilation):
    bsz, s, h, hd = q.shape
    steps = window // dilation
    blk = ATT_BLOCK
    L = s // dilation
    nb = -(-L // blk)
    pad = nb * blk - L

    def to_sub(t):
        t = t.reshape(bsz, L, dilation, h, hd).transpose(0, 2, 3, 1, 4)
        return jnp.pad(t, ((0, 0), (0, 0), (0, 0), (0, pad), (0, 0)))

    def key_blocks(t):
        t = jnp.pad(to_sub(t), ((0, 0), (0, 0), (0, 0), (blk, 0), (0, 0)))
        t = t.reshape(bsz, dilation, h, nb + 1, blk, hd)
        return jnp.concatenate([t[:, :, :, :-1], t[:, :, :, 1:]], axis=4)

    qb = to_sub(q).reshape(bsz, dilation, h, nb, blk, hd)
    kb = key_blocks(k)
    vb = key_blocks(v)
    sc = jnp.einsum('brhnqd,brhnkd->brhnqk', qb, kb).astype(jnp.float32)
    iq = jnp.arange(blk)[:, None]
    ik = jnp.arange(2 * blk)[None, :]
    dist = blk + iq - ik
    kpos = (jnp.arange(nb) * blk - blk)[:, None, None] + ik[None]
    valid = (dist >= 0) & (dist <= steps) & (kpos >= 0)
    sc = jnp.where(valid, sc, -jnp.inf)
    mx = jnp.max(sc, axis=-1, keepdims=True)
    pr = jnp.exp(sc - mx)
    den = jnp.sum(pr, axis=-1, keepdims=True)
    out = jnp.einsum('brhnqk,brhnkd->brhnqd', pr, vb.astype(jnp.float32)) / den
    lse = (mx + jnp.log(den))[..., 0]

    def from_sub(t):
        t = t.reshape(bsz, dilation, h, nb * blk, *t.shape[5:])[:, :, :, :L]
        t = jnp.moveaxis(t, 3, 1)
        return t.reshape(bsz, s, h, *t.shape[4:])

    return from_sub(out), from_sub(lse)


def retention_chunked(q, k, v):
    bsz, s, h, dk = q.shape
    dv = v.shape[-1]
    l = RET_CHUNK
    c = s // l
    log_gamma = jnp.log1p(-jnp.exp2(-5.0 - jnp.arange(h, dtype=jnp.float32)))
    idx = jnp.arange(l, dtype=jnp.float32)
    rel = idx[:, None] - idx[None, :]
    decay_in = jnp.where(rel >= 0,
                         jnp.exp(jnp.maximum(rel, 0.0)[None] * log_gamma[:, None, None]),
                         0.0)
    qc = q.reshape(bsz, c, l, h, dk)
    kc = k.reshape(bsz, c, l, h, dk)
    vc = v.reshape(bsz, c, l, h, dv)
    scores = jnp.einsum('bclhd,bcshd->bchls', qc, kc) * decay_in
    inner = jnp.einsum('bchls,bcshe->bclhe', scores, vc)
    k_to_end = jnp.exp((l - 1 - idx)[:, None] * log_gamma[None, :])
    kv = jnp.einsum('bclhd,bclhe->bchde', kc * k_to_end[:, :, None], vc)
    chunk_decay = jnp.exp(l * log_gamma)

    def step(state, kv_c):
        return state * chunk_decay[:, None, None] + kv_c, state

    init = jnp.zeros((bsz, h, dk, dv), jnp.float32)
    _, prev = lax.scan(step, init, kv.transpose(1, 0, 2, 3, 4))
    prev = prev.transpose(1, 0, 2, 3, 4)
    q_from_start = jnp.exp((idx + 1.0)[:, None] * log_gamma[None, :])
    cross = jnp.einsum('bclhd,bchde->bclhe', qc * q_from_start[:, :, None], prev)
    return (inner + cross).reshape(bsz, s, h, dv)


def hybrid_mixer(hn, w_in, conv_w, conv_b, dt_bias, a_log, d_skip, ssd_norm,
                 q_norm, k_norm, ret_norm, w_out, cos, sin):
    bsz, s, _ = hn.shape
    f32 = jnp.float32
    proj = (hn @ w_in).astype(f32)
    cuts = list(np.cumsum(IN_SPLITS)[:-1])
    z, xbc, dt_raw, aq, ak, av, rq, rk, rv, rg = jnp.split(proj, cuts, axis=-1)

    xbc = jax.nn.silu(causal_dwconv(xbc, conv_w.astype(f32), conv_b.astype(f32)))
    xs, bm, cm = jnp.split(xbc, [SSD_WIDTH, SSD_WIDTH + SSD_GROUPS * SSD_STATE], axis=-1)
    xs = xs.reshape(bsz, s, SSD_HEADS, SSD_HEAD_DIM)
    bm = bm.reshape(bsz, s, SSD_GROUPS, SSD_STATE)
    cm = cm.reshape(bsz, s, SSD_GROUPS, SSD_STATE)
    dt = jax.nn.softplus(dt_raw + dt_bias.astype(f32))
    a_neg = -jnp.exp(a_log.astype(f32))
    y_ssd = ssd_chunked(xs, dt, a_neg, bm, cm) + xs * d_skip.astype(f32)[:, None]
    y_ssd = grouped_rms(y_ssd.reshape(bsz, s, SSD_WIDTH) * jax.nn.silu(z), SSD_GROUPS, ssd_norm)

    aq = apply_rope(rms_norm(aq.reshape(bsz, s, ATT_HEADS, ATT_HEAD_DIM), q_norm), cos, sin)
    aq = aq * (ATT_HEAD_DIM ** -0.5)
    ak = apply_rope(rms_norm(ak.reshape(bsz, s, ATT_HEADS, ATT_HEAD_DIM), k_norm), cos, sin)
    av = av.reshape(bsz, s, ATT_HEADS, ATT_HEAD_DIM)
    branches = [dilated_branch(aq, ak, av, w, d) for (w, d) in DILATED_PAIRS]
    outs = jnp.stack([o for o, _ in branches])
    wts = jax.nn.softmax(jnp.stack([lse for _, lse in branches]), axis=0)
    y_att = jnp.sum(wts[..., None] * outs, axis=0).reshape(bsz, s, ATT_WIDTH)

    rq = apply_rope(rq.reshape(bsz, s, RET_HEADS, RET_QK_DIM), cos, sin)
    rk = apply_rope(rk.reshape(bsz, s, RET_HEADS, RET_QK_DIM), cos, sin) * (RET_QK_DIM ** -0.5)
    rv = rv.reshape(bsz, s, RET_HEADS, RET_V_DIM)
    y_ret = retention_chunked(rq, rk, rv).reshape(bsz, s, RET_V_WIDTH)
    y_ret = grouped_rms(y_ret, RET_HEADS, ret_norm) * jax.nn.silu(rg)

    y = jnp.concatenate([y_ssd, y_att, y_ret], axis=-1).astype(hn.dtype)
    return y @ w_out


def swiglu(hn, w_gate, w_up, w_down):
    return (jax.nn.silu(hn @ w_gate) * (hn @ w_up)) @ w_down


def setup_inputs(seed: int = 0) -> dict:
    key = jax.random.key(seed)
    ks = jax.random.split(key, 17)
    f32 = jnp.float32

    def nrm(k, shape, scale):
        return jax.random.normal(k, shape, f32) * scale

    x = nrm(ks[0], (BATCH, SEQ, D_MODEL), 1.0)
    ln_mix = 1.0 + nrm(ks[1], (DEPTH, D_MODEL), 0.02)
    w_in = nrm(ks[2], (DEPTH, D_MODEL, IN_WIDTH), D_MODEL ** -0.5)
    conv_w = nrm(ks[3], (DEPTH, SSD_CONV, SSD_CONV_CH), SSD_CONV ** -0.5)
    conv_b = nrm(ks[4], (DEPTH, SSD_CONV_CH), 0.01)
    dt0 = jnp.exp(jax.random.uniform(ks[5], (DEPTH, SSD_HEADS), f32,
                                     math.log(1e-3), math.log(1e-1)))
    dt_bias = dt0 + jnp.log(-jnp.expm1(-dt0))
    a_log = jnp.log(jax.random.uniform(ks[6], (DEPTH, SSD_HEADS), f32, 1.0, 16.0))
    d_skip = 1.0 + nrm(ks[7], (DEPTH, SSD_HEADS), 0.1)
    ssd_norm = 1.0 + nrm(ks[8], (DEPTH, SSD_WIDTH), 0.02)
    q_norm = 1.0 + nrm(ks[9], (DEPTH, ATT_HEAD_DIM), 0.02)
    k_norm = 1.0 + nrm(ks[10], (DEPTH, ATT_HEAD_DIM), 0.02)
    ret_norm = 1.0 + nrm(ks[11], (DEPTH, RET_V_WIDTH), 0.02)
    w_out = nrm(ks[12], (DEPTH, MIX_WIDTH, D_MODEL), MIX_WIDTH ** -0.5)
    ln_ffn = 1.0 + nrm(ks[13], (DEPTH, D_MODEL), 0.02)
    w_gate = nrm(ks[14], (DEPTH, D_MODEL, D_FF), D_MODEL ** -0.5)
    w_up = nrm(ks[15], (DEPTH, D_MODEL, D_FF), D_MODEL ** -0.5)
    w_down = nrm(ks[16], (DEPTH, D_FF, D_MODEL), D_FF ** -0.5)
    return {'x': x, 'ln_mix': ln_mix, 'w_in': w_in, 'conv_w': conv_w, 'conv_b': conv_b,
            'dt_bias': dt_bias, 'a_log': a_log, 'd_skip': d_skip, 'ssd_norm': ssd_norm,
            'q_norm': q_norm, 'k_norm': k_norm, 'ret_norm': ret_norm, 'w_out': w_out,
            'ln_ffn': ln_ffn, 'w_gate': w_gate, 'w_up': w_up, 'w_down': w_down}


def reference(x, ln_mix, w_in, conv_w, conv_b, dt_bias, a_log, d_skip, ssd_norm,
              q_norm, k_norm, ret_norm, w_out, ln_ffn, w_gate, w_up, w_down):
    cos, sin = rope_tables(x.shape[1], ATT_HEAD_DIM)
    for i in range(DEPTH):
        x = x + hybrid_mixer(rms_norm(x, ln_mix[i]), w_in[i], conv_w[i], conv_b[i],
                             dt_bias[i], a_log[i], d_skip[i], ssd_norm[i],
                             q_norm[i], k_norm[i], ret_norm[i], w_out[i], cos, sin)
        x = x + swiglu(rms_norm(x, ln_ffn[i]), w_gate[i], w_up[i], w_down[i])
    return x
```

```python
import numpy as np
import ml_dtypes
from contextlib import ExitStack
import concourse.bass as bass
import concourse.mybir as mybir
from concourse.bass_utils import run_bass_kernel_spmd

F32 = mybir.dt.float32
BF16 = mybir.dt.bfloat16
AF = mybir.ActivationFunctionType
ALU = mybir.AluOpType
NPBF = ml_dtypes.bfloat16

SAME_ENG_SYNC = True
RET_STAGE = 99
EXP = ""
EPS = 1e-6

D = 2048
SEQ = 8192
NB = 4
NPROJ = 2824
DFF = 5632
OZ, OX, OB, OC, ODT, OAQ, OAK, OAV, ORQ, ORK, ORV, ORG = 0, 512, 1024, 1152, 1280, 1288, 1544, 1800, 2056, 2184, 2312, 2568


class T:
    __slots__ = ("name", "lw", "rd", "excl")

    def __init__(self, name="", excl=False):
        self.name = name
        self.lw = None
        self.rd = []
        self.excl = excl


class Buf:
    __slots__ = ("t", "d")

    def __init__(self, t, name=""):
        self.t = t
        self.d = T(name)


class Sched:
    ENGS = ("pe", "act", "dve", "pool", "sp")

    def __init__(self, nc, stack, ndma=16):
        self.nc = nc
        self.prog = {e: [] for e in self.ENGS}
        self.sem = {}
        self.cnt = {}
        self.known = {e: {} for e in self.ENGS}
        for e in self.ENGS:
            self.sem[e] = stack.enter_context(nc.semaphore("s_" + e))
            self.cnt[e] = 0
        self.dsem = {}
        self.dcount = {}
        self.ndma = ndma
        for e in ("sp", "pool", "act"):
            self.dsem[e] = [stack.enter_context(nc.semaphore("d_%s_%d" % (e, i))) for i in range(ndma)]
            self.dcount[e] = 0
        self.ninst = 0

    def _deps(self, eng, reads, writes):
        need = {}
        for t in reads:
            if t.lw is not None:
                k, v = t.lw
                if need.get(k, 0) < v:
                    need[k] = v
        for t in writes:
            if t.lw is not None:
                k, v = t.lw
                if need.get(k, 0) < v:
                    need[k] = v
            for k, v in t.rd:
                if need.get(k, 0) < v:
                    need[k] = v
        waits = []
        kn = self.known[eng]
        for k, v in need.items():
            if k == eng and (not SAME_ENG_SYNC or eng == "pe" or eng == "sp"):
                continue
            if kn.get(k, 0) >= v:
                continue
            kn[k] = v
            waits.append((k, v))
        return waits

    def _semof(self, k):
        if isinstance(k, tuple):
            return self.dsem[k[0]][k[1]]
        return self.sem[k]

    @staticmethod
    def _compact(t, key, ticket):
        t.rd = [(k, v) for (k, v) in t.rd if k != key]
        t.rd.append((key, ticket))

    def op(self, eng, fn, reads=(), writes=(), signal=True):
        ex = [t for t in reads if t.excl]
        if ex:
            reads = [t for t in reads if not t.excl]
            writes = list(writes) + ex
        waits = self._deps(eng, reads, writes)
        if signal:
            self.cnt[eng] += 1
            ticket = self.cnt[eng]
        else:
            ticket = self.cnt[eng] + 1
        sem = self.sem[eng]
        wl = [(self._semof(k), v) for k, v in waits]

        def thunk(e, fn=fn, wl=wl, signal=signal, sem=sem):
            for s, v in wl:
                e.wait_ge(s, v)
            ins = fn(e)
            if signal:
                ins.then_inc(sem, 1)
        self.prog[eng].append(thunk)
        for t in reads:
            self._compact(t, eng, ticket)
        for t in writes:
            t.lw = (eng, ticket)
            t.rd = []
        self.ninst += 1

    def dma(self, eng, fn, reads=(), writes=()):
        i = self.dcount[eng]
        self.dcount[eng] += 1
        slot = i % self.ndma
        ticket = 16 * (i // self.ndma + 1)
        key = (eng, slot)
        waits = self._deps(eng, reads, writes)
        if i >= self.ndma:
            kn = self.known[eng]
            if kn.get(key, 0) < ticket - 16:
                kn[key] = ticket - 16
                waits.append((key, ticket - 16))
        sem = self.dsem[eng][slot]
        wl = [(self._semof(k), v) for k, v in waits]

        def thunk(e, fn=fn, wl=wl, sem=sem):
            for s, v in wl:
                e.wait_ge(s, v)
            fn(e).then_inc(sem, 16)
        self.prog[eng].append(thunk)
        for t in reads:
            self._compact(t, key, ticket)
        for t in writes:
            t.lw = (key, ticket)
            t.rd = []
        self.ninst += 1

    def barrier(self):
        targets = [(e, self.cnt[e]) for e in self.ENGS if self.cnt[e] > 0]
        for q in self.dsem:
            n = self.dcount[q]
            for slot in range(min(n, self.ndma)):
                cnt = (n - slot + self.ndma - 1) // self.ndma
                targets.append(((q, slot), 16 * cnt))
        for eng in self.ENGS:
            kn = self.known[eng]
            wl = []
            for k, v in targets:
                if k == eng and eng in ("pe", "sp"):
                    continue
                if kn.get(k, 0) >= v:
                    continue
                kn[k] = v
                wl.append((self._semof(k), v))

            def thunk(e, wl=wl):
                for s_, v in wl:
                    e.wait_ge(s_, v)
            self.prog[eng].append(thunk)

    def finish(self, eng, tiles):
        waits = self._deps(eng, tiles, ())
        wl = [(self._semof(k), v) for k, v in waits]

        def thunk(e, wl=wl):
            for s, v in wl:
                e.wait_ge(s, v)
        self.prog[eng].append(thunk)

    def replay(self, block):
        prog = self.prog

        @block.tensor
        def _(e):
            for th in prog["pe"]:
                th(e)

        @block.scalar
        def _(e):
            for th in prog["act"]:
                th(e)

        @block.vector
        def _(e):
            for th in prog["dve"]:
                th(e)

        @block.gpsimd
        def _(e):
            for th in prog["pool"]:
                th(e)

        @block.sync
        def _(e):
            for th in prog["sp"]:
                th(e)


class Ctx:
    def __init__(self, st):
        self.nc = bass.Bass("TRN2", target_bir_lowering=False)
        self.st = st
        self.S = Sched(self.nc, st)
        self.outs = []
        self.phase_id = 0
        self.tag = ""
        self.bank = [Buf(st.enter_context(self.nc.psum_tensor("bank%d" % i, [128, 512], F32)), "bank%d" % i)
                     for i in range(8)]
        for b in self.bank:
            b.d.excl = True

    def sb(self, name, shape, dt=F32):
        return Buf(self.st.enter_context(self.nc.sbuf_tensor("sb%d%s_%s" % (self.phase_id, self.tag, name), shape, dt)), name)

    def run_phase(self, fn):
        old = self.st
        self.phase_id += 1
        with ExitStack() as pst:
            self.st = pst
            fn()
            self.S.barrier()
            with self.nc.Block() as block:
                self.S.replay(block)
            self.S.prog = {e: [] for e in self.S.ENGS}
        self.st = old

    def scratch(self, name, shape, dt=F32):
        return self.nc.dram_tensor(name, list(shape), dt, kind="Internal").ap()

    def din(self, name, shape, dt=F32):
        return self.nc.dram_tensor(name, list(shape), dt, kind="ExternalInput").ap()

    def dout(self, name, shape, dt=F32):
        return self.nc.dram_tensor(name, list(shape), dt, kind="ExternalOutput").ap()

    def done(self):
        self.S.finish("pool", self.outs)
        self.S.finish("sp", self.outs)
        with self.nc.Block() as block:
            self.S.replay(block)
        return self.nc


def bfv(bk):
    return bk.t[:, :].bitcast(BF16)


def make_identb(C, CST):
    IDB = C.sb("idb", [128, 128], BF16)
    C.S.op("act", lambda e: e.activation(out=IDB.t[:], in_=CST.t[:, 0:128], func=AF.Copy), reads=[CST.d], writes=[IDB.d])
    return IDB


def const_mats():
    i = np.arange(128)
    ident = (i[:, None] == i[None, :])
    tri = (i[:, None] <= i[None, :])
    strict = (i[:, None] > i[None, :])
    ones = np.ones((128, 128), bool)
    return np.concatenate([ident, tri, strict, ones], axis=1).astype(np.float32)


def ssd_params(conv_w, conv_b, dt_bias, a_log, d_skip, ssd_norm, h):
    ch = np.concatenate([np.arange(h * 512, (h + 1) * 512),
                         1024 + h * 128 + np.arange(128),
                         1280 + h * 128 + np.arange(128)])
    cw = conv_w[:, ch]
    cb = conv_b[ch]
    prm = np.zeros((128, 64), np.float32)
    prm[:, 0:24] = cw.reshape(4, 6, 128).transpose(2, 1, 0).reshape(128, 24)
    prm[:, 24:30] = cb.reshape(6, 128).T
    prm[:, 30:38] = np.broadcast_to(dt_bias[h * 8:(h + 1) * 8], (128, 8))
    prm[:, 38:46] = np.broadcast_to(a_log[h * 8:(h + 1) * 8], (128, 8))
    prm[:, 46:54] = np.broadcast_to(d_skip[h * 8:(h + 1) * 8], (128, 8))
    prm[:, 54:58] = ssd_norm[h * 512:(h + 1) * 512].reshape(4, 128).T
    return prm


def make_ssd(C, NCH, proj, cst_d, prm_d, yT):
    if True:
        nc, S, bank = C.nc, C.S, C.bank
        Stok = NCH * 128
        yT_v = yT.rearrange("(t p) n -> p t n", p=128)

        CST = C.sb("cst", [128, 512]); PRM = C.sb("prm", [128, 64])
        ident = CST.t[:, 0:128]; tri = CST.t[:, 128:256]; strict = CST.t[:, 256:384]; ones = CST.t[:, 384:512]
        PIN = [C.sb("pin%d" % i, [128, 1288]) for i in range(2)]
        XB = [C.sb("xb%d" % i, [128, 6, 131]) for i in range(2)]
        CVX = C.sb("cvx", [128, 4, 128]); CVBC = C.sb("cvbc", [128, 2, 128])
        XA = C.sb("xa", [128, 4, 128]); BCA = C.sb("bca", [128, 2, 128]); BCT = C.sb("bct", [128, 2, 128], BF16)
        BTOK = C.sb("btok", [128, 128], BF16)
        DTV = C.sb("dtv", [128, 8]); DTE = C.sb("dte", [128, 8]); DT = C.sb("dt", [128, 8]); DA = C.sb("da", [128, 8])
        ANEG = C.sb("aneg", [128, 8]); ACS = C.sb("acs", [128, 16]); EAC = C.sb("eac", [128, 8]); CD = C.sb("cd", [128, 8])
        TMP8 = C.sb("tmp8", [128, 8]); DTEND = C.sb("dtend", [128, 8]); DTDTE = C.sb("dtdte", [128, 8])
        XG = C.sb("xg", [128, 512], BF16); XW = C.sb("xw", [128, 512], BF16); SKIP = C.sb("skip", [128, 512])
        U = C.sb("u", [128, 8, 128]); E = C.sb("e", [128, 8, 128]); CBM = C.sb("cbm", [128, 128])
        M = C.sb("m", [128, 8, 128], BF16)
        Y1 = C.sb("y1", [128, 512]); PREV = C.sb("prev", [128, 512]); PREVB = C.sb("prevb", [128, 512], BF16)
        SZ = C.sb("sz", [128, 512]); YZ = C.sb("yz", [128, 512]); SQ = C.sb("sq", [128, 512])
        SS = C.sb("ss", [128, 1]); RSTD = C.sb("rstd", [128, 1]); YN = C.sb("yn", [128, 512])
        YT = [C.sb("yt%d" % i, [128, 4, 512], BF16) for i in range(2)]

        S.dma("sp", lambda e: e.dma_start(out=CST.t[:], in_=cst_d[:, :]), writes=[CST.d])
        S.dma("sp", lambda e: e.dma_start(out=PRM.t[:], in_=prm_d[:, :]), writes=[PRM.d])
        S.op("pool", lambda e: e.memset(XB[0].t[:], 0.0), writes=[XB[0].d])
        S.op("pool", lambda e: e.memset(XB[1].t[:], 0.0), writes=[XB[1].d])
        S.op("pool", lambda e: e.memset(PREV.t[:], 0.0), writes=[PREV.d])
        S.op("pool", lambda e: e.memset(PREVB.t[:], 0.0), writes=[PREVB.d])
        S.op("act", lambda e: e.activation(out=ANEG.t[:], in_=PRM.t[:, 38:46], func=AF.Exp), reads=[PRM.d], writes=[ANEG.d])
        S.op("dve", lambda e: e.tensor_scalar(out=ANEG.t[:], in0=ANEG.t[:], scalar1=-1.0, scalar2=None, op0=ALU.mult),
             reads=[ANEG.d], writes=[ANEG.d])

        def bc8(b):
            return b.t[:, :].unsqueeze(2).broadcast_to([128, 8, 64])

        def v8(ap):
            return ap.rearrange("p (h d) -> p h d", h=8)

        def chunk(c):
            pin = PIN[c % 2]; xb = XB[c % 2]; xbn = XB[(c + 1) % 2]
            S.dma("sp", lambda e, pin=pin, c=c: e.dma_start(out=pin.t[:], in_=proj[c * 128:(c + 1) * 128, 0:1288]),
                  writes=[pin.d])
            for t in range(4):
                S.op("pe", lambda e, t=t, pin=pin: e.transpose(out=bank[0].t[:, t * 128:(t + 1) * 128],
                                                                 in_=pin.t[:, OX + t * 128:OX + (t + 1) * 128], identity=ident),
                     reads=[pin.d, CST.d], writes=[bank[0].d], signal=(t == 3))
            for t in range(2):
                S.op("pe", lambda e, t=t, pin=pin: e.transpose(out=bank[1].t[:, t * 128:(t + 1) * 128],
                                                                 in_=pin.t[:, OB + t * 128:OB + (t + 1) * 128], identity=ident),
                     reads=[pin.d, CST.d], writes=[bank[1].d], signal=(t == 1))
            S.op("act", lambda e, xb=xb: e.activation(out=xb.t[:, 0:4, 3:131], in_=bank[0].t[:, 0:512].rearrange("p (t n) -> p t n", t=4),
                                                       func=AF.Copy), reads=[bank[0].d], writes=[xb.d])
            S.op("act", lambda e, xb=xb: e.activation(out=xb.t[:, 4:6, 3:131], in_=bank[1].t[:, 0:256].rearrange("p (t n) -> p t n", t=2),
                                                       func=AF.Copy), reads=[bank[1].d], writes=[xb.d])
            for t in range(6):
                eng = "dve"
                cv = CVX if t < 4 else CVBC
                tt = t if t < 4 else t - 4
                S.op(eng, lambda e, t=t, tt=tt, cv=cv, xb=xb: e.tensor_scalar(
                    out=cv.t[:, tt, :], in0=xb.t[:, t, 0:128], scalar1=PRM.t[:, t * 4:t * 4 + 1],
                    scalar2=PRM.t[:, 24 + t:25 + t], op0=ALU.mult, op1=ALU.add),
                    reads=[xb.d, PRM.d], writes=[cv.d])
                for i in range(1, 4):
                    S.op(eng, lambda e, t=t, tt=tt, i=i, cv=cv, xb=xb: e.scalar_tensor_tensor(
                        out=cv.t[:, tt, :], in0=xb.t[:, t, i:i + 128], scalar=PRM.t[:, t * 4 + i:t * 4 + i + 1],
                        in1=cv.t[:, tt, :], op0=ALU.mult, op1=ALU.add),
                        reads=[xb.d, PRM.d, cv.d], writes=[cv.d])
            S.op("pool", lambda e, xb=xb, xbn=xbn: e.tensor_copy(out=xbn.t[:, :, 0:3], in_=xb.t[:, :, 128:131]),
                 reads=[xb.d], writes=[xbn.d])
            S.op("act", lambda e: e.activation(out=XA.t[:], in_=CVX.t[:], func=AF.Silu), reads=[CVX.d], writes=[XA.d])
            S.op("act", lambda e: e.activation(out=BCA.t[:], in_=CVBC.t[:], func=AF.Silu), reads=[CVBC.d], writes=[BCA.d])
            S.op("act", lambda e: e.activation(out=BCT.t[:], in_=BCA.t[:], func=AF.Copy), reads=[BCA.d], writes=[BCT.d])
            for t in range(4):
                S.op("pe", lambda e, t=t: e.transpose(out=bank[2].t[:, t * 128:(t + 1) * 128], in_=XA.t[:, t, :], identity=ident),
                     reads=[XA.d, CST.d], writes=[bank[2].d], signal=(t == 3))
            S.op("pe", lambda e: e.transpose(out=bank[3].t[:, 0:128], in_=BCA.t[:, 0, :], identity=ident),
                 reads=[BCA.d, CST.d], writes=[bank[3].d])
            S.op("act", lambda e: e.activation(out=BTOK.t[:], in_=bank[3].t[:, 0:128], func=AF.Copy),
                 reads=[bank[3].d], writes=[BTOK.d])
            S.op("dve", lambda e, pin=pin: e.tensor_tensor(out=DTV.t[:], in0=pin.t[:, ODT:ODT + 8], in1=PRM.t[:, 30:38], op=ALU.add),
                 reads=[pin.d, PRM.d], writes=[DTV.d])
            S.op("act", lambda e: e.activation(out=DTE.t[:], in_=DTV.t[:], func=AF.Exp), reads=[DTV.d], writes=[DTE.d])
            S.op("act", lambda e: e.activation(out=DT.t[:], in_=DTE.t[:], func=AF.Ln, bias=1.0), reads=[DTE.d], writes=[DT.d])
            S.op("dve", lambda e: e.tensor_tensor(out=DA.t[:], in0=DT.t[:], in1=ANEG.t[:], op=ALU.mult),
                 reads=[DT.d, ANEG.d], writes=[DA.d])
            S.op("pe", lambda e: e.matmul(bank[1].t[:, 256:264], lhsT=tri, rhs=DA.t[:], start=True, stop=True),
                 reads=[DA.d, CST.d], writes=[bank[1].d], signal=False)
            S.op("pe", lambda e: e.matmul(bank[1].t[:, 264:272], lhsT=ones, rhs=DA.t[:], start=True, stop=True),
                 reads=[DA.d, CST.d], writes=[bank[1].d])
            S.op("act", lambda e: e.activation(out=ACS.t[:], in_=bank[1].t[:, 256:272], func=AF.Copy),
                 reads=[bank[1].d], writes=[ACS.d])
            S.op("act", lambda e: e.activation(out=EAC.t[:], in_=ACS.t[:, 0:8], func=AF.Exp), reads=[ACS.d], writes=[EAC.d])
            S.op("act", lambda e: e.activation(out=CD.t[:], in_=ACS.t[:, 8:16], func=AF.Exp), reads=[ACS.d], writes=[CD.d])
            S.op("dve", lambda e: e.tensor_tensor(out=TMP8.t[:], in0=ACS.t[:, 8:16], in1=ACS.t[:, 0:8], op=ALU.subtract),
                 reads=[ACS.d], writes=[TMP8.d])
            S.op("act", lambda e: e.activation(out=DTEND.t[:], in_=TMP8.t[:], func=AF.Exp), reads=[TMP8.d], writes=[DTEND.d])
            S.op("dve", lambda e: e.tensor_tensor(out=DTDTE.t[:], in0=DT.t[:], in1=DTEND.t[:], op=ALU.mult),
                 reads=[DT.d, DTEND.d], writes=[DTDTE.d])
            S.op("dve", lambda e: e.tensor_tensor(out=v8(XG.t[:, :]), in0=v8(bank[2].t[:, :]), in1=bc8(DT), op=ALU.mult),
                 reads=[bank[2].d, DT.d], writes=[XG.d])
            S.op("dve", lambda e: e.tensor_tensor(out=v8(XW.t[:, :]), in0=v8(bank[2].t[:, :]), in1=bc8(DTDTE), op=ALU.mult),
                 reads=[bank[2].d, DTDTE.d], writes=[XW.d])
            S.op("dve", lambda e: e.tensor_tensor(out=v8(SKIP.t[:, :]), in0=v8(bank[2].t[:, :]),
                                                  in1=PRM.t[:, 46:54].unsqueeze(2).broadcast_to([128, 8, 64]), op=ALU.mult),
                 reads=[bank[2].d, PRM.d], writes=[SKIP.d])
            S.op("dve", lambda e: e.tensor_tensor(out=U.t[:], in0=strict.unsqueeze(1).broadcast_to([128, 8, 128]),
                                                  in1=DA.t[:, :].unsqueeze(2).broadcast_to([128, 8, 128]), op=ALU.mult),
                 reads=[CST.d, DA.d], writes=[U.d])
            for h in range(8):
                bk = bank[4 + h // 4]
                S.op("pe", lambda e, h=h, bk=bk: e.matmul(bk.t[:, (h % 4) * 128:(h % 4 + 1) * 128], lhsT=U.t[:, h, :], rhs=tri,
                                                          start=True, stop=True),
                     reads=[U.d, CST.d], writes=[bk.d], signal=(h % 4 == 3))
            S.op("act", lambda e: e.activation(out=E.t[:, 0:4, :], in_=bank[4].t[:, :].rearrange("p (h n) -> p h n", h=4), func=AF.Exp),
                 reads=[bank[4].d], writes=[E.d])
            S.op("act", lambda e: e.activation(out=E.t[:, 4:8, :], in_=bank[5].t[:, :].rearrange("p (h n) -> p h n", h=4), func=AF.Exp),
                 reads=[bank[5].d], writes=[E.d])
            S.op("pe", lambda e: e.matmul(bank[3].t[:, 128:256], lhsT=BCT.t[:, 0, :], rhs=BCT.t[:, 1, :], start=True, stop=True),
                 reads=[BCT.d], writes=[bank[3].d])
            S.op("dve", lambda e: e.tensor_tensor(out=CBM.t[:], in0=bank[3].t[:, 128:256], in1=tri, op=ALU.mult),
                 reads=[bank[3].d, CST.d], writes=[CBM.d])
            S.op("dve", lambda e: e.tensor_tensor(out=M.t[:], in0=E.t[:], in1=CBM.t[:, :].unsqueeze(1).broadcast_to([128, 8, 128]),
                                                  op=ALU.mult), reads=[E.d, CBM.d], writes=[M.d])
            for h in range(8):
                S.op("pe", lambda e, h=h: e.matmul(bank[0].t[:, h * 64:(h + 1) * 64], lhsT=M.t[:, h, :], rhs=XG.t[:, h * 64:(h + 1) * 64],
                                                   start=True, stop=True),
                     reads=[M.d, XG.d], writes=[bank[0].d], signal=(h == 7))
            S.op("pe", lambda e: e.matmul(bank[6].t[:, :], lhsT=BCT.t[:, 1, :], rhs=PREVB.t[:, :], start=True, stop=True),
                 reads=[BCT.d, PREVB.d], writes=[bank[6].d])
            S.op("pe", lambda e: e.matmul(bank[7].t[:, :], lhsT=BTOK.t[:, :], rhs=XW.t[:, :], start=True, stop=True),
                 reads=[BTOK.d, XW.d], writes=[bank[7].d])
            S.op("dve", lambda e: e.tensor_tensor(out=v8(Y1.t[:, :]), in0=v8(bank[6].t[:, :]), in1=bc8(EAC), op=ALU.mult),
                 reads=[bank[6].d, EAC.d], writes=[Y1.d])
            S.op("dve", lambda e: e.tensor_tensor(out=Y1.t[:], in0=Y1.t[:], in1=bank[0].t[:, :], op=ALU.add),
                 reads=[Y1.d, bank[0].d], writes=[Y1.d])
            S.op("dve", lambda e: e.tensor_tensor(out=Y1.t[:], in0=Y1.t[:], in1=SKIP.t[:], op=ALU.add),
                 reads=[Y1.d, SKIP.d], writes=[Y1.d])
            S.op("dve", lambda e: e.tensor_tensor(out=v8(PREV.t[:, :]), in0=v8(PREV.t[:, :]), in1=bc8(CD), op=ALU.mult),
                 reads=[PREV.d, CD.d], writes=[PREV.d])
            S.op("dve", lambda e: e.tensor_tensor(out=PREV.t[:], in0=PREV.t[:], in1=bank[7].t[:, :], op=ALU.add),
                 reads=[PREV.d, bank[7].d], writes=[PREV.d])
            S.op("act", lambda e: e.activation(out=PREVB.t[:], in_=PREV.t[:], func=AF.Copy), reads=[PREV.d], writes=[PREVB.d])
            S.op("act", lambda e, pin=pin: e.activation(out=SZ.t[:], in_=pin.t[:, OZ:OZ + 512], func=AF.Silu),
                 reads=[pin.d], writes=[SZ.d])
            S.op("dve", lambda e: e.tensor_tensor(out=YZ.t[:], in0=Y1.t[:], in1=SZ.t[:], op=ALU.mult),
                 reads=[Y1.d, SZ.d], writes=[YZ.d])
            S.op("act", lambda e: e.activation(out=SQ.t[:], in_=YZ.t[:], func=AF.Square, accum_out=SS.t[:]),
                 reads=[YZ.d], writes=[SQ.d, SS.d])
            S.op("act", lambda e: e.activation(out=RSTD.t[:], in_=SS.t[:], func=AF.Sqrt, bias=EPS, scale=1.0 / 512),
                 reads=[SS.d], writes=[RSTD.d])
            S.op("dve", lambda e: e.reciprocal(out=RSTD.t[:], in_=RSTD.t[:]), reads=[RSTD.d], writes=[RSTD.d])
            S.op("act", lambda e: e.activation(out=YN.t[:], in_=YZ.t[:], func=AF.Copy, scale=RSTD.t[:, 0:1]),
                 reads=[YZ.d, RSTD.d], writes=[YN.d])
            for t in range(4):
                S.op("pe", lambda e, t=t: e.transpose(out=bank[2].t[:, t * 128:(t + 1) * 128], in_=YN.t[:, t * 128:(t + 1) * 128],
                                                      identity=ident),
                     reads=[YN.d, CST.d], writes=[bank[2].d], signal=(t == 3))
            yt = YT[(c // 4) % 2]; c4 = c % 4
            S.op("dve", lambda e, yt=yt, c4=c4: e.tensor_tensor(
                out=yt.t[:, :, c4 * 128:(c4 + 1) * 128], in0=bank[2].t[:, :].rearrange("p (t n) -> p t n", t=4),
                in1=PRM.t[:, 54:58].unsqueeze(2).broadcast_to([128, 4, 128]), op=ALU.mult),
                reads=[bank[2].d, PRM.d], writes=[yt.d])
            if c4 == 3:
                c0 = (c - 3) * 128
                o = T("out"); C.outs.append(o)
                S.dma("pool", lambda e, yt=yt, c0=c0: e.dma_start(out=yT_v[:, :, c0:c0 + 512], in_=yt.t[:]),
                      reads=[yt.d], writes=[o])
        return chunk


def emit_multi(C, NCH, makers):
    chunks = []
    for i, mk in enumerate(makers):
        C.tag = "i%d" % i
        chunks.append(mk())
    C.tag = ""
    for c in range(NCH):
        for ch in chunks:
            ch(c)


def half_cols(h):
    r = np.arange
    return np.concatenate([
        h * 512 + r(512),
        1024 + h * 512 + r(512),
        2048 + h * 128 + r(128),
        2304 + h * 128 + r(128),
        2560 + h * 8 + r(8),
        2576 + h * 256 + r(256),
        3088 + h * 256 + r(256),
        3600 + h * 256 + r(256),
        4112 + h * 128 + r(128),
        4368 + h * 128 + r(128),
        4624 + h * 256 + r(256),
        5136 + h * 256 + r(256),
    ])


def rope_table(S):
    pos = np.arange(S, dtype=np.float32)
    inv = (np.float32(10000.0) ** (-np.arange(0, 64, 2, dtype=np.float32) / np.float32(64))).astype(np.float32)
    ang = (pos[:, None] * inv[None, :]).astype(np.float32)
    c = np.cos(ang).astype(np.float32); s = np.sin(ang).astype(np.float32)
    return np.concatenate([c, s, c * np.float32(0.125), s * np.float32(0.125)], axis=1).astype(np.float32)


def ret_consts(ret_norm, h):
    out = np.zeros((128, 1024), np.float32)
    idx = np.arange(128, dtype=np.float64)
    for j in range(2):
        hh = 2 * h + j
        lg = np.log1p(-np.exp2(-5.0 - hh))
        rel = idx[None, :] - idx[:, None]
        dec = np.where(rel >= 0, np.exp(np.maximum(rel, 0) * lg), 0.0)
        out[:, j * 128:(j + 1) * 128] = dec
        out[j * 64:(j + 1) * 64, 256:384] = np.exp((idx + 1.0) * lg)[None, :]
        out[:, 512 + j] = np.exp((127 - idx) * lg)
        out[j * 64:(j + 1) * 64, 514] = np.exp(128 * lg)
    out[:, 516:518] = ret_norm[h * 256:(h + 1) * 256].reshape(2, 128).T
    return out


def make_ret(C, NCH, proj, cst_d, rc_d, rope_d, yT):
    if True:
        nc, S, bank = C.nc, C.S, C.bank
        Stok = NCH * 128
        yT_v = yT.rearrange("(t p) n -> p t n", p=128)

        CST = C.sb("cst", [128, 512]); RC = C.sb("rc", [128, 1024])
        ident = CST.t[:, 0:128]
        PIN = [C.sb("pin%d" % i, [128, 768]) for i in range(2)]
        RP = [C.sb("rp%d" % i, [128, 128]) for i in range(2)]
        TA = C.sb("ta", [128, 4, 32]); TB = C.sb("tb", [128, 4, 32])
        QKR = C.sb("qkr", [128, 4, 64])
        KS = C.sb("ks", [128, 2, 64], BF16); VB = C.sb("vb", [128, 256], BF16)
        QT = C.sb("qt", [128, 128], BF16); KT = C.sb("kt", [128, 128], BF16); QST = C.sb("qst", [128, 128], BF16)
        SC = C.sb("sc", [128, 2, 128], BF16)
        PREV = C.sb("prev", [128, 128]); PREVB = C.sb("prevb", [128, 128], BF16)
        Y = C.sb("y", [128, 256]); SQ = C.sb("sq", [128, 128]); SS = C.sb("ss", [128, 2]); RSTD = C.sb("rstd", [128, 2])
        SG = C.sb("sg", [128, 256]); YN = C.sb("yn", [128, 256])
        YT = [C.sb("yt%d" % i, [128, 2, 512], BF16) for i in range(2)]

        S.dma("sp", lambda e: e.dma_start(out=CST.t[:], in_=cst_d[:, :]), writes=[CST.d])
        S.dma("sp", lambda e: e.dma_start(out=RC.t[:], in_=rc_d[:, :]), writes=[RC.d])
        S.op("pool", lambda e: e.memset(PREV.t[:], 0.0), writes=[PREV.d])
        S.op("pool", lambda e: e.memset(PREVB.t[:], 0.0), writes=[PREVB.d])

        def chunk(c):
            pin = PIN[c % 2]; rp = RP[c % 2]
            S.dma("sp", lambda e, pin=pin, c=c: e.dma_start(out=pin.t[:], in_=proj[c * 128:(c + 1) * 128, ORQ:ORQ + 768]),
                  writes=[pin.d])
            S.dma("sp", lambda e, rp=rp, c=c: e.dma_start(out=rp.t[:], in_=rope_d[c * 128:(c + 1) * 128, :]), writes=[rp.d])
            qk = pin.t[:, 0:256].rearrange("p (a h d) -> p a h d", a=2, h=2)

            def tab(rp, off):
                return rp.t[:, :].rearrange("p (a f) -> p a f", a=2)[:, :, off:off + 32].unsqueeze(2).broadcast_to([128, 2, 2, 32])
            v4 = lambda b: b.t[:, :, :].rearrange("p (a h) d -> p a h d", a=2)
            t1 = qk[:, :, :, 0:32]; t2 = qk[:, :, :, 32:64]
            o1 = QKR.t[:, :, 0:32].rearrange("p (a h) d -> p a h d", a=2)
            o2 = QKR.t[:, :, 32:64].rearrange("p (a h) d -> p a h d", a=2)
            S.op("dve", lambda e, rp=rp, t1=t1: e.tensor_tensor(out=v4(TA), in0=t1, in1=tab(rp, 0), op=ALU.mult),
                 reads=[pin.d, rp.d], writes=[TA.d])
            S.op("dve", lambda e, rp=rp, t2=t2: e.tensor_tensor(out=v4(TB), in0=t2, in1=tab(rp, 32), op=ALU.mult),
                 reads=[pin.d, rp.d], writes=[TB.d])
            S.op("dve", lambda e, o1=o1: e.tensor_tensor(out=o1, in0=v4(TA), in1=v4(TB), op=ALU.subtract),
                 reads=[TA.d, TB.d], writes=[QKR.d])
            S.op("dve", lambda e, rp=rp, t1=t1: e.tensor_tensor(out=v4(TA), in0=t1, in1=tab(rp, 32), op=ALU.mult),
                 reads=[pin.d, rp.d], writes=[TA.d])
            S.op("dve", lambda e, rp=rp, t2=t2: e.tensor_tensor(out=v4(TB), in0=t2, in1=tab(rp, 0), op=ALU.mult),
                 reads=[pin.d, rp.d], writes=[TB.d])
            S.op("dve", lambda e, o2=o2: e.tensor_tensor(out=o2, in0=v4(TA), in1=v4(TB), op=ALU.add),
                 reads=[TA.d, TB.d], writes=[QKR.d])
            if RET_STAGE < 2:
                return
            S.op("dve", lambda e: e.tensor_tensor(out=KS.t[:], in0=QKR.t[:, 2:4, :],
                                                   in1=RC.t[:, 512:514].unsqueeze(2).broadcast_to([128, 2, 64]), op=ALU.mult),
                 reads=[QKR.d, RC.d], writes=[KS.d])
            S.op("act", lambda e, pin=pin: e.activation(out=VB.t[:], in_=pin.t[:, 256:512], func=AF.Copy), reads=[pin.d], writes=[VB.d])
            if RET_STAGE < 3:
                return
            for a in range(2):
                S.op("pe", lambda e, a=a: e.transpose(out=bank[0].t[:, a * 128:(a + 1) * 128],
                                                      in_=QKR.t[:, 2 * a:2 * a + 2, :].rearrange("p h d -> p (h d)"), identity=ident),
                     reads=[QKR.d, CST.d], writes=[bank[0].d], signal=(a == 1))
            S.op("act", lambda e: e.activation(out=QT.t[:], in_=bank[0].t[:, 0:128], func=AF.Copy), reads=[bank[0].d], writes=[QT.d])
            S.op("act", lambda e: e.activation(out=KT.t[:], in_=bank[0].t[:, 128:256], func=AF.Copy), reads=[bank[0].d], writes=[KT.d])
            S.op("dve", lambda e: e.tensor_tensor(out=QST.t[:], in0=bank[0].t[:, 0:128], in1=RC.t[:, 256:384], op=ALU.mult),
                 reads=[bank[0].d, RC.d], writes=[QST.d])
            if RET_STAGE < 4:
                return
            for j in range(2):
                bk = bank[1 + 4 * j]
                S.op("pe", lambda e, j=j, bk=bk: e.matmul(bk.t[:, 0:128], lhsT=KT.t[j * 64:(j + 1) * 64, :],
                                                          rhs=QT.t[j * 64:(j + 1) * 64, :], start=True, stop=True),
                     reads=[KT.d, QT.d], writes=[bk.d])
            for j in range(2):
                bk = bank[1 + 4 * j]
                S.op("dve", lambda e, j=j, bk=bk: e.tensor_tensor(out=SC.t[:, j, :], in0=bk.t[:, 0:128],
                                                                  in1=RC.t[:, j * 128:(j + 1) * 128], op=ALU.mult),
                     reads=[bk.d, RC.d], writes=[SC.d])
            if RET_STAGE < 5:
                return
            for j in range(2):
                bk = bank[2 + 4 * j]
                S.op("pe", lambda e, j=j, bk=bk: e.matmul(bk.t[:, 0:128], lhsT=SC.t[:, j, :], rhs=VB.t[:, j * 128:(j + 1) * 128],
                                                          start=True, stop=False),
                     reads=[SC.d, VB.d], writes=[bk.d], signal=False)
                S.op("pe", lambda e, j=j, bk=bk: e.matmul(bk.t[:, 0:128], lhsT=QST.t[j * 64:(j + 1) * 64, :],
                                                          rhs=PREVB.t[j * 64:(j + 1) * 64, :], start=False, stop=True),
                     reads=[QST.d, PREVB.d], writes=[bk.d])
            if RET_STAGE < 6:
                return
            for j in range(2):
                S.op("pe", lambda e, j=j: e.matmul(bank[3].t[j * 64:(j + 1) * 64, 0:128], lhsT=KS.t[:, j, :], rhs=VB.t[:, j * 128:(j + 1) * 128],
                                                   start=True, stop=True),
                     reads=[KS.d, VB.d], writes=[bank[3].d], signal=(j == 1))
            S.op("dve", lambda e: e.scalar_tensor_tensor(out=PREV.t[:], in0=PREV.t[:], scalar=RC.t[:, 514:515],
                                                         in1=bank[3].t[:, 0:128], op0=ALU.mult, op1=ALU.add),
                 reads=[PREV.d, RC.d, bank[3].d], writes=[PREV.d])
            S.op("act", lambda e: e.activation(out=PREVB.t[:], in_=PREV.t[:], func=AF.Copy), reads=[PREV.d], writes=[PREVB.d])
            if RET_STAGE < 7:
                return
            for j in range(2):
                bk = bank[2 + 4 * j]
                S.op("act", lambda e, j=j, bk=bk: e.activation(out=Y.t[:, j * 128:(j + 1) * 128], in_=bk.t[:, 0:128], func=AF.Copy),
                     reads=[bk.d], writes=[Y.d])
            for j in range(2):
                S.op("act", lambda e, j=j: e.activation(out=SQ.t[:], in_=Y.t[:, j * 128:(j + 1) * 128], func=AF.Square,
                                                        accum_out=SS.t[:, j:j + 1]),
                     reads=[Y.d], writes=[SQ.d, SS.d])
            S.op("act", lambda e: e.activation(out=RSTD.t[:], in_=SS.t[:], func=AF.Sqrt, bias=EPS, scale=1.0 / 128),
                 reads=[SS.d], writes=[RSTD.d])
            S.op("dve", lambda e: e.reciprocal(out=RSTD.t[:], in_=RSTD.t[:]), reads=[RSTD.d], writes=[RSTD.d])
            S.op("act", lambda e, pin=pin: e.activation(out=SG.t[:], in_=pin.t[:, 512:768], func=AF.Silu), reads=[pin.d], writes=[SG.d])
            S.op("dve", lambda e: e.tensor_tensor(out=YN.t[:, :].rearrange("p (h n) -> p h n", h=2),
                                                   in0=Y.t[:, :].rearrange("p (h n) -> p h n", h=2),
                                                   in1=RSTD.t[:, :].unsqueeze(2).broadcast_to([128, 2, 128]), op=ALU.mult),
                 reads=[Y.d, RSTD.d], writes=[YN.d])
            S.op("dve", lambda e: e.tensor_tensor(out=YN.t[:], in0=YN.t[:], in1=SG.t[:], op=ALU.mult),
                 reads=[YN.d, SG.d], writes=[YN.d])
            if RET_STAGE < 8:
                return
            for t in range(2):
                S.op("pe", lambda e, t=t: e.transpose(out=bank[4].t[:, t * 128:(t + 1) * 128], in_=YN.t[:, t * 128:(t + 1) * 128],
                                                      identity=ident),
                     reads=[YN.d, CST.d], writes=[bank[4].d], signal=(t == 1))
            yt = YT[(c // 4) % 2]; c4 = c % 4
            S.op("dve", lambda e, yt=yt, c4=c4: e.tensor_tensor(
                out=yt.t[:, :, c4 * 128:(c4 + 1) * 128], in0=bank[4].t[:, 0:256].rearrange("p (t n) -> p t n", t=2),
                in1=RC.t[:, 516:518].unsqueeze(2).broadcast_to([128, 2, 128]), op=ALU.mult),
                reads=[bank[4].d, RC.d], writes=[yt.d])
            if c4 == 3:
                c0 = (c - 3) * 128
                o = T("out"); C.outs.append(o)
                S.dma("pool", lambda e, yt=yt, c0=c0: e.dma_start(out=yT_v[:, :, c0:c0 + 512], in_=yt.t[:]),
                      reads=[yt.d], writes=[o])


        return chunk


def att_rope_table(S):
    t = rope_table(S)
    return np.ascontiguousarray(np.concatenate([t[:, 64:128], t[:, 0:64]], axis=1))


def att_consts(q_norm, k_norm):
    out = np.zeros((128, 1024), np.float32)
    out[:, 0:256] = np.tile(q_norm, 4)[None, :]
    out[:, 256:512] = np.tile(k_norm, 4)[None, :]
    i = np.arange(128)
    out[:, 512:640] = (i[:, None] <= i[None, :])
    out[:, 640:768] = (i[:, None] >= i[None, :])
    out[64, 768:832] = 1.0
    return out


def emit_att(C, NCH, proj, cst_d, ac_d, rope_d, yT):
    if True:
        nc, S, bank = C.nc, C.S, C.bank
        Stok = NCH * 128

        CST = C.sb("cst", [128, 512]); AC = C.sb("ac", [128, 1024]); MSK = C.sb("msk", [128, 256], BF16)
        ident = CST.t[:, 0:128]
        PIN = [C.sb("pin%d" % i, [128, 512]) for i in range(2)]
        RP = [C.sb("rp%d" % i, [128, 128]) for i in range(2)]
        SQ = C.sb("sq", [128, 512]); SS = C.sb("ss", [128, 8]); RSTD = C.sb("rstd", [128, 8]); QN = C.sb("qn", [128, 512])
        TA = C.sb("ta", [128, 8, 32]); TB = C.sb("tb", [128, 8, 32]); QKR = C.sb("qkr", [128, 8, 64], BF16)
        IDB = make_identb(C, CST)
        QT = C.sb("qt", [128, 2, Stok], BF16); KT = C.sb("kt", [128, 2, Stok], BF16)
        ACC = [C.sb("acc%d" % j, [65, Stok]) for j in range(2)]
        VF = [C.sb("vf%d" % i, [128, 2, 64]) for i in range(2)]
        VE = [C.sb("ve%d" % i, [128, 2, 65], BF16) for i in range(3)]
        PT = [[C.sb("pt%d_%d" % (j, i), [128, 256], BF16) for i in range(2)] for j in range(2)]
        RD = C.sb("rd", [64, 512]); YO = [C.sb("yo%d" % i, [64, 2048], BF16) for i in range(2)]

        S.dma("sp", lambda e: e.dma_start(out=CST.t[:], in_=cst_d[:, :]), writes=[CST.d])
        S.dma("sp", lambda e: e.dma_start(out=AC.t[:], in_=ac_d[:, :]), writes=[AC.d])
        S.op("pool", lambda e: e.tensor_copy(out=MSK.t[:], in_=AC.t[:, 512:768]), reads=[AC.d], writes=[MSK.d])
        for i in range(3):
            S.op("pool", lambda e, i=i: e.memset(VE[i].t[:], 1.0), writes=[VE[i].d])

        for c in range(NCH):
            pin = PIN[c % 2]; rp = RP[c % 2]
            S.dma("sp", lambda e, pin=pin, c=c: e.dma_start(out=pin.t[:], in_=proj[c * 128:(c + 1) * 128, OAQ:OAQ + 512]),
                  writes=[pin.d])
            S.dma("sp", lambda e, rp=rp, c=c: e.dma_start(out=rp.t[:], in_=rope_d[c * 128:(c + 1) * 128, :]), writes=[rp.d])
            S.op("act", lambda e, pin=pin: e.activation(out=SQ.t[:], in_=pin.t[:], func=AF.Square), reads=[pin.d], writes=[SQ.d])
            S.op("dve", lambda e: e.tensor_reduce(out=SS.t[:], in_=SQ.t[:, :].rearrange("p (h d) -> p h d", h=8),
                                                  axis=mybir.AxisListType.X, op=ALU.add), reads=[SQ.d], writes=[SS.d])
            S.op("act", lambda e: e.activation(out=RSTD.t[:], in_=SS.t[:], func=AF.Sqrt, bias=EPS, scale=1.0 / 64),
                 reads=[SS.d], writes=[RSTD.d])
            S.op("dve", lambda e: e.reciprocal(out=RSTD.t[:], in_=RSTD.t[:]), reads=[RSTD.d], writes=[RSTD.d])
            S.op("dve", lambda e, pin=pin: e.tensor_tensor(out=QN.t[:, :].rearrange("p (h d) -> p h d", h=8),
                                                           in0=pin.t[:, :].rearrange("p (h d) -> p h d", h=8),
                                                           in1=RSTD.t[:, :].unsqueeze(2).broadcast_to([128, 8, 64]), op=ALU.mult),
                 reads=[pin.d, RSTD.d], writes=[QN.d])
            S.op("dve", lambda e: e.tensor_tensor(out=QN.t[:], in0=QN.t[:], in1=AC.t[:, 0:512], op=ALU.mult),
                 reads=[QN.d, AC.d], writes=[QN.d])
            qk = QN.t[:, :].rearrange("p (a h d) -> p a h d", a=2, h=4)

            def tab(rp, off):
                return rp.t[:, :].rearrange("p (a f) -> p a f", a=2)[:, :, off:off + 32].unsqueeze(2).broadcast_to([128, 2, 4, 32])
            v4 = lambda b: b.t[:, :, :].rearrange("p (a h) d -> p a h d", a=2)
            t1 = qk[:, :, :, 0:32]; t2 = qk[:, :, :, 32:64]
            o1 = QKR.t[:, :, 0:32].rearrange("p (a h) d -> p a h d", a=2)
            o2 = QKR.t[:, :, 32:64].rearrange("p (a h) d -> p a h d", a=2)
            S.op("dve", lambda e, rp=rp, t1=t1: e.tensor_tensor(out=v4(TA), in0=t1, in1=tab(rp, 0), op=ALU.mult),
                 reads=[QN.d, rp.d], writes=[TA.d])
            S.op("dve", lambda e, rp=rp, t2=t2: e.tensor_tensor(out=v4(TB), in0=t2, in1=tab(rp, 32), op=ALU.mult),
                 reads=[QN.d, rp.d], writes=[TB.d])
            S.op("dve", lambda e, o1=o1: e.tensor_tensor(out=o1, in0=v4(TA), in1=v4(TB), op=ALU.subtract),
                 reads=[TA.d, TB.d], writes=[QKR.d])
            S.op("dve", lambda e, rp=rp, t1=t1: e.tensor_tensor(out=v4(TA), in0=t1, in1=tab(rp, 32), op=ALU.mult),
                 reads=[QN.d, rp.d], writes=[TA.d])
            S.op("dve", lambda e, rp=rp, t2=t2: e.tensor_tensor(out=v4(TB), in0=t2, in1=tab(rp, 0), op=ALU.mult),
                 reads=[QN.d, rp.d], writes=[TB.d])
            S.op("dve", lambda e, o2=o2: e.tensor_tensor(out=o2, in0=v4(TA), in1=v4(TB), op=ALU.add),
                 reads=[TA.d, TB.d], writes=[QKR.d])
            for a in range(4):
                S.op("pe", lambda e, a=a: e.transpose(out=bfv(bank[0])[:, a * 128:(a + 1) * 128],
                                                      in_=QKR.t[:, 2 * a:2 * a + 2, :].rearrange("p h d -> p (h d)"), identity=IDB.t[:]),
                     reads=[QKR.d, IDB.d], writes=[bank[0].d], signal=(a == 3))
            S.op("act", lambda e, c=c: e.activation(out=QT.t[:, :, c * 128:(c + 1) * 128],
                                                    in_=bfv(bank[0])[:, 0:256].rearrange("p (a n) -> p a n", a=2), func=AF.Copy),
                 reads=[bank[0].d], writes=[QT.d])
            S.op("act", lambda e, c=c: e.activation(out=KT.t[:, :, c * 128:(c + 1) * 128],
                                                    in_=bfv(bank[0])[:, 256:512].rearrange("p (a n) -> p a n", a=2), func=AF.Copy),
                 reads=[bank[0].d], writes=[KT.d])

        ti = 0
        for p in range(2):
            for j in range(2):
                S.op("pool", lambda e, j=j: e.memset(ACC[j].t[:], 0.0), writes=[ACC[j].d])
            for d in (1, 4, 16):
                L = Stok // d
                for r in range(d):
                    for blk in range(L // 128):
                        vf = VF[ti % 2]; ve = VE[ti % 3]; vprev = VE[(ti - 1) % 3]
                        t0 = blk * 128 * d + r
                        S.dma("sp", lambda e, vf=vf, t0=t0, d=d, p=p: e.dma_start(
                            out=vf.t[:], in_=proj[t0:t0 + 127 * d + 1:d, OAV + p * 128:OAV + (p + 1) * 128].rearrange("t (h e) -> t h e", h=2)),
                            writes=[vf.d])
                        S.op("act", lambda e, vf=vf, ve=ve: e.activation(out=ve.t[:, :, 0:64], in_=vf.t[:], func=AF.Copy), reads=[vf.d], writes=[ve.d])
                        tok = slice(t0, t0 + 127 * d + 1, d)
                        tokp = slice(t0 - 128 * d, t0 - d + 1, d)
                        for j in range(2):
                            pr = slice(j * 64, (j + 1) * 64)
                            sb_ = bank[1 + j + 2 * (ti % 2)]
                            ob = bank[(5 + j) if ti % 2 == 0 else (0 if j == 0 else 7)]
                            pt = PT[j][ti % 2]
                            ncol = 256 if blk > 0 else 128
                            S.op("pe", lambda e, sb_=sb_, pr=pr, p=p, tok=tok: e.matmul(
                                sb_.t[:, 0:128], lhsT=KT.t[pr, p, tok], rhs=QT.t[pr, p, tok], start=True, stop=True),
                                reads=[KT.d, QT.d], writes=[sb_.d], signal=(blk == 0))
                            if blk > 0:
                                S.op("pe", lambda e, sb_=sb_, pr=pr, p=p, tok=tok, tokp=tokp: e.matmul(
                                    sb_.t[:, 128:256], lhsT=KT.t[pr, p, tokp], rhs=QT.t[pr, p, tok], start=True, stop=True),
                                    reads=[KT.d, QT.d], writes=[sb_.d])
                            S.op("act", lambda e, pt=pt, sb_=sb_, ncol=ncol: e.activation(out=pt.t[:, 0:ncol], in_=sb_.t[:, 0:ncol], func=AF.Exp),
                                 reads=[sb_.d], writes=[pt.d])
                            meng = "dve"
                            S.op(meng, lambda e, pt=pt, ncol=ncol: e.tensor_tensor(out=pt.t[:, 0:ncol], in0=pt.t[:, 0:ncol], in1=MSK.t[:, 0:ncol],
                                                                                   op=ALU.mult),
                                 reads=[pt.d, MSK.d], writes=[pt.d])
                            S.op("pe", lambda e, ob=ob, ve=ve, j=j, pt=pt, blk=blk: e.matmul(
                                ob.t[0:65, 0:128], lhsT=ve.t[:, j, :], rhs=pt.t[:, 0:128], start=True, stop=(blk == 0)),
                                reads=[ve.d, pt.d], writes=[ob.d], signal=(blk == 0))
                            if blk > 0:
                                S.op("pe", lambda e, ob=ob, vprev=vprev, j=j, pt=pt: e.matmul(
                                    ob.t[0:65, 0:128], lhsT=vprev.t[:, j, :], rhs=pt.t[:, 128:256], start=False, stop=True),
                                    reads=[vprev.d, pt.d], writes=[ob.d])
                            S.op("dve", lambda e, j=j, ob=ob, tok=tok: e.tensor_tensor(
                                out=ACC[j].t[0:65, tok], in0=ACC[j].t[0:65, tok], in1=ob.t[0:65, 0:128], op=ALU.add),
                                reads=[ACC[j].d, ob.d], writes=[ACC[j].d])
                        ti += 1
            for j in range(2):
                h = 2 * p + j
                for n0 in range(0, Stok, 512):
                    yo = YO[(n0 // 2048) % 2]
                    S.op("pe", lambda e, j=j, n0=n0: e.matmul(bank[7].t[0:64, :], lhsT=AC.t[0:65, 768:832], rhs=ACC[j].t[0:65, n0:n0 + 512],
                                                              start=True, stop=True),
                         reads=[AC.d, ACC[j].d], writes=[bank[7].d])
                    S.op("dve", lambda e: e.reciprocal(out=RD.t[:], in_=bank[7].t[0:64, :]), reads=[bank[7].d], writes=[RD.d])
                    S.op("dve", lambda e, j=j, n0=n0, yo=yo: e.tensor_tensor(out=yo.t[:, n0 % 2048:n0 % 2048 + 512], in0=ACC[j].t[0:64, n0:n0 + 512],
                                                                             in1=RD.t[:], op=ALU.mult),
                         reads=[ACC[j].d, RD.d], writes=[yo.d])
                    if (n0 + 512) % 2048 == 0 or n0 + 512 == Stok:
                        nb0 = (n0 // 2048) * 2048; nn = n0 + 512 - nb0
                        o = T("out"); C.outs.append(o)
                        S.dma("pool", lambda e, yo=yo, h=h, nb0=nb0, nn=nn: e.dma_start(out=yT[h * 64:(h + 1) * 64, nb0:nb0 + nn], in_=yo.t[:, 0:nn]),
                              reads=[yo.d], writes=[o])


def emit_wconv(C, L, TW, wf, wb):
    if True:
        nc, S = C.nc, C.S
        NB_ = 3
        FB = [C.sb("f%d" % i, [128, TW]) for i in range(NB_)]
        BB = [C.sb("b%d" % i, [128, TW], BF16) for i in range(NB_)]
        engs = ("act", "dve")
        for i in range(L // TW):
            f = FB[i % NB_]; b = BB[i % NB_]
            S.dma("sp", lambda e, f=f, i=i: e.dma_start(out=f.t[:], in_=wf[:, i * TW:(i + 1) * TW]), writes=[f.d])
            eng = engs[i % 2]
            if eng == "act":
                S.op("act", lambda e, f=f, b=b: e.activation(out=b.t[:], in_=f.t[:], func=AF.Copy), reads=[f.d], writes=[b.d])
            else:
                S.op(eng, lambda e, f=f, b=b: e.tensor_copy(out=b.t[:], in_=f.t[:]), reads=[f.d], writes=[b.d])
            o = T("out"); C.outs.append(o)
            S.dma("pool", lambda e, b=b, i=i: e.dma_start(out=wb[:, i * TW:(i + 1) * TW], in_=b.t[:]), reads=[b.d], writes=[o])


def emit_norm_prep(C, xt_ap, xt_dep, XN, SQ, SS, RSTD):
    S = C.S
    S.op("act", lambda e: e.activation(out=SQ.t[:], in_=xt_ap, func=AF.Square, accum_out=SS.t[:]), reads=[xt_dep], writes=[SQ.d, SS.d])
    S.op("act", lambda e: e.activation(out=RSTD.t[:], in_=SS.t[:], func=AF.Sqrt, bias=EPS, scale=1.0 / D), reads=[SS.d], writes=[RSTD.d])
    S.op("dve", lambda e: e.reciprocal(out=RSTD.t[:], in_=RSTD.t[:]), reads=[RSTD.d], writes=[RSTD.d])
    S.op("act", lambda e: e.activation(out=XN.t[:], in_=xt_ap, func=AF.Copy, scale=RSTD.t[:, 0:1]),
         reads=[xt_dep, RSTD.d], writes=[XN.d])


def emit_norm_tr(C, XN, GAIN, ident, CST, HNT, col0, banks):
    S, bank = C.S, C.bank
    for g in range(4):
        bk = bank[banks[g % len(banks)]]
        for q in range(4):
            k = 4 * g + q
            S.op("pe", lambda e, bk=bk, q=q, k=k: e.transpose(out=bfv(bk)[:, q * 128:(q + 1) * 128], in_=XN.t[:, k * 128:(k + 1) * 128],
                                                              identity=ident.t[:]),
                 reads=[XN.d, ident.d], writes=[bk.d], signal=(q == 3))
        S.op("dve", lambda e, bk=bk, g=g: e.tensor_tensor(out=HNT.t[:, 4 * g:4 * g + 4, col0:col0 + 128],
                                                          in0=bfv(bk)[:, 0:512].rearrange("p (q n) -> p q n", q=4),
                                                          in1=GAIN.t[:, 4 * g:4 * g + 4].unsqueeze(2).broadcast_to([128, 4, 128]), op=ALU.mult),
             reads=[bk.d, GAIN.d], writes=[HNT.d])


def emit_norm_transpose(C, xt_ap, xt_dep, XN, SQ, SS, RSTD, GAIN, ident, CST, HNT, col0, banks):
    emit_norm_prep(C, xt_ap, xt_dep, XN, SQ, SS, RSTD)
    emit_norm_tr(C, XN, GAIN, ident, CST, HNT, col0, banks)


def emit_inproj(C, NCH, x, cst_d, g_d, w_d, proj):
    if True:
        nc, S, bank = C.nc, C.S, C.bank
        Stok = NCH * 128
        CST = C.sb("cst", [128, 512]); GAIN = C.sb("gain", [128, 16])
        ident = CST.t[:, 0:128]
        WB = C.sb("wb", [128, 16, NPROJ], BF16)
        XT = [C.sb("xt%d" % i, [128, D]) for i in range(3)]
        XN = C.sb("xn", [128, D], BF16); SQ = C.sb("sq", [128, D]); SS = C.sb("ss", [128, 1]); RSTD = C.sb("rstd", [128, 1])
        HNT = [C.sb("hnt%d" % i, [128, 16, 128], BF16) for i in range(2)]
        OUT = [C.sb("out%d" % i, [128, NPROJ]) for i in range(2)]
        S.dma("sp", lambda e: e.dma_start(out=CST.t[:], in_=cst_d[:, :]), writes=[CST.d])
        S.dma("sp", lambda e: e.dma_start(out=GAIN.t[:], in_=g_d[:, :]), writes=[GAIN.d])
        ident = make_identb(C, CST)
        wv = w_d.rearrange("(k p) n -> p k n", p=128)
        for k in range(16):
            S.dma("act" if k % 2 else "sp", lambda e, k=k: e.dma_start(out=WB.t[:, k, :], in_=wv[:, k, :]), writes=[WB.d])
        ncols = [(i * 512, min(512, NPROJ - i * 512)) for i in range(6)]

        def load_x(c):
            xt = XT[c % 3]
            S.dma("sp", lambda e, xt=xt, c=c: e.dma_start(out=xt.t[:], in_=x[c * 128:(c + 1) * 128, :]), writes=[xt.d])

        def mm_groups(c, groups):
            hnt = HNT[c % 2]; out = OUT[c % 2]
            for i in groups:
                n0, nw = ncols[i]
                bk = bank[2 + (c * 6 + i) % 6]
                for k in range(16):
                    S.op("pe", lambda e, bk=bk, k=k, n0=n0, nw=nw, hnt=hnt: e.matmul(bk.t[:, 0:nw], lhsT=hnt.t[:, k, :], rhs=WB.t[:, k, n0:n0 + nw],
                                                                                  start=(k == 0), stop=(k == 15)),
                         reads=[hnt.d, WB.d], writes=[bk.d], signal=(k == 15))
                if i % 2 == 0:
                    S.op("act", lambda e, bk=bk, n0=n0, nw=nw, out=out: e.activation(out=out.t[:, n0:n0 + nw], in_=bk.t[:, 0:nw], func=AF.Copy),
                         reads=[bk.d], writes=[out.d])
                else:
                    S.op("dve", lambda e, bk=bk, n0=n0, nw=nw, out=out: e.tensor_copy(out=out.t[:, n0:n0 + nw], in_=bk.t[:, 0:nw]),
                         reads=[bk.d], writes=[out.d])

        load_x(0)
        if NCH > 1:
            load_x(1)
        emit_norm_prep(C, XT[0].t[:], XT[0].d, XN, SQ, SS, RSTD)
        emit_norm_tr(C, XN, GAIN, ident, CST, HNT[0], 0, (0, 1))
        for c in range(NCH):
            if c + 2 < NCH:
                load_x(c + 2)
            if c + 1 < NCH and EXP != "noprep":
                emit_norm_prep(C, XT[(c + 1) % 3].t[:], XT[(c + 1) % 3].d, XN, SQ, SS, RSTD)
            mm_groups(c, (0, 1, 2))
            if c + 1 < NCH and EXP != "notr":
                emit_norm_tr(C, XN, GAIN, ident, CST, HNT[(c + 1) % 2], 0, (0, 1))
            mm_groups(c, (3, 4, 5))
            out = OUT[c % 2]
            o = T("out"); C.outs.append(o)
            S.dma("pool", lambda e, out=out, c=c: e.dma_start(out=proj[c * 128:(c + 1) * 128, :], in_=out.t[:]), reads=[out.d], writes=[o])


def emit_outffn(C, NT, x, yT, cst_d, g_d, wo_d, wg_d, wu_d, wd_d, xo):
    if True:
        nc, S, bank = C.nc, C.S, C.bank
        NBLK = NT // 512
        NF = DFF // 128
        CST = C.sb("cst", [128, 512]); GAIN = C.sb("gain", [128, 16])
        ident = CST.t[:, 0:128]
        X1 = C.sb("x1", [128, 4, D])
        YT = C.sb("yt", [128, 16, 512], BF16)
        HNT = C.sb("hnt", [128, 16, 512], BF16)
        HT = C.sb("ht", [128, NF, 512], BF16)
        XN = C.sb("xn", [128, D], BF16); SQ = C.sb("sq", [128, D]); SS = C.sb("ss", [128, 1]); RSTD = C.sb("rstd", [128, 1])
        WS = [C.sb("ws%d" % i, [128, 1024], BF16) for i in range(4)]
        WG = [C.sb("wg%d" % i, [128, 16, 128], BF16) for i in range(3)]
        WU = [C.sb("wu%d" % i, [128, 16, 128], BF16) for i in range(3)]
        SG = [C.sb("sg%d" % i, [128, 512]) for i in range(2)]
        S.dma("sp", lambda e: e.dma_start(out=CST.t[:], in_=cst_d[:, :]), writes=[CST.d])
        S.dma("sp", lambda e: e.dma_start(out=GAIN.t[:], in_=g_d[:, :]), writes=[GAIN.d])
        ident = make_identb(C, CST)
        yT_v = yT.rearrange("(k p) n -> p k n", p=128)
        wsi = [0]

        def gemm_res(lhs, lhs_dep, KCH, wdram):
            for half in range(2):
                for k in range(KCH):
                    ws = WS[wsi[0] % 4]; q = "sp" if wsi[0] % 2 == 0 else "act"; wsi[0] += 1
                    S.dma(q, lambda e, ws=ws, k=k, half=half: e.dma_start(out=ws.t[:], in_=wdram[k * 128:(k + 1) * 128, half * 1024:(half + 1) * 1024]),
                          writes=[ws.d])
                    for t in range(4):
                        for nn in range(2):
                            bk = bank[t * 2 + nn]
                            S.op("pe", lambda e, bk=bk, k=k, t=t, nn=nn, ws=ws: e.matmul(bk.t[:, :], lhsT=lhs(k, t), rhs=ws.t[:, nn * 512:(nn + 1) * 512],
                                                                                      start=(k == 0), stop=(k == KCH - 1)),
                                 reads=[lhs_dep, ws.d], writes=[bk.d], signal=(k == KCH - 1 or (t == 3 and nn == 1)))
                for t in range(4):
                    for nn in range(2):
                        bk = bank[t * 2 + nn]; c0 = half * 1024 + nn * 512
                        S.op("dve", lambda e, bk=bk, t=t, c0=c0: e.tensor_tensor(out=X1.t[:, t, c0:c0 + 512], in0=X1.t[:, t, c0:c0 + 512],
                                                                                 in1=bk.t[:, :], op=ALU.add),
                             reads=[X1.d, bk.d], writes=[X1.d])

        for b in range(NBLK):
            t0 = b * 512
            S.dma("sp", lambda e, t0=t0: e.dma_start(out=X1.t[:], in_=x[t0:t0 + 512, :].rearrange("(t p) n -> p t n", p=128)), writes=[X1.d])
            S.dma("act", lambda e, t0=t0: e.dma_start(out=YT.t[:], in_=yT_v[:, :, t0:t0 + 512]), writes=[YT.d])
            gemm_res(lambda k, t: YT.t[:, k, t * 128:(t + 1) * 128], YT.d, 16, wo_d)
            for t in range(4):
                emit_norm_transpose(C, X1.t[:, t, :], X1.d, XN, SQ, SS, RSTD, GAIN, ident, CST, HNT, t * 128, (0, 1, 2, 3))
            for f in range(NF):
                wg = WG[f % 3]; wu = WU[f % 3]; sg = SG[f % 2]
                S.dma("sp", lambda e, wg=wg, f=f: e.dma_start(out=wg.t[:], in_=wg_d[f].rearrange("p (k c) -> p k c", k=16)), writes=[wg.d])
                S.dma("act", lambda e, wu=wu, f=f: e.dma_start(out=wu.t[:], in_=wu_d[f].rearrange("p (k c) -> p k c", k=16)), writes=[wu.d])
                ba = bank[4 + (f % 2) * 2]; bb = bank[5 + (f % 2) * 2]
                for k in range(16):
                    S.op("pe", lambda e, ba=ba, wg=wg, k=k: e.matmul(ba.t[:, :], lhsT=wg.t[:, k, :], rhs=HNT.t[:, k, :], start=(k == 0), stop=(k == 15)),
                         reads=[wg.d, HNT.d], writes=[ba.d], signal=(k == 15))
                for k in range(16):
                    S.op("pe", lambda e, bb=bb, wu=wu, k=k: e.matmul(bb.t[:, :], lhsT=wu.t[:, k, :], rhs=HNT.t[:, k, :], start=(k == 0), stop=(k == 15)),
                         reads=[wu.d, HNT.d], writes=[bb.d], signal=(k == 15))
                S.op("act", lambda e, ba=ba, sg=sg: e.activation(out=sg.t[:], in_=ba.t[:, :], func=AF.Silu), reads=[ba.d], writes=[sg.d])
                S.op("dve", lambda e, bb=bb, sg=sg, f=f: e.tensor_tensor(out=HT.t[:, f, :], in0=bb.t[:, :], in1=sg.t[:], op=ALU.mult),
                     reads=[bb.d, sg.d], writes=[HT.d])
            gemm_res(lambda k, t: HT.t[:, k, t * 128:(t + 1) * 128], HT.d, NF, wd_d)
            o = T("out"); C.outs.append(o)
            S.dma("pool", lambda e, t0=t0: e.dma_start(out=xo[t0:t0 + 512, :].rearrange("(t p) n -> p t n", p=128), in_=X1.t[:]),
                  reads=[X1.d], writes=[o])


def gate_layout(w):
    return np.ascontiguousarray(w.reshape(16, 128, DFF // 128, 128).transpose(2, 1, 0, 3)).reshape(DFF // 128, 128, 2048)


NL = 2
W_SHAPES = [(D, NPROJ), (D, NPROJ), (D, D), (DFF // 128, 128, 2048), (DFF // 128, 128, 2048), (DFF, D)]
W_SIZES = [int(np.prod(sh)) for sh in W_SHAPES]
LW = sum(W_SIZES) // 128
TW = 3392


def build_fused(NCH=SEQ // 128):
    st = ExitStack()
    with st:
        C = Ctx(st)
        nc, S = C.nc, C.S
        Stok = NCH * 128
        x_in = C.din("x", [Stok, D])
        cst_d = C.din("cst", [128, 512])
        rope_d = C.din("rope", [Stok, 128])
        arope_d = C.din("arope", [Stok, 128])
        sprm_d = C.din("ssd_prm", [NL * 2 * 128, 64])
        retc_d = C.din("ret_c", [NL * 2 * 128, 1024])
        attc_d = C.din("att_c", [NL * 128, 1024])
        gains_d = C.din("gains", [NL * 2 * 128, 16])
        wf = [C.din("wf%d" % l, [128, LW]) for l in range(NL)]
        out = C.dout("out", [Stok, D])
        wb = C.scratch("wb", [128, LW], BF16)
        projs = [C.scratch("proj_s%d" % h, [Stok, NPROJ]) for h in range(2)]
        yT = C.scratch("yT_s", [D, Stok], BF16)
        xs = C.scratch("xs", [Stok, D])
        wbf = wb.rearrange("p n -> (p n)")
        wv, off = [], 0
        for sh, n in zip(W_SHAPES, W_SIZES):
            v = wbf[off:off + n]
            if len(sh) == 2:
                v = v.rearrange("(r c) -> r c", c=sh[1])
            else:
                v = v.rearrange("(f p c) -> f p c", p=sh[1], c=sh[2])
            wv.append(v); off += n
        for l in range(NL):
            xl = x_in if l == 0 else xs
            xo = out if l == NL - 1 else xs
            C.run_phase(lambda: emit_wconv(C, LW, TW, wf[l], wb))
            g_mix = gains_d[(l * 2) * 128:(l * 2 + 1) * 128, :]
            for h in range(2):
                C.run_phase(lambda: emit_inproj(C, NCH, xl, cst_d, g_mix, wv[h], projs[h]))
            rr = [(l * 2 + h) * 128 for h in range(2)]
            C.run_phase(lambda: emit_multi(C, NCH, [
                (lambda h=h: make_ssd(C, NCH, projs[h], cst_d, sprm_d[rr[h]:rr[h] + 128, :], yT[h * 512:(h + 1) * 512, :])) for h in range(2)]))
            for h in range(2):
                C.run_phase(lambda: emit_att(C, NCH, projs[h], cst_d, attc_d[l * 128:(l + 1) * 128, :], arope_d,
                                             yT[1024 + h * 256:1024 + (h + 1) * 256, :]))
            C.run_phase(lambda: emit_multi(C, NCH, [
                (lambda h=h: make_ret(C, NCH, projs[h], cst_d, retc_d[rr[h]:rr[h] + 128, :], rope_d,
                                      yT[1536 + h * 256:1536 + (h + 1) * 256, :])) for h in range(2)]))
            C.run_phase(lambda: emit_outffn(C, Stok, xl, yT, cst_d, gains_d[(l * 2 + 1) * 128:(l * 2 + 2) * 128, :],
                                            wv[2], wv[3], wv[4], wv[5], xo))
        return nc


def fused_inputs(inputs, S_=SEQ):
    x = np.ascontiguousarray(inputs["x"], dtype=np.float32)
    common = {"cst": const_mats(), "rope": rope_table(S_), "arope": att_rope_table(S_)}
    sprm = np.concatenate([ssd_params(inputs["conv_w"][l], inputs["conv_b"][l], inputs["dt_bias"][l], inputs["a_log"][l],
                                      inputs["d_skip"][l], inputs["ssd_norm"][l], h) for l in range(NL) for h in range(2)], axis=0)
    retc = np.concatenate([ret_consts(inputs["ret_norm"][l], h) for l in range(NL) for h in range(2)], axis=0)
    attc = np.concatenate([att_consts(inputs["q_norm"][l], inputs["k_norm"][l]) for l in range(NL)], axis=0)
    gains = np.concatenate([np.ascontiguousarray(inputs[k][l].reshape(16, 128).T) for l in range(NL) for k in ("ln_mix", "ln_ffn")], axis=0)
    common.update({"ssd_prm": sprm, "ret_c": retc, "att_c": attc, "gains": gains.astype(np.float32)})
    for l in range(NL):
        w_in = inputs["w_in"][l]
        lay = [w_in[:, half_cols(0)], w_in[:, half_cols(1)], inputs["w_out"][l], gate_layout(inputs["w_gate"][l]),
               gate_layout(inputs["w_up"][l]), inputs["w_down"][l]]
        common["wf%d" % l] = np.concatenate([np.asarray(a, np.float32).reshape(-1) for a in lay]).reshape(128, LW)
    maps = []
    for c in range(NCORES):
        m = dict(common); m["x"] = np.ascontiguousarray(x[c // 2, :S_]); maps.append(m)
    return maps


NCORES = 8


def kernel(**inputs):
    inputs = {k: np.asarray(v) for k, v in inputs.items()}
    nc = build_fused()
    res = run_bass_kernel_spmd(nc, fused_inputs(inputs), core_ids=list(range(NCORES))).results
    return np.ascontiguousarray(np.stack([np.asarray(res[2 * b]["out"]) for b in range(NB)]), dtype=np.float32)
```

```python
import numpy as np
import ml_dtypes
from contextlib import ExitStack
import concourse.bass as bass
import concourse.mybir as mybir
from concourse.bass_utils import run_bass_kernel_spmd

F32 = mybir.dt.float32
BF16 = mybir.dt.bfloat16
AF = mybir.ActivationFunctionType
ALU = mybir.AluOpType
NPBF = ml_dtypes.bfloat16

SAME_ENG_SYNC = True
RET_STAGE = 99
EXP = ""
EPS = 1e-6

D = 2048
SEQ = 8192
NB = 4
NPROJ = 2824
DFF = 5632
OZ, OX, OB, OC, ODT, OAQ, OAK, OAV, ORQ, ORK, ORV, ORG = 0, 512, 1024, 1152, 1280, 1288, 1544, 1800, 2056, 2184, 2312, 2568


class T:
    __slots__ = ("name", "lw", "rd", "excl")

    def __init__(self, name="", excl=False):
        self.name = name
        self.lw = None
        self.rd = []
        self.excl = excl


class Buf:
    __slots__ = ("t", "d")

    def __init__(self, t, name=""):
        self.t = t
        self.d = T(name)


class Sched:
    ENGS = ("pe", "act", "dve", "pool", "sp")

    def __init__(self, nc, stack, ndma=16):
        self.nc = nc
        self.prog = {e: [] for e in self.ENGS}
        self.sem = {}
        self.cnt = {}
        self.known = {e: {} for e in self.ENGS}
        for e in self.ENGS:
            self.sem[e] = stack.enter_context(nc.semaphore("s_" + e))
            self.cnt[e] = 0
        self.dsem = {}
        self.dcount = {}
        self.ndma = ndma
        for e in ("sp", "pool", "act"):
            self.dsem[e] = [stack.enter_context(nc.semaphore("d_%s_%d" % (e, i))) for i in range(ndma)]
            self.dcount[e] = 0
        self.ninst = 0

    def _deps(self, eng, reads, writes):
        need = {}
        for t in reads:
            if t.lw is not None:
                k, v = t.lw
                if need.get(k, 0) < v:
                    need[k] = v
        for t in writes:
            if t.lw is not None:
                k, v = t.lw
                if need.get(k, 0) < v:
                    need[k] = v
            for k, v in t.rd:
                if need.get(k, 0) < v:
                    need[k] = v
        waits = []
        kn = self.known[eng]
        for k, v in need.items():
            if k == eng and (not SAME_ENG_SYNC or eng == "pe" or eng == "sp"):
                continue
            if kn.get(k, 0) >= v:
                continue
            kn[k] = v
            waits.append((k, v))
        return waits

    def _semof(self, k):
        if isinstance(k, tuple):
            return self.dsem[k[0]][k[1]]
        return self.sem[k]

    @staticmethod
    def _compact(t, key, ticket):
        t.rd = [(k, v) for (k, v) in t.rd if k != key]
        t.rd.append((key, ticket))

    def op(self, eng, fn, reads=(), writes=(), signal=True):
        ex = [t for t in reads if t.excl]
        if ex:
            reads = [t for t in reads if not t.excl]
            writes = list(writes) + ex
        waits = self._deps(eng, reads, writes)
        if signal:
            self.cnt[eng] += 1
            ticket = self.cnt[eng]
        else:
            ticket = self.cnt[eng] + 1
        sem = self.sem[eng]
        wl = [(self._semof(k), v) for k, v in waits]

        def thunk(e, fn=fn, wl=wl, signal=signal, sem=sem):
            for s, v in wl:
                e.wait_ge(s, v)
            ins = fn(e)
            if signal:
                ins.then_inc(sem, 1)
        self.prog[eng].append(thunk)
        for t in reads:
            self._compact(t, eng, ticket)
        for t in writes:
            t.lw = (eng, ticket)
            t.rd = []
        self.ninst += 1

    def dma(self, eng, fn, reads=(), writes=()):
        i = self.dcount[eng]
        self.dcount[eng] += 1
        slot = i % self.ndma
        ticket = 16 * (i // self.ndma + 1)
        key = (eng, slot)
        waits = self._deps(eng, reads, writes)
        if i >= self.ndma:
            kn = self.known[eng]
            if kn.get(key, 0) < ticket - 16:
                kn[key] = ticket - 16
                waits.append((key, ticket - 16))
        sem = self.dsem[eng][slot]
        wl = [(self._semof(k), v) for k, v in waits]

        def thunk(e, fn=fn, wl=wl, sem=sem):
            for s, v in wl:
                e.wait_ge(s, v)
            fn(e).then_inc(sem, 16)
        self.prog[eng].append(thunk)
        for t in reads:
            self._compact(t, key, ticket)
        for t in writes:
            t.lw = (key, ticket)
            t.rd = []
        self.ninst += 1

    def barrier(self):
        targets = [(e, self.cnt[e]) for e in self.ENGS if self.cnt[e] > 0]
        for q in self.dsem:
            n = self.dcount[q]
            for slot in range(min(n, self.ndma)):
                cnt = (n - slot + self.ndma - 1) // self.ndma
                targets.append(((q, slot), 16 * cnt))
        for eng in self.ENGS:
            kn = self.known[eng]
            wl = []
            for k, v in targets:
                if k == eng and eng in ("pe", "sp"):
                    continue
                if kn.get(k, 0) >= v:
                    continue
                kn[k] = v
                wl.append((self._semof(k), v))

            def thunk(e, wl=wl):
                for s_, v in wl:
                    e.wait_ge(s_, v)
            self.prog[eng].append(thunk)

    def finish(self, eng, tiles):
        waits = self._deps(eng, tiles, ())
        wl = [(self._semof(k), v) for k, v in waits]

        def thunk(e, wl=wl):
            for s, v in wl:
                e.wait_ge(s, v)
        self.prog[eng].append(thunk)

    def replay(self, block):
        prog = self.prog

        @block.tensor
        def _(e):
            for th in prog["pe"]:
                th(e)

        @block.scalar
        def _(e):
            for th in prog["act"]:
                th(e)

        @block.vector
        def _(e):
            for th in prog["dve"]:
                th(e)

        @block.gpsimd
        def _(e):
            for th in prog["pool"]:
                th(e)

        @block.sync
        def _(e):
            for th in prog["sp"]:
                th(e)


class Ctx:
    def __init__(self, st):
        self.nc = bass.Bass("TRN2", target_bir_lowering=False)
        self.st = st
        self.S = Sched(self.nc, st)
        self.outs = []
        self.phase_id = 0
        self.tag = ""
        self.bankmap = None
        self.bank = [Buf(st.enter_context(self.nc.psum_tensor("bank%d" % i, [128, 512], F32)), "bank%d" % i)
                     for i in range(8)]
        for b in self.bank:
            b.d.excl = True

    def sb(self, name, shape, dt=F32):
        return Buf(self.st.enter_context(self.nc.sbuf_tensor("sb%d%s_%s" % (self.phase_id, self.tag, name), shape, dt)), name)

    def run_phase(self, fn):
        old = self.st
        self.phase_id += 1
        with ExitStack() as pst:
            self.st = pst
            fn()
            self.S.barrier()
            with self.nc.Block() as block:
                self.S.replay(block)
            self.S.prog = {e: [] for e in self.S.ENGS}
        self.st = old

    def scratch(self, name, shape, dt=F32):
        return self.nc.dram_tensor(name, list(shape), dt, kind="Internal").ap()

    def din(self, name, shape, dt=F32):
        return self.nc.dram_tensor(name, list(shape), dt, kind="ExternalInput").ap()

    def dout(self, name, shape, dt=F32):
        return self.nc.dram_tensor(name, list(shape), dt, kind="ExternalOutput").ap()

    def done(self):
        self.S.finish("pool", self.outs)
        self.S.finish("sp", self.outs)
        with self.nc.Block() as block:
            self.S.replay(block)
        return self.nc


def bfv(bk):
    return bk.t[:, :].bitcast(BF16)


def make_identb(C, CST):
    IDB = C.sb("idb", [128, 128], BF16)
    C.S.op("act", lambda e: e.activation(out=IDB.t[:], in_=CST.t[:, 0:128], func=AF.Copy), reads=[CST.d], writes=[IDB.d])
    return IDB


def const_mats():
    i = np.arange(128)
    ident = (i[:, None] == i[None, :])
    tri = (i[:, None] <= i[None, :])
    strict = (i[:, None] > i[None, :])
    ones = np.ones((128, 128), bool)
    return np.concatenate([ident, tri, strict, ones], axis=1).astype(np.float32)


def ssd_params(conv_w, conv_b, dt_bias, a_log, d_skip, ssd_norm, h):
    ch = np.concatenate([np.arange(h * 512, (h + 1) * 512),
                         1024 + h * 128 + np.arange(128),
                         1280 + h * 128 + np.arange(128)])
    cw = conv_w[:, ch]
    cb = conv_b[ch]
    prm = np.zeros((128, 64), np.float32)
    prm[:, 0:24] = cw.reshape(4, 6, 128).transpose(2, 1, 0).reshape(128, 24)
    prm[:, 24:30] = cb.reshape(6, 128).T
    prm[:, 30:38] = np.broadcast_to(dt_bias[h * 8:(h + 1) * 8], (128, 8))
    prm[:, 38:46] = np.broadcast_to(a_log[h * 8:(h + 1) * 8], (128, 8))
    prm[:, 46:54] = np.broadcast_to(d_skip[h * 8:(h + 1) * 8], (128, 8))
    prm[:, 54:58] = ssd_norm[h * 512:(h + 1) * 512].reshape(4, 128).T
    return prm


def make_ssd(C, NCH, proj, cst_d, prm_d, yT):
    if True:
        nc, S = C.nc, C.S
        bank = [C.bank[i] for i in (C.bankmap or range(8))]
        Stok = NCH * 128
        yT_v = yT.rearrange("(t p) n -> p t n", p=128)

        CST = C.sb("cst", [128, 512]); PRM = C.sb("prm", [128, 64])
        ident = CST.t[:, 0:128]; tri = CST.t[:, 128:256]; strict = CST.t[:, 256:384]; ones = CST.t[:, 384:512]
        PIN = [C.sb("pin%d" % i, [128, 1288]) for i in range(2)]
        XB = [C.sb("xb%d" % i, [128, 6, 131]) for i in range(2)]
        CVX = C.sb("cvx", [128, 4, 128]); CVBC = C.sb("cvbc", [128, 2, 128])
        XA = C.sb("xa", [128, 4, 128]); BCA = C.sb("bca", [128, 2, 128]); BCT = C.sb("bct", [128, 2, 128], BF16)
        BTOK = C.sb("btok", [128, 128], BF16)
        DTV = C.sb("dtv", [128, 8]); DTE = C.sb("dte", [128, 8]); DT = C.sb("dt", [128, 8]); DA = C.sb("da", [128, 8])
        ANEG = C.sb("aneg", [128, 8]); ACS = C.sb("acs", [128, 16]); EAC = C.sb("eac", [128, 8]); CD = C.sb("cd", [128, 8])
        TMP8 = C.sb("tmp8", [128, 8]); DTEND = C.sb("dtend", [128, 8]); DTDTE = C.sb("dtdte", [128, 8])
        XG = C.sb("xg", [128, 512], BF16); XW = C.sb("xw", [128, 512], BF16); SKIP = C.sb("skip", [128, 512])
        U = C.sb("u", [128, 8, 128]); E = C.sb("e", [128, 8, 128]); CBM = C.sb("cbm", [128, 128])
        M = C.sb("m", [128, 8, 128], BF16)
        Y1 = C.sb("y1", [128, 512]); PREV = C.sb("prev", [128, 512]); PREVB = C.sb("prevb", [128, 512], BF16)
        SZ = C.sb("sz", [128, 512]); YZ = C.sb("yz", [128, 512]); SQ = C.sb("sq", [128, 512])
        SS = C.sb("ss", [128, 1]); RSTD = C.sb("rstd", [128, 1]); YN = C.sb("yn", [128, 512])
        YT = [C.sb("yt%d" % i, [128, 4, 512], BF16) for i in range(2)]

        S.dma("sp", lambda e: e.dma_start(out=CST.t[:], in_=cst_d[:, :]), writes=[CST.d])
        S.dma("sp", lambda e: e.dma_start(out=PRM.t[:], in_=prm_d[:, :]), writes=[PRM.d])
        S.op("pool", lambda e: e.memset(XB[0].t[:], 0.0), writes=[XB[0].d])
        S.op("pool", lambda e: e.memset(XB[1].t[:], 0.0), writes=[XB[1].d])
        S.op("pool", lambda e: e.memset(PREV.t[:], 0.0), writes=[PREV.d])
        S.op("pool", lambda e: e.memset(PREVB.t[:], 0.0), writes=[PREVB.d])
        S.op("act", lambda e: e.activation(out=ANEG.t[:], in_=PRM.t[:, 38:46], func=AF.Exp), reads=[PRM.d], writes=[ANEG.d])
        S.op("dve", lambda e: e.tensor_scalar(out=ANEG.t[:], in0=ANEG.t[:], scalar1=-1.0, scalar2=None, op0=ALU.mult),
             reads=[ANEG.d], writes=[ANEG.d])

        def bc8(b):
            return b.t[:, :].unsqueeze(2).broadcast_to([128, 8, 64])

        def v8(ap):
            return ap.rearrange("p (h d) -> p h d", h=8)

        def chunk(c):
            pin = PIN[c % 2]; xb = XB[c % 2]; xbn = XB[(c + 1) % 2]
            S.dma("sp", lambda e, pin=pin, c=c: e.dma_start(out=pin.t[:], in_=proj[c * 128:(c + 1) * 128, 0:1288]),
                  writes=[pin.d])
            for t in range(4):
                S.op("pe", lambda e, t=t, pin=pin: e.transpose(out=bank[0].t[:, t * 128:(t + 1) * 128],
                                                                 in_=pin.t[:, OX + t * 128:OX + (t + 1) * 128], identity=ident),
                     reads=[pin.d, CST.d], writes=[bank[0].d], signal=(t == 3))
            for t in range(2):
                S.op("pe", lambda e, t=t, pin=pin: e.transpose(out=bank[1].t[:, t * 128:(t + 1) * 128],
                                                                 in_=pin.t[:, OB + t * 128:OB + (t + 1) * 128], identity=ident),
                     reads=[pin.d, CST.d], writes=[bank[1].d], signal=(t == 1))
            S.op("act", lambda e, xb=xb: e.activation(out=xb.t[:, 0:4, 3:131], in_=bank[0].t[:, 0:512].rearrange("p (t n) -> p t n", t=4),
                                                       func=AF.Copy), reads=[bank[0].d], writes=[xb.d])
            S.op("act", lambda e, xb=xb: e.activation(out=xb.t[:, 4:6, 3:131], in_=bank[1].t[:, 0:256].rearrange("p (t n) -> p t n", t=2),
                                                       func=AF.Copy), reads=[bank[1].d], writes=[xb.d])
            for t in range(6):
                eng = "dve"
                cv = CVX if t < 4 else CVBC
                tt = t if t < 4 else t - 4
                S.op(eng, lambda e, t=t, tt=tt, cv=cv, xb=xb: e.tensor_scalar(
                    out=cv.t[:, tt, :], in0=xb.t[:, t, 0:128], scalar1=PRM.t[:, t * 4:t * 4 + 1],
                    scalar2=PRM.t[:, 24 + t:25 + t], op0=ALU.mult, op1=ALU.add),
                    reads=[xb.d, PRM.d], writes=[cv.d])
                for i in range(1, 4):
                    S.op(eng, lambda e, t=t, tt=tt, i=i, cv=cv, xb=xb: e.scalar_tensor_tensor(
                        out=cv.t[:, tt, :], in0=xb.t[:, t, i:i + 128], scalar=PRM.t[:, t * 4 + i:t * 4 + i + 1],
                        in1=cv.t[:, tt, :], op0=ALU.mult, op1=ALU.add),
                        reads=[xb.d, PRM.d, cv.d], writes=[cv.d])
            S.op("pool", lambda e, xb=xb, xbn=xbn: e.tensor_copy(out=xbn.t[:, :, 0:3], in_=xb.t[:, :, 128:131]),
                 reads=[xb.d], writes=[xbn.d])
            S.op("act", lambda e: e.activation(out=XA.t[:], in_=CVX.t[:], func=AF.Silu), reads=[CVX.d], writes=[XA.d])
            S.op("act", lambda e: e.activation(out=BCA.t[:], in_=CVBC.t[:], func=AF.Silu), reads=[CVBC.d], writes=[BCA.d])
            S.op("act", lambda e: e.activation(out=BCT.t[:], in_=BCA.t[:], func=AF.Copy), reads=[BCA.d], writes=[BCT.d])
            for t in range(4):
                S.op("pe", lambda e, t=t: e.transpose(out=bank[2].t[:, t * 128:(t + 1) * 128], in_=XA.t[:, t, :], identity=ident),
                     reads=[XA.d, CST.d], writes=[bank[2].d], signal=(t == 3))
            S.op("pe", lambda e: e.transpose(out=bank[3].t[:, 0:128], in_=BCA.t[:, 0, :], identity=ident),
                 reads=[BCA.d, CST.d], writes=[bank[3].d])
            S.op("act", lambda e: e.activation(out=BTOK.t[:], in_=bank[3].t[:, 0:128], func=AF.Copy),
                 reads=[bank[3].d], writes=[BTOK.d])
            S.op("dve", lambda e, pin=pin: e.tensor_tensor(out=DTV.t[:], in0=pin.t[:, ODT:ODT + 8], in1=PRM.t[:, 30:38], op=ALU.add),
                 reads=[pin.d, PRM.d], writes=[DTV.d])
            S.op("act", lambda e: e.activation(out=DTE.t[:], in_=DTV.t[:], func=AF.Exp), reads=[DTV.d], writes=[DTE.d])
            S.op("act", lambda e: e.activation(out=DT.t[:], in_=DTE.t[:], func=AF.Ln, bias=1.0), reads=[DTE.d], writes=[DT.d])
            S.op("dve", lambda e: e.tensor_tensor(out=DA.t[:], in0=DT.t[:], in1=ANEG.t[:], op=ALU.mult),
                 reads=[DT.d, ANEG.d], writes=[DA.d])
            S.op("pe", lambda e: e.matmul(bank[1].t[:, 256:264], lhsT=tri, rhs=DA.t[:], start=True, stop=True),
                 reads=[DA.d, CST.d], writes=[bank[1].d], signal=False)
            S.op("pe", lambda e: e.matmul(bank[1].t[:, 264:272], lhsT=ones, rhs=DA.t[:], start=True, stop=True),
                 reads=[DA.d, CST.d], writes=[bank[1].d])
            S.op("act", lambda e: e.activation(out=ACS.t[:], in_=bank[1].t[:, 256:272], func=AF.Copy),
                 reads=[bank[1].d], writes=[ACS.d])
            S.op("act", lambda e: e.activation(out=EAC.t[:], in_=ACS.t[:, 0:8], func=AF.Exp), reads=[ACS.d], writes=[EAC.d])
            S.op("act", lambda e: e.activation(out=CD.t[:], in_=ACS.t[:, 8:16], func=AF.Exp), reads=[ACS.d], writes=[CD.d])
            S.op("dve", lambda e: e.tensor_tensor(out=TMP8.t[:], in0=ACS.t[:, 8:16], in1=ACS.t[:, 0:8], op=ALU.subtract),
                 reads=[ACS.d], writes=[TMP8.d])
            S.op("act", lambda e: e.activation(out=DTEND.t[:], in_=TMP8.t[:], func=AF.Exp), reads=[TMP8.d], writes=[DTEND.d])
            S.op("dve", lambda e: e.tensor_tensor(out=DTDTE.t[:], in0=DT.t[:], in1=DTEND.t[:], op=ALU.mult),
                 reads=[DT.d, DTEND.d], writes=[DTDTE.d])
            S.op("dve", lambda e: e.tensor_tensor(out=v8(XG.t[:, :]), in0=v8(bank[2].t[:, :]), in1=bc8(DT), op=ALU.mult),
                 reads=[bank[2].d, DT.d], writes=[XG.d])
            S.op("dve", lambda e: e.tensor_tensor(out=v8(XW.t[:, :]), in0=v8(bank[2].t[:, :]), in1=bc8(DTDTE), op=ALU.mult),
                 reads=[bank[2].d, DTDTE.d], writes=[XW.d])
            S.op("dve", lambda e: e.tensor_tensor(out=v8(SKIP.t[:, :]), in0=v8(bank[2].t[:, :]),
                                                  in1=PRM.t[:, 46:54].unsqueeze(2).broadcast_to([128, 8, 64]), op=ALU.mult),
                 reads=[bank[2].d, PRM.d], writes=[SKIP.d])
            S.op("dve", lambda e: e.tensor_tensor(out=U.t[:], in0=strict.unsqueeze(1).broadcast_to([128, 8, 128]),
                                                  in1=DA.t[:, :].unsqueeze(2).broadcast_to([128, 8, 128]), op=ALU.mult),
                 reads=[CST.d, DA.d], writes=[U.d])
            for h in range(8):
                bk = bank[4 + h // 4]
                S.op("pe", lambda e, h=h, bk=bk: e.matmul(bk.t[:, (h % 4) * 128:(h % 4 + 1) * 128], lhsT=U.t[:, h, :], rhs=tri,
                                                          start=True, stop=True),
                     reads=[U.d, CST.d], writes=[bk.d], signal=(h % 4 == 3))
            S.op("act", lambda e: e.activation(out=E.t[:, 0:4, :], in_=bank[4].t[:, :].rearrange("p (h n) -> p h n", h=4), func=AF.Exp),
                 reads=[bank[4].d], writes=[E.d])
            S.op("act", lambda e: e.activation(out=E.t[:, 4:8, :], in_=bank[5].t[:, :].rearrange("p (h n) -> p h n", h=4), func=AF.Exp),
                 reads=[bank[5].d], writes=[E.d])
            S.op("pe", lambda e: e.matmul(bank[3].t[:, 128:256], lhsT=BCT.t[:, 0, :], rhs=BCT.t[:, 1, :], start=True, stop=True),
                 reads=[BCT.d], writes=[bank[3].d])
            S.op("dve", lambda e: e.tensor_tensor(out=CBM.t[:], in0=bank[3].t[:, 128:256], in1=tri, op=ALU.mult),
                 reads=[bank[3].d, CST.d], writes=[CBM.d])
            S.op("dve", lambda e: e.tensor_tensor(out=M.t[:], in0=E.t[:], in1=CBM.t[:, :].unsqueeze(1).broadcast_to([128, 8, 128]),
                                                  op=ALU.mult), reads=[E.d, CBM.d], writes=[M.d])
            for h in range(8):
                S.op("pe", lambda e, h=h: e.matmul(bank[0].t[:, h * 64:(h + 1) * 64], lhsT=M.t[:, h, :], rhs=XG.t[:, h * 64:(h + 1) * 64],
                                                   start=True, stop=True),
                     reads=[M.d, XG.d], writes=[bank[0].d], signal=(h == 7))
            S.op("pe", lambda e: e.matmul(bank[6].t[:, :], lhsT=BCT.t[:, 1, :], rhs=PREVB.t[:, :], start=True, stop=True),
                 reads=[BCT.d, PREVB.d], writes=[bank[6].d])
            S.op("pe", lambda e: e.matmul(bank[7].t[:, :], lhsT=BTOK.t[:, :], rhs=XW.t[:, :], start=True, stop=True),
                 reads=[BTOK.d, XW.d], writes=[bank[7].d])
            S.op("dve", lambda e: e.tensor_tensor(out=v8(Y1.t[:, :]), in0=v8(bank[6].t[:, :]), in1=bc8(EAC), op=ALU.mult),
                 reads=[bank[6].d, EAC.d], writes=[Y1.d])
            S.op("dve", lambda e: e.tensor_tensor(out=Y1.t[:], in0=Y1.t[:], in1=bank[0].t[:, :], op=ALU.add),
                 reads=[Y1.d, bank[0].d], writes=[Y1.d])
            S.op("dve", lambda e: e.tensor_tensor(out=Y1.t[:], in0=Y1.t[:], in1=SKIP.t[:], op=ALU.add),
                 reads=[Y1.d, SKIP.d], writes=[Y1.d])
            S.op("dve", lambda e: e.tensor_tensor(out=v8(PREV.t[:, :]), in0=v8(PREV.t[:, :]), in1=bc8(CD), op=ALU.mult),
                 reads=[PREV.d, CD.d], writes=[PREV.d])
            S.op("dve", lambda e: e.tensor_tensor(out=PREV.t[:], in0=PREV.t[:], in1=bank[7].t[:, :], op=ALU.add),
                 reads=[PREV.d, bank[7].d], writes=[PREV.d])
            S.op("act", lambda e: e.activation(out=PREVB.t[:], in_=PREV.t[:], func=AF.Copy), reads=[PREV.d], writes=[PREVB.d])
            S.op("act", lambda e, pin=pin: e.activation(out=SZ.t[:], in_=pin.t[:, OZ:OZ + 512], func=AF.Silu),
                 reads=[pin.d], writes=[SZ.d])
            S.op("dve", lambda e: e.tensor_tensor(out=YZ.t[:], in0=Y1.t[:], in1=SZ.t[:], op=ALU.mult),
                 reads=[Y1.d, SZ.d], writes=[YZ.d])
            S.op("act", lambda e: e.activation(out=SQ.t[:], in_=YZ.t[:], func=AF.Square, accum_out=SS.t[:]),
                 reads=[YZ.d], writes=[SQ.d, SS.d])
            S.op("act", lambda e: e.activation(out=RSTD.t[:], in_=SS.t[:], func=AF.Sqrt, bias=EPS, scale=1.0 / 512),
                 reads=[SS.d], writes=[RSTD.d])
            S.op("dve", lambda e: e.reciprocal(out=RSTD.t[:], in_=RSTD.t[:]), reads=[RSTD.d], writes=[RSTD.d])
            S.op("act", lambda e: e.activation(out=YN.t[:], in_=YZ.t[:], func=AF.Copy, scale=RSTD.t[:, 0:1]),
                 reads=[YZ.d, RSTD.d], writes=[YN.d])
            for t in range(4):
                S.op("pe", lambda e, t=t: e.transpose(out=bank[2].t[:, t * 128:(t + 1) * 128], in_=YN.t[:, t * 128:(t + 1) * 128],
                                                      identity=ident),
                     reads=[YN.d, CST.d], writes=[bank[2].d], signal=(t == 3))
            yt = YT[(c // 4) % 2]; c4 = c % 4
            S.op("dve", lambda e, yt=yt, c4=c4: e.tensor_tensor(
                out=yt.t[:, :, c4 * 128:(c4 + 1) * 128], in0=bank[2].t[:, :].rearrange("p (t n) -> p t n", t=4),
                in1=PRM.t[:, 54:58].unsqueeze(2).broadcast_to([128, 4, 128]), op=ALU.mult),
                reads=[bank[2].d, PRM.d], writes=[yt.d])
            if c4 == 3:
                c0 = (c - 3) * 128
                o = T("out"); C.outs.append(o)
                S.dma("pool", lambda e, yt=yt, c0=c0: e.dma_start(out=yT_v[:, :, c0:c0 + 512], in_=yt.t[:]),
                      reads=[yt.d], writes=[o])
        return chunk


class Recorder:
    def __init__(self):
        self.l = []

    def op(self, eng, fn, reads=(), writes=(), signal=True):
        self.l.append(("op", eng, fn, tuple(reads), tuple(writes), signal))

    def dma(self, eng, fn, reads=(), writes=()):
        self.l.append(("dma", eng, fn, tuple(reads), tuple(writes)))

    def flush_to(self, S):
        for it in self.l:
            play(S, it)
        self.l = []


def play(S, it):
    if it[0] == "op":
        S.op(it[1], it[2], reads=it[3], writes=it[4], signal=it[5])
    else:
        S.dma(it[1], it[2], reads=it[3], writes=it[4])


def emit_multi(C, NCH, makers, bankmaps, extra=()):
    S = C.S
    recs, chunks = [], []
    makers = list(makers) + list(extra)
    bankmaps = list(bankmaps) + [None] * len(extra)
    for i, mk in enumerate(makers):
        C.tag = "i%d" % i
        C.bankmap = bankmaps[i]
        r = Recorder()
        C.S = r
        chunks.append(mk())
        C.S = S
        C.bankmap = None
        r.flush_to(S)
        recs.append(r)
    C.tag = ""
    for c in range(NCH):
        lists = []
        for r, ch in zip(recs, chunks):
            ch(c)
            lists.append(r.l); r.l = []
        n = max(len(l) for l in lists)
        for k in range(n):
            for l in lists:
                if k < len(l):
                    play(S, l[k])


def half_cols(h):
    r = np.arange
    return np.concatenate([
        h * 512 + r(512),
        1024 + h * 512 + r(512),
        2048 + h * 128 + r(128),
        2304 + h * 128 + r(128),
        2560 + h * 8 + r(8),
        2576 + h * 256 + r(256),
        3088 + h * 256 + r(256),
        3600 + h * 256 + r(256),
        4112 + h * 128 + r(128),
        4368 + h * 128 + r(128),
        4624 + h * 256 + r(256),
        5136 + h * 256 + r(256),
    ])


def rope_table(S):
    pos = np.arange(S, dtype=np.float32)
    inv = (np.float32(10000.0) ** (-np.arange(0, 64, 2, dtype=np.float32) / np.float32(64))).astype(np.float32)
    ang = (pos[:, None] * inv[None, :]).astype(np.float32)
    c = np.cos(ang).astype(np.float32); s = np.sin(ang).astype(np.float32)
    return np.concatenate([c, s, c * np.float32(0.125), s * np.float32(0.125)], axis=1).astype(np.float32)


def ret_consts(ret_norm, h):
    out = np.zeros((128, 1024), np.float32)
    idx = np.arange(128, dtype=np.float64)
    for j in range(2):
        hh = 2 * h + j
        lg = np.log1p(-np.exp2(-5.0 - hh))
        rel = idx[None, :] - idx[:, None]
        dec = np.where(rel >= 0, np.exp(np.maximum(rel, 0) * lg), 0.0)
        out[:, j * 128:(j + 1) * 128] = dec
        out[j * 64:(j + 1) * 64, 256:384] = np.exp((idx + 1.0) * lg)[None, :]
        out[:, 512 + j] = np.exp((127 - idx) * lg)
        out[j * 64:(j + 1) * 64, 514] = np.exp(128 * lg)
    out[:, 516:518] = ret_norm[h * 256:(h + 1) * 256].reshape(2, 128).T
    return out


def make_ret(C, NCH, proj, cst_d, rc_d, rope_d, yT):
    if True:
        nc, S = C.nc, C.S
        bank = [C.bank[i] for i in (C.bankmap or range(8))]
        Stok = NCH * 128
        yT_v = yT.rearrange("(t p) n -> p t n", p=128)

        CST = C.sb("cst", [128, 512]); RC = C.sb("rc", [128, 1024])
        ident = CST.t[:, 0:128]
        PIN = [C.sb("pin%d" % i, [128, 768]) for i in range(2)]
        RP = [C.sb("rp%d" % i, [128, 128]) for i in range(2)]
        TA = C.sb("ta", [128, 4, 32]); TB = C.sb("tb", [128, 4, 32])
        QKR = C.sb("qkr", [128, 4, 64])
        KS = C.sb("ks", [128, 2, 64], BF16); VB = C.sb("vb", [128, 256], BF16)
        QT = C.sb("qt", [128, 128], BF16); KT = C.sb("kt", [128, 128], BF16); QST = C.sb("qst", [128, 128], BF16)
        SC = C.sb("sc", [128, 2, 128], BF16)
        PREV = C.sb("prev", [128, 128]); PREVB = C.sb("prevb", [128, 128], BF16)
        Y = C.sb("y", [128, 256]); SQ = C.sb("sq", [128, 128]); SS = C.sb("ss", [128, 2]); RSTD = C.sb("rstd", [128, 2])
        SG = C.sb("sg", [128, 256]); YN = C.sb("yn", [128, 256])
        YT = [C.sb("yt%d" % i, [128, 2, 512], BF16) for i in range(2)]

        S.dma("sp", lambda e: e.dma_start(out=CST.t[:], in_=cst_d[:, :]), writes=[CST.d])
        S.dma("sp", lambda e: e.dma_start(out=RC.t[:], in_=rc_d[:, :]), writes=[RC.d])
        S.op("pool", lambda e: e.memset(PREV.t[:], 0.0), writes=[PREV.d])
        S.op("pool", lambda e: e.memset(PREVB.t[:], 0.0), writes=[PREVB.d])

        def chunk(c):
            pin = PIN[c % 2]; rp = RP[c % 2]
            S.dma("sp", lambda e, pin=pin, c=c: e.dma_start(out=pin.t[:], in_=proj[c * 128:(c + 1) * 128, ORQ:ORQ + 768]),
                  writes=[pin.d])
            S.dma("sp", lambda e, rp=rp, c=c: e.dma_start(out=rp.t[:], in_=rope_d[c * 128:(c + 1) * 128, :]), writes=[rp.d])
            qk = pin.t[:, 0:256].rearrange("p (a h d) -> p a h d", a=2, h=2)

            def tab(rp, off):
                return rp.t[:, :].rearrange("p (a f) -> p a f", a=2)[:, :, off:off + 32].unsqueeze(2).broadcast_to([128, 2, 2, 32])
            v4 = lambda b: b.t[:, :, :].rearrange("p (a h) d -> p a h d", a=2)
            t1 = qk[:, :, :, 0:32]; t2 = qk[:, :, :, 32:64]
            o1 = QKR.t[:, :, 0:32].rearrange("p (a h) d -> p a h d", a=2)
            o2 = QKR.t[:, :, 32:64].rearrange("p (a h) d -> p a h d", a=2)
            S.op("dve", lambda e, rp=rp, t1=t1: e.tensor_tensor(out=v4(TA), in0=t1, in1=tab(rp, 0), op=ALU.mult),
                 reads=[pin.d, rp.d], writes=[TA.d])
            S.op("dve", lambda e, rp=rp, t2=t2: e.tensor_tensor(out=v4(TB), in0=t2, in1=tab(rp, 32), op=ALU.mult),
                 reads=[pin.d, rp.d], writes=[TB.d])
            S.op("dve", lambda e, o1=o1: e.tensor_tensor(out=o1, in0=v4(TA), in1=v4(TB), op=ALU.subtract),
                 reads=[TA.d, TB.d], writes=[QKR.d])
            S.op("dve", lambda e, rp=rp, t1=t1: e.tensor_tensor(out=v4(TA), in0=t1, in1=tab(rp, 32), op=ALU.mult),
                 reads=[pin.d, rp.d], writes=[TA.d])
            S.op("dve", lambda e, rp=rp, t2=t2: e.tensor_tensor(out=v4(TB), in0=t2, in1=tab(rp, 0), op=ALU.mult),
                 reads=[pin.d, rp.d], writes=[TB.d])
            S.op("dve", lambda e, o2=o2: e.tensor_tensor(out=o2, in0=v4(TA), in1=v4(TB), op=ALU.add),
                 reads=[TA.d, TB.d], writes=[QKR.d])
            if RET_STAGE < 2:
                return
            S.op("dve", lambda e: e.tensor_tensor(out=KS.t[:], in0=QKR.t[:, 2:4, :],
                                                   in1=RC.t[:, 512:514].unsqueeze(2).broadcast_to([128, 2, 64]), op=ALU.mult),
                 reads=[QKR.d, RC.d], writes=[KS.d])
            S.op("act", lambda e, pin=pin: e.activation(out=VB.t[:], in_=pin.t[:, 256:512], func=AF.Copy), reads=[pin.d], writes=[VB.d])
            if RET_STAGE < 3:
                return
            for a in range(2):
                S.op("pe", lambda e, a=a: e.transpose(out=bank[0].t[:, a * 128:(a + 1) * 128],
                                                      in_=QKR.t[:, 2 * a:2 * a + 2, :].rearrange("p h d -> p (h d)"), identity=ident),
                     reads=[QKR.d, CST.d], writes=[bank[0].d], signal=(a == 1))
            S.op("act", lambda e: e.activation(out=QT.t[:], in_=bank[0].t[:, 0:128], func=AF.Copy), reads=[bank[0].d], writes=[QT.d])
            S.op("act", lambda e: e.activation(out=KT.t[:], in_=bank[0].t[:, 128:256], func=AF.Copy), reads=[bank[0].d], writes=[KT.d])
            S.op("dve", lambda e: e.tensor_tensor(out=QST.t[:], in0=bank[0].t[:, 0:128], in1=RC.t[:, 256:384], op=ALU.mult),
                 reads=[bank[0].d, RC.d], writes=[QST.d])
            if RET_STAGE < 4:
                return
            for j in range(2):
                bk = bank[1 + 4 * j]
                S.op("pe", lambda e, j=j, bk=bk: e.matmul(bk.t[:, 0:128], lhsT=KT.t[j * 64:(j + 1) * 64, :],
                                                          rhs=QT.t[j * 64:(j + 1) * 64, :], start=True, stop=True),
                     reads=[KT.d, QT.d], writes=[bk.d])
            for j in range(2):
                bk = bank[1 + 4 * j]
                S.op("dve", lambda e, j=j, bk=bk: e.tensor_tensor(out=SC.t[:, j, :], in0=bk.t[:, 0:128],
                                                                  in1=RC.t[:, j * 128:(j + 1) * 128], op=ALU.mult),
                     reads=[bk.d, RC.d], writes=[SC.d])
            if RET_STAGE < 5:
                return
            for j in range(2):
                bk = bank[2 + 4 * j]
                S.op("pe", lambda e, j=j, bk=bk: e.matmul(bk.t[:, 0:128], lhsT=SC.t[:, j, :], rhs=VB.t[:, j * 128:(j + 1) * 128],
                                                          start=True, stop=False),
                     reads=[SC.d, VB.d], writes=[bk.d], signal=False)
                S.op("pe", lambda e, j=j, bk=bk: e.matmul(bk.t[:, 0:128], lhsT=QST.t[j * 64:(j + 1) * 64, :],
                                                          rhs=PREVB.t[j * 64:(j + 1) * 64, :], start=False, stop=True),
                     reads=[QST.d, PREVB.d], writes=[bk.d])
            if RET_STAGE < 6:
                return
            for j in range(2):
                S.op("pe", lambda e, j=j: e.matmul(bank[3].t[j * 64:(j + 1) * 64, 0:128], lhsT=KS.t[:, j, :], rhs=VB.t[:, j * 128:(j + 1) * 128],
                                                   start=True, stop=True),
                     reads=[KS.d, VB.d], writes=[bank[3].d], signal=(j == 1))
            S.op("dve", lambda e: e.scalar_tensor_tensor(out=PREV.t[:], in0=PREV.t[:], scalar=RC.t[:, 514:515],
                                                         in1=bank[3].t[:, 0:128], op0=ALU.mult, op1=ALU.add),
                 reads=[PREV.d, RC.d, bank[3].d], writes=[PREV.d])
            S.op("act", lambda e: e.activation(out=PREVB.t[:], in_=PREV.t[:], func=AF.Copy), reads=[PREV.d], writes=[PREVB.d])
            if RET_STAGE < 7:
                return
            for j in range(2):
                bk = bank[2 + 4 * j]
                S.op("act", lambda e, j=j, bk=bk: e.activation(out=Y.t[:, j * 128:(j + 1) * 128], in_=bk.t[:, 0:128], func=AF.Copy),
                     reads=[bk.d], writes=[Y.d])
            for j in range(2):
                S.op("act", lambda e, j=j: e.activation(out=SQ.t[:], in_=Y.t[:, j * 128:(j + 1) * 128], func=AF.Square,
                                                        accum_out=SS.t[:, j:j + 1]),
                     reads=[Y.d], writes=[SQ.d, SS.d])
            S.op("act", lambda e: e.activation(out=RSTD.t[:], in_=SS.t[:], func=AF.Sqrt, bias=EPS, scale=1.0 / 128),
                 reads=[SS.d], writes=[RSTD.d])
            S.op("dve", lambda e: e.reciprocal(out=RSTD.t[:], in_=RSTD.t[:]), reads=[RSTD.d], writes=[RSTD.d])
            S.op("act", lambda e, pin=pin: e.activation(out=SG.t[:], in_=pin.t[:, 512:768], func=AF.Silu), reads=[pin.d], writes=[SG.d])
            S.op("dve", lambda e: e.tensor_tensor(out=YN.t[:, :].rearrange("p (h n) -> p h n", h=2),
                                                   in0=Y.t[:, :].rearrange("p (h n) -> p h n", h=2),
                                                   in1=RSTD.t[:, :].unsqueeze(2).broadcast_to([128, 2, 128]), op=ALU.mult),
                 reads=[Y.d, RSTD.d], writes=[YN.d])
            S.op("dve", lambda e: e.tensor_tensor(out=YN.t[:], in0=YN.t[:], in1=SG.t[:], op=ALU.mult),
                 reads=[YN.d, SG.d], writes=[YN.d])
            if RET_STAGE < 8:
                return
            for t in range(2):
                S.op("pe", lambda e, t=t: e.transpose(out=bank[4].t[:, t * 128:(t + 1) * 128], in_=YN.t[:, t * 128:(t + 1) * 128],
                                                      identity=ident),
                     reads=[YN.d, CST.d], writes=[bank[4].d], signal=(t == 1))
            yt = YT[(c // 4) % 2]; c4 = c % 4
            S.op("dve", lambda e, yt=yt, c4=c4: e.tensor_tensor(
                out=yt.t[:, :, c4 * 128:(c4 + 1) * 128], in0=bank[4].t[:, 0:256].rearrange("p (t n) -> p t n", t=2),
                in1=RC.t[:, 516:518].unsqueeze(2).broadcast_to([128, 2, 128]), op=ALU.mult),
                reads=[bank[4].d, RC.d], writes=[yt.d])
            if c4 == 3:
                c0 = (c - 3) * 128
                o = T("out"); C.outs.append(o)
                S.dma("pool", lambda e, yt=yt, c0=c0: e.dma_start(out=yT_v[:, :, c0:c0 + 512], in_=yt.t[:]),
                      reads=[yt.d], writes=[o])


        return chunk


def att_rope_table(S):
    t = rope_table(S)
    return np.ascontiguousarray(np.concatenate([t[:, 64:128], t[:, 0:64]], axis=1))


def att_consts(q_norm, k_norm):
    out = np.zeros((128, 1024), np.float32)
    out[:, 0:256] = np.tile(q_norm, 4)[None, :]
    out[:, 256:512] = np.tile(k_norm, 4)[None, :]
    i = np.arange(128)
    out[:, 512:640] = (i[:, None] <= i[None, :])
    out[:, 640:768] = (i[:, None] >= i[None, :])
    out[64, 768:832] = 1.0
    return out


def emit_att(C, NCH, proj, cst_d, ac_d, rope_d, yT):
    if True:
        nc, S, bank = C.nc, C.S, C.bank
        Stok = NCH * 128

        CST = C.sb("cst", [128, 512]); AC = C.sb("ac", [128, 1024]); MSK = C.sb("msk", [128, 256], BF16)
        ident = CST.t[:, 0:128]
        PIN = [C.sb("pin%d" % i, [128, 512]) for i in range(2)]
        RP = [C.sb("rp%d" % i, [128, 128]) for i in range(2)]
        SQ = C.sb("sq", [128, 512]); SS = C.sb("ss", [128, 8]); RSTD = C.sb("rstd", [128, 8]); QN = C.sb("qn", [128, 512])
        TA = C.sb("ta", [128, 8, 32]); TB = C.sb("tb", [128, 8, 32]); QKR = C.sb("qkr", [128, 8, 64], BF16)
        IDB = make_identb(C, CST)
        QT = C.sb("qt", [128, 2, Stok], BF16); KT = C.sb("kt", [128, 2, Stok], BF16)
        ACC = [C.sb("acc%d" % j, [65, Stok]) for j in range(2)]
        VF = [C.sb("vf%d" % i, [128, 2, 64]) for i in range(2)]
        VE = [C.sb("ve%d" % i, [128, 2, 65], BF16) for i in range(3)]
        PT = [[C.sb("pt%d_%d" % (j, i), [128, 256], BF16) for i in range(2)] for j in range(2)]
        RD = C.sb("rd", [64, 512]); YO = [C.sb("yo%d" % i, [64, 2048], BF16) for i in range(2)]

        S.dma("sp", lambda e: e.dma_start(out=CST.t[:], in_=cst_d[:, :]), writes=[CST.d])
        S.dma("sp", lambda e: e.dma_start(out=AC.t[:], in_=ac_d[:, :]), writes=[AC.d])
        S.op("pool", lambda e: e.tensor_copy(out=MSK.t[:], in_=AC.t[:, 512:768]), reads=[AC.d], writes=[MSK.d])
        for i in range(3):
            S.op("pool", lambda e, i=i: e.memset(VE[i].t[:], 1.0), writes=[VE[i].d])

        for c in range(NCH):
            pin = PIN[c % 2]; rp = RP[c % 2]
            S.dma("sp", lambda e, pin=pin, c=c: e.dma_start(out=pin.t[:], in_=proj[c * 128:(c + 1) * 128, OAQ:OAQ + 512]),
                  writes=[pin.d])
            S.dma("sp", lambda e, rp=rp, c=c: e.dma_start(out=rp.t[:], in_=rope_d[c * 128:(c + 1) * 128, :]), writes=[rp.d])
            S.op("act", lambda e, pin=pin: e.activation(out=SQ.t[:], in_=pin.t[:], func=AF.Square), reads=[pin.d], writes=[SQ.d])
            S.op("dve", lambda e: e.tensor_reduce(out=SS.t[:], in_=SQ.t[:, :].rearrange("p (h d) -> p h d", h=8),
                                                  axis=mybir.AxisListType.X, op=ALU.add), reads=[SQ.d], writes=[SS.d])
            S.op("act", lambda e: e.activation(out=RSTD.t[:], in_=SS.t[:], func=AF.Sqrt, bias=EPS, scale=1.0 / 64),
                 reads=[SS.d], writes=[RSTD.d])
            S.op("dve", lambda e: e.reciprocal(out=RSTD.t[:], in_=RSTD.t[:]), reads=[RSTD.d], writes=[RSTD.d])
            S.op("dve", lambda e, pin=pin: e.tensor_tensor(out=QN.t[:, :].rearrange("p (h d) -> p h d", h=8),
                                                           in0=pin.t[:, :].rearrange("p (h d) -> p h d", h=8),
                                                           in1=RSTD.t[:, :].unsqueeze(2).broadcast_to([128, 8, 64]), op=ALU.mult),
                 reads=[pin.d, RSTD.d], writes=[QN.d])
            S.op("dve", lambda e: e.tensor_tensor(out=QN.t[:], in0=QN.t[:], in1=AC.t[:, 0:512], op=ALU.mult),
                 reads=[QN.d, AC.d], writes=[QN.d])
            qk = QN.t[:, :].rearrange("p (a h d) -> p a h d", a=2, h=4)

            def tab(rp, off):
                return rp.t[:, :].rearrange("p (a f) -> p a f", a=2)[:, :, off:off + 32].unsqueeze(2).broadcast_to([128, 2, 4, 32])
            v4 = lambda b: b.t[:, :, :].rearrange("p (a h) d -> p a h d", a=2)
            t1 = qk[:, :, :, 0:32]; t2 = qk[:, :, :, 32:64]
            o1 = QKR.t[:, :, 0:32].rearrange("p (a h) d -> p a h d", a=2)
            o2 = QKR.t[:, :, 32:64].rearrange("p (a h) d -> p a h d", a=2)
            S.op("dve", lambda e, rp=rp, t1=t1: e.tensor_tensor(out=v4(TA), in0=t1, in1=tab(rp, 0), op=ALU.mult),
                 reads=[QN.d, rp.d], writes=[TA.d])
            S.op("dve", lambda e, rp=rp, t2=t2: e.tensor_tensor(out=v4(TB), in0=t2, in1=tab(rp, 32), op=ALU.mult),
                 reads=[QN.d, rp.d], writes=[TB.d])
            S.op("dve", lambda e, o1=o1: e.tensor_tensor(out=o1, in0=v4(TA), in1=v4(TB), op=ALU.subtract),
                 reads=[TA.d, TB.d], writes=[QKR.d])
            S.op("dve", lambda e, rp=rp, t1=t1: e.tensor_tensor(out=v4(TA), in0=t1, in1=tab(rp, 32), op=ALU.mult),
                 reads=[QN.d, rp.d], writes=[TA.d])
            S.op("dve", lambda e, rp=rp, t2=t2: e.tensor_tensor(out=v4(TB), in0=t2, in1=tab(rp, 0), op=ALU.mult),
                 reads=[QN.d, rp.d], writes=[TB.d])
            S.op("dve", lambda e, o2=o2: e.tensor_tensor(out=o2, in0=v4(TA), in1=v4(TB), op=ALU.add),
                 reads=[TA.d, TB.d], writes=[QKR.d])
            for a in range(4):
                S.op("pe", lambda e, a=a: e.transpose(out=bfv(bank[0])[:, a * 128:(a + 1) * 128],
                                                      in_=QKR.t[:, 2 * a:2 * a + 2, :].rearrange("p h d -> p (h d)"), identity=IDB.t[:]),
                     reads=[QKR.d, IDB.d], writes=[bank[0].d], signal=(a == 3))
            S.op("act", lambda e, c=c: e.activation(out=QT.t[:, :, c * 128:(c + 1) * 128],
                                                    in_=bfv(bank[0])[:, 0:256].rearrange("p (a n) -> p a n", a=2), func=AF.Copy),
                 reads=[bank[0].d], writes=[QT.d])
            S.op("act", lambda e, c=c: e.activation(out=KT.t[:, :, c * 128:(c + 1) * 128],
                                                    in_=bfv(bank[0])[:, 256:512].rearrange("p (a n) -> p a n", a=2), func=AF.Copy),
                 reads=[bank[0].d], writes=[KT.d])

        ti = 0
        for p in range(2):
            for j in range(2):
                S.op("pool", lambda e, j=j: e.memset(ACC[j].t[:], 0.0), writes=[ACC[j].d])
            for d in (1, 4, 16):
                L = Stok // d
                for r in range(d):
                    for blk in range(L // 128):
                        vf = VF[ti % 2]; ve = VE[ti % 3]; vprev = VE[(ti - 1) % 3]
                        t0 = blk * 128 * d + r
                        S.dma("sp", lambda e, vf=vf, t0=t0, d=d, p=p: e.dma_start(
                            out=vf.t[:], in_=proj[t0:t0 + 127 * d + 1:d, OAV + p * 128:OAV + (p + 1) * 128].rearrange("t (h e) -> t h e", h=2)),
                            writes=[vf.d])
                        S.op("act", lambda e, vf=vf, ve=ve: e.activation(out=ve.t[:, :, 0:64], in_=vf.t[:], func=AF.Copy), reads=[vf.d], writes=[ve.d])
                        tok = slice(t0, t0 + 127 * d + 1, d)
                        tokp = slice(t0 - 128 * d, t0 - d + 1, d)
                        for j in range(2):
                            pr = slice(j * 64, (j + 1) * 64)
                            sb_ = bank[1 + j + 2 * (ti % 2)]
                            ob = bank[(5 + j) if ti % 2 == 0 else (0 if j == 0 else 7)]
                            pt = PT[j][ti % 2]
                            ncol = 256 if blk > 0 else 128
                            S.op("pe", lambda e, sb_=sb_, pr=pr, p=p, tok=tok: e.matmul(
                                sb_.t[:, 0:128], lhsT=KT.t[pr, p, tok], rhs=QT.t[pr, p, tok], start=True, stop=True),
                                reads=[KT.d, QT.d], writes=[sb_.d], signal=(blk == 0))
                            if blk > 0:
                                S.op("pe", lambda e, sb_=sb_, pr=pr, p=p, tok=tok, tokp=tokp: e.matmul(
                                    sb_.t[:, 128:256], lhsT=KT.t[pr, p, tokp], rhs=QT.t[pr, p, tok], start=True, stop=True),
                                    reads=[KT.d, QT.d], writes=[sb_.d])
                            S.op("act", lambda e, pt=pt, sb_=sb_, ncol=ncol: e.activation(out=pt.t[:, 0:ncol], in_=sb_.t[:, 0:ncol], func=AF.Exp),
                                 reads=[sb_.d], writes=[pt.d])
                            meng = "dve"
                            S.op(meng, lambda e, pt=pt, ncol=ncol: e.tensor_tensor(out=pt.t[:, 0:ncol], in0=pt.t[:, 0:ncol], in1=MSK.t[:, 0:ncol],
                                                                                   op=ALU.mult),
                                 reads=[pt.d, MSK.d], writes=[pt.d])
                            S.op("pe", lambda e, ob=ob, ve=ve, j=j, pt=pt, blk=blk: e.matmul(
                                ob.t[0:65, 0:128], lhsT=ve.t[:, j, :], rhs=pt.t[:, 0:128], start=True, stop=(blk == 0)),
                                reads=[ve.d, pt.d], writes=[ob.d], signal=(blk == 0))
                            if blk > 0:
                                S.op("pe", lambda e, ob=ob, vprev=vprev, j=j, pt=pt: e.matmul(
                                    ob.t[0:65, 0:128], lhsT=vprev.t[:, j, :], rhs=pt.t[:, 128:256], start=False, stop=True),
                                    reads=[vprev.d, pt.d], writes=[ob.d])
                            S.op("dve", lambda e, j=j, ob=ob, tok=tok: e.tensor_tensor(
                                out=ACC[j].t[0:65, tok], in0=ACC[j].t[0:65, tok], in1=ob.t[0:65, 0:128], op=ALU.add),
                                reads=[ACC[j].d, ob.d], writes=[ACC[j].d])
                        ti += 1
            for j in range(2):
                h = 2 * p + j
                for n0 in range(0, Stok, 512):
                    yo = YO[(n0 // 2048) % 2]
                    S.op("pe", lambda e, j=j, n0=n0: e.matmul(bank[7].t[0:64, :], lhsT=AC.t[0:65, 768:832], rhs=ACC[j].t[0:65, n0:n0 + 512],
                                                              start=True, stop=True),
                         reads=[AC.d, ACC[j].d], writes=[bank[7].d])
                    S.op("dve", lambda e: e.reciprocal(out=RD.t[:], in_=bank[7].t[0:64, :]), reads=[bank[7].d], writes=[RD.d])
                    S.op("dve", lambda e, j=j, n0=n0, yo=yo: e.tensor_tensor(out=yo.t[:, n0 % 2048:n0 % 2048 + 512], in0=ACC[j].t[0:64, n0:n0 + 512],
                                                                             in1=RD.t[:], op=ALU.mult),
                         reads=[ACC[j].d, RD.d], writes=[yo.d])
                    if (n0 + 512) % 2048 == 0 or n0 + 512 == Stok:
                        nb0 = (n0 // 2048) * 2048; nn = n0 + 512 - nb0
                        o = T("out"); C.outs.append(o)
                        S.dma("pool", lambda e, yo=yo, h=h, nb0=nb0, nn=nn: e.dma_start(out=yT[h * 64:(h + 1) * 64, nb0:nb0 + nn], in_=yo.t[:, 0:nn]),
                              reads=[yo.d], writes=[o])


def make_wconv(C, tiles, per_chunk, wf, wb, nbuf=2):
    S = C.S
    TW_ = wf.shape[1]
    FB = [C.sb("f%d" % i, [128, TW_]) for i in range(nbuf)]
    BB = [C.sb("b%d" % i, [128, TW_], BF16) for i in range(nbuf)]
    cnt = [0]

    def chunk(c):
        for i in tiles[c * per_chunk:(c + 1) * per_chunk]:
            k = cnt[0]; cnt[0] += 1
            f = FB[k % nbuf]; b = BB[k % nbuf]
            S.dma("sp", lambda e, f=f, i=i: e.dma_start(out=f.t[:], in_=wf[i * 128:(i + 1) * 128, :]), writes=[f.d])
            if k % 2 == 0:
                S.op("act", lambda e, f=f, b=b: e.activation(out=b.t[:], in_=f.t[:], func=AF.Copy), reads=[f.d], writes=[b.d])
            else:
                S.op("dve", lambda e, f=f, b=b: e.tensor_copy(out=b.t[:], in_=f.t[:]), reads=[f.d], writes=[b.d])
            o = T("out"); C.outs.append(o)
            S.dma("pool", lambda e, b=b, i=i: e.dma_start(out=wb[i * 128:(i + 1) * 128, :], in_=b.t[:]), reads=[b.d], writes=[o])
    return chunk


def emit_wconv(C, tiles, wf, wb):
    ch = make_wconv(C, tiles, len(tiles), wf, wb, nbuf=3)
    ch(0)


def emit_norm_prep(C, xt_ap, xt_dep, XN, SQ, SS, RSTD):
    S = C.S
    S.op("act", lambda e: e.activation(out=SQ.t[:], in_=xt_ap, func=AF.Square, accum_out=SS.t[:]), reads=[xt_dep], writes=[SQ.d, SS.d])
    S.op("act", lambda e: e.activation(out=RSTD.t[:], in_=SS.t[:], func=AF.Sqrt, bias=EPS, scale=1.0 / D), reads=[SS.d], writes=[RSTD.d])
    S.op("dve", lambda e: e.reciprocal(out=RSTD.t[:], in_=RSTD.t[:]), reads=[RSTD.d], writes=[RSTD.d])
    S.op("act", lambda e: e.activation(out=XN.t[:], in_=xt_ap, func=AF.Copy, scale=RSTD.t[:, 0:1]),
         reads=[xt_dep, RSTD.d], writes=[XN.d])


def emit_norm_tr(C, XN, GAIN, ident, CST, HNT, col0, banks):
    S, bank = C.S, C.bank
    for g in range(4):
        bk = bank[banks[g % len(banks)]]
        for q in range(4):
            k = 4 * g + q
            S.op("pe", lambda e, bk=bk, q=q, k=k: e.transpose(out=bfv(bk)[:, q * 128:(q + 1) * 128], in_=XN.t[:, k * 128:(k + 1) * 128],
                                                              identity=ident.t[:]),
                 reads=[XN.d, ident.d], writes=[bk.d], signal=(q == 3))
        S.op("dve", lambda e, bk=bk, g=g: e.tensor_tensor(out=HNT.t[:, 4 * g:4 * g + 4, col0:col0 + 128],
                                                          in0=bfv(bk)[:, 0:512].rearrange("p (q n) -> p q n", q=4),
                                                          in1=GAIN.t[:, 4 * g:4 * g + 4].unsqueeze(2).broadcast_to([128, 4, 128]), op=ALU.mult),
             reads=[bk.d, GAIN.d], writes=[HNT.d])


def emit_norm_transpose(C, xt_ap, xt_dep, XN, SQ, SS, RSTD, GAIN, ident, CST, HNT, col0, banks):
    emit_norm_prep(C, xt_ap, xt_dep, XN, SQ, SS, RSTD)
    emit_norm_tr(C, XN, GAIN, ident, CST, HNT, col0, banks)


def emit_inproj(C, NCH, x, cst_d, g_d, w_d, proj):
    if True:
        nc, S, bank = C.nc, C.S, C.bank
        Stok = NCH * 128
        CST = C.sb("cst", [128, 512]); GAIN = C.sb("gain", [128, 16])
        ident = CST.t[:, 0:128]
        WB = C.sb("wb", [128, 16, NPROJ], BF16)
        XT = [C.sb("xt%d" % i, [128, D]) for i in range(3)]
        XN = C.sb("xn", [128, D], BF16); SQ = C.sb("sq", [128, D]); SS = C.sb("ss", [128, 1]); RSTD = C.sb("rstd", [128, 1])
        HNT = [C.sb("hnt%d" % i, [128, 16, 128], BF16) for i in range(2)]
        OUT = [C.sb("out%d" % i, [128, NPROJ]) for i in range(2)]
        S.dma("sp", lambda e: e.dma_start(out=CST.t[:], in_=cst_d[:, :]), writes=[CST.d])
        S.dma("sp", lambda e: e.dma_start(out=GAIN.t[:], in_=g_d[:, :]), writes=[GAIN.d])
        ident = make_identb(C, CST)
        wv = w_d.rearrange("(k p) n -> p k n", p=128)
        for k in range(16):
            S.dma("act" if k % 2 else "sp", lambda e, k=k: e.dma_start(out=WB.t[:, k, :], in_=wv[:, k, :]), writes=[WB.d])
        ncols = [(i * 512, min(512, NPROJ - i * 512)) for i in range(6)]

        def load_x(c):
            xt = XT[c % 3]
            S.dma("sp", lambda e, xt=xt, c=c: e.dma_start(out=xt.t[:], in_=x[c * 128:(c + 1) * 128, :]), writes=[xt.d])

        def mm_groups(c, groups):
            hnt = HNT[c % 2]; out = OUT[c % 2]
            for i in groups:
                n0, nw = ncols[i]
                bk = bank[2 + (c * 6 + i) % 6]
                for k in range(16):
                    S.op("pe", lambda e, bk=bk, k=k, n0=n0, nw=nw, hnt=hnt: e.matmul(bk.t[:, 0:nw], lhsT=hnt.t[:, k, :], rhs=WB.t[:, k, n0:n0 + nw],
                                                                                  start=(k == 0), stop=(k == 15)),
                         reads=[hnt.d, WB.d], writes=[bk.d], signal=(k == 15))
                if i % 2 == 0:
                    S.op("act", lambda e, bk=bk, n0=n0, nw=nw, out=out: e.activation(out=out.t[:, n0:n0 + nw], in_=bk.t[:, 0:nw], func=AF.Copy),
                         reads=[bk.d], writes=[out.d])
                else:
                    S.op("dve", lambda e, bk=bk, n0=n0, nw=nw, out=out: e.tensor_copy(out=out.t[:, n0:n0 + nw], in_=bk.t[:, 0:nw]),
                         reads=[bk.d], writes=[out.d])

        load_x(0)
        if NCH > 1:
            load_x(1)
        emit_norm_prep(C, XT[0].t[:], XT[0].d, XN, SQ, SS, RSTD)
        emit_norm_tr(C, XN, GAIN, ident, CST, HNT[0], 0, (0, 1))
        for c in range(NCH):
            if c + 2 < NCH:
                load_x(c + 2)
            if c + 1 < NCH and EXP != "noprep":
                emit_norm_prep(C, XT[(c + 1) % 3].t[:], XT[(c + 1) % 3].d, XN, SQ, SS, RSTD)
            mm_groups(c, (0, 1, 2))
            if c + 1 < NCH and EXP != "notr":
                emit_norm_tr(C, XN, GAIN, ident, CST, HNT[(c + 1) % 2], 0, (0, 1))
            mm_groups(c, (3, 4, 5))
            out = OUT[c % 2]
            o = T("out"); C.outs.append(o)
            S.dma("pool", lambda e, out=out, c=c: e.dma_start(out=proj[c * 128:(c + 1) * 128, :], in_=out.t[:]), reads=[out.d], writes=[o])


def emit_outffn(C, NT, x, yT, cst_d, g_d, wo_d, wg_d, wu_d, wd_d, xo):
    if True:
        nc, S, bank = C.nc, C.S, C.bank
        NBLK = NT // 512
        NF = DFF // 128
        CST = C.sb("cst", [128, 512]); GAIN = C.sb("gain", [128, 16])
        ident = CST.t[:, 0:128]
        X1 = C.sb("x1", [128, 4, D])
        YT = C.sb("yt", [128, 16, 512], BF16)
        HNT = C.sb("hnt", [128, 16, 512], BF16)
        HT = C.sb("ht", [128, NF, 512], BF16)
        XN = C.sb("xn", [128, D], BF16); SQ = C.sb("sq", [128, D]); SS = C.sb("ss", [128, 1]); RSTD = C.sb("rstd", [128, 1])
        WS = [C.sb("ws%d" % i, [128, 1024], BF16) for i in range(4)]
        WG = [C.sb("wg%d" % i, [128, 16, 128], BF16) for i in range(3)]
        WU = [C.sb("wu%d" % i, [128, 16, 128], BF16) for i in range(3)]
        SG = [C.sb("sg%d" % i, [128, 512]) for i in range(2)]
        S.dma("sp", lambda e: e.dma_start(out=CST.t[:], in_=cst_d[:, :]), writes=[CST.d])
        S.dma("sp", lambda e: e.dma_start(out=GAIN.t[:], in_=g_d[:, :]), writes=[GAIN.d])
        ident = make_identb(C, CST)
        yT_v = yT.rearrange("(k p) n -> p k n", p=128)
        wsi = [0]

        def gemm_res(lhs, lhs_dep, KCH, wdram):
            for half in range(2):
                for k in range(KCH):
                    ws = WS[wsi[0] % 4]; q = "sp" if wsi[0] % 2 == 0 else "act"; wsi[0] += 1
                    S.dma(q, lambda e, ws=ws, k=k, half=half: e.dma_start(out=ws.t[:], in_=wdram[k * 128:(k + 1) * 128, half * 1024:(half + 1) * 1024]),
                          writes=[ws.d])
                    for t in range(4):
                        for nn in range(2):
                            bk = bank[t * 2 + nn]
                            S.op("pe", lambda e, bk=bk, k=k, t=t, nn=nn, ws=ws: e.matmul(bk.t[:, :], lhsT=lhs(k, t), rhs=ws.t[:, nn * 512:(nn + 1) * 512],
                                                                                      start=(k == 0), stop=(k == KCH - 1)),
                                 reads=[lhs_dep, ws.d], writes=[bk.d], signal=(k == KCH - 1 or (t == 3 and nn == 1)))
                for t in range(4):
                    for nn in range(2):
                        bk = bank[t * 2 + nn]; c0 = half * 1024 + nn * 512
                        S.op("dve", lambda e, bk=bk, t=t, c0=c0: e.tensor_tensor(out=X1.t[:, t, c0:c0 + 512], in0=X1.t[:, t, c0:c0 + 512],
                                                                                 in1=bk.t[:, :], op=ALU.add),
                             reads=[X1.d, bk.d], writes=[X1.d])

        for b in range(NBLK):
            t0 = b * 512
            S.dma("sp", lambda e, t0=t0: e.dma_start(out=X1.t[:], in_=x[t0:t0 + 512, :].rearrange("(t p) n -> p t n", p=128)), writes=[X1.d])
            S.dma("act", lambda e, t0=t0: e.dma_start(out=YT.t[:], in_=yT_v[:, :, t0:t0 + 512]), writes=[YT.d])
            gemm_res(lambda k, t: YT.t[:, k, t * 128:(t + 1) * 128], YT.d, 16, wo_d)
            for t in range(4):
                emit_norm_transpose(C, X1.t[:, t, :], X1.d, XN, SQ, SS, RSTD, GAIN, ident, CST, HNT, t * 128, (0, 1, 2, 3))
            for f in range(NF):
                wg = WG[f % 3]; wu = WU[f % 3]; sg = SG[f % 2]
                S.dma("sp", lambda e, wg=wg, f=f: e.dma_start(out=wg.t[:], in_=wg_d[f].rearrange("p (k c) -> p k c", k=16)), writes=[wg.d])
                S.dma("act", lambda e, wu=wu, f=f: e.dma_start(out=wu.t[:], in_=wu_d[f].rearrange("p (k c) -> p k c", k=16)), writes=[wu.d])
                ba = bank[4 + (f % 2) * 2]; bb = bank[5 + (f % 2) * 2]
                for k in range(16):
                    S.op("pe", lambda e, ba=ba, wg=wg, k=k: e.matmul(ba.t[:, :], lhsT=wg.t[:, k, :], rhs=HNT.t[:, k, :], start=(k == 0), stop=(k == 15)),
                         reads=[wg.d, HNT.d], writes=[ba.d], signal=(k == 15))
                for k in range(16):
                    S.op("pe", lambda e, bb=bb, wu=wu, k=k: e.matmul(bb.t[:, :], lhsT=wu.t[:, k, :], rhs=HNT.t[:, k, :], start=(k == 0), stop=(k == 15)),
                         reads=[wu.d, HNT.d], writes=[bb.d], signal=(k == 15))
                S.op("act", lambda e, ba=ba, sg=sg: e.activation(out=sg.t[:], in_=ba.t[:, :], func=AF.Silu), reads=[ba.d], writes=[sg.d])
                S.op("dve", lambda e, bb=bb, sg=sg, f=f: e.tensor_tensor(out=HT.t[:, f, :], in0=bb.t[:, :], in1=sg.t[:], op=ALU.mult),
                     reads=[bb.d, sg.d], writes=[HT.d])
            gemm_res(lambda k, t: HT.t[:, k, t * 128:(t + 1) * 128], HT.d, NF, wd_d)
            o = T("out"); C.outs.append(o)
            S.dma("pool", lambda e, t0=t0: e.dma_start(out=xo[t0:t0 + 512, :].rearrange("(t p) n -> p t n", p=128), in_=X1.t[:]),
                  reads=[X1.d], writes=[o])


def gate_layout(w):
    return np.ascontiguousarray(w.reshape(16, 128, DFF // 128, 128).transpose(2, 1, 0, 3)).reshape(DFF // 128, 128, 2048)


NL = 2
W_SHAPES = [(D, NPROJ), (D, NPROJ), (D, D), (DFF // 128, 128, 2048), (DFF // 128, 128, 2048), (DFF, D)]
W_SIZES = [int(np.prod(sh)) for sh in W_SHAPES]
LW = sum(W_SIZES) // 128
TW = 3392


def build_fused(NCH=SEQ // 128):
    st = ExitStack()
    with st:
        C = Ctx(st)
        nc, S = C.nc, C.S
        Stok = NCH * 128
        x_in = C.din("x", [Stok, D])
        cst_d = C.din("cst", [128, 512])
        rope_d = C.din("rope", [Stok, 128])
        arope_d = C.din("arope", [Stok, 128])
        sprm_d = C.din("ssd_prm", [NL * 2 * 128, 64])
        retc_d = C.din("ret_c", [NL * 2 * 128, 1024])
        attc_d = C.din("att_c", [NL * 128, 1024])
        gains_d = C.din("gains", [NL * 2 * 128, 16])
        NTILE = LW * 128 // (128 * TW)
        wf = [C.din("wf%d" % l, [NTILE * 128, TW]) for l in range(NL)]
        out = C.dout("out", [Stok, D])
        wbs = [C.scratch("wb%d" % l, [NTILE * 128, TW], BF16) for l in range(NL)]
        projs = [C.scratch("proj_s%d" % h, [Stok, NPROJ]) for h in range(2)]
        yT = C.scratch("yT_s", [D, Stok], BF16)
        xs = C.scratch("xs", [Stok, D])
        wvs = []
        for l in range(NL):
            wbf = wbs[l].rearrange("p n -> (p n)")
            wv, off = [], 0
            for sh, n in zip(W_SHAPES, W_SIZES):
                v = wbf[off:off + n]
                if len(sh) == 2:
                    v = v.rearrange("(r c) -> r c", c=sh[1])
                else:
                    v = v.rearrange("(f p c) -> f p c", p=sh[1], c=sh[2])
                wv.append(v); off += n
            wvs.append(wv)
        n_in = -(-(2 * W_SIZES[0]) // (128 * TW))
        C.run_phase(lambda: emit_wconv(C, list(range(n_in)), wf[0], wbs[0]))
        for l in range(NL):
            xl = x_in if l == 0 else xs
            xo = out if l == NL - 1 else xs
            wv = wvs[l]
            g_mix = gains_d[(l * 2) * 128:(l * 2 + 1) * 128, :]
            for h in range(2):
                C.run_phase(lambda: emit_inproj(C, NCH, xl, cst_d, g_mix, wv[h], projs[h]))
            rr = [(l * 2 + h) * 128 for h in range(2)]
            C.run_phase(lambda: emit_multi(C, NCH, [
                (lambda h=h: make_ssd(C, NCH, projs[h], cst_d, sprm_d[rr[h]:rr[h] + 128, :], yT[h * 512:(h + 1) * 512, :])) for h in range(2)],
                [[0, 1, 2, 3, 0, 1, 3, 2], [4, 5, 6, 7, 4, 5, 7, 6]],
                extra=([lambda: make_wconv(C, list(range(n_in, NTILE)), -(-(NTILE - n_in) // NCH), wf[0], wbs[0])] if l == 0 else [])))
            for h in range(2):
                C.run_phase(lambda: emit_att(C, NCH, projs[h], cst_d, attc_d[l * 128:(l + 1) * 128, :], arope_d,
                                             yT[1024 + h * 256:1024 + (h + 1) * 256, :]))
            C.run_phase(lambda: emit_multi(C, NCH, [
                (lambda h=h: make_ret(C, NCH, projs[h], cst_d, retc_d[rr[h]:rr[h] + 128, :], rope_d,
                                      yT[1536 + h * 256:1536 + (h + 1) * 256, :])) for h in range(2)],
                [[0, 1, 3, 1, 2, 2, 0, 0], [4, 5, 7, 5, 6, 6, 4, 4]],
                extra=([lambda: make_wconv(C, list(range(NTILE)), -(-NTILE // NCH), wf[l + 1], wbs[l + 1])] if l + 1 < NL else [])))
            C.run_phase(lambda: emit_outffn(C, Stok, xl, yT, cst_d, gains_d[(l * 2 + 1) * 128:(l * 2 + 2) * 128, :],
                                            wv[2], wv[3], wv[4], wv[5], xo))
        return nc


def fused_inputs(inputs, S_=SEQ):
    x = np.ascontiguousarray(inputs["x"], dtype=np.float32)
    common = {"cst": const_mats(), "rope": rope_table(S_), "arope": att_rope_table(S_)}
    sprm = np.concatenate([ssd_params(inputs["conv_w"][l], inputs["conv_b"][l], inputs["dt_bias"][l], inputs["a_log"][l],
                                      inputs["d_skip"][l], inputs["ssd_norm"][l], h) for l in range(NL) for h in range(2)], axis=0)
    retc = np.concatenate([ret_consts(inputs["ret_norm"][l], h) for l in range(NL) for h in range(2)], axis=0)
    attc = np.concatenate([att_consts(inputs["q_norm"][l], inputs["k_norm"][l]) for l in range(NL)], axis=0)
    gains = np.concatenate([np.ascontiguousarray(inputs[k][l].reshape(16, 128).T) for l in range(NL) for k in ("ln_mix", "ln_ffn")], axis=0)
    common.update({"ssd_prm": sprm, "ret_c": retc, "att_c": attc, "gains": gains.astype(np.float32)})
    for l in range(NL):
        w_in = inputs["w_in"][l]
        lay = [w_in[:, half_cols(0)], w_in[:, half_cols(1)], inputs["w_out"][l], gate_layout(inputs["w_gate"][l]),
               gate_layout(inputs["w_up"][l]), inputs["w_down"][l]]
        common["wf%d" % l] = np.concatenate([np.asarray(a, np.float32).reshape(-1) for a in lay]).reshape(-1, TW)
    maps = []
    for c in range(NCORES):
        m = dict(common); m["x"] = np.ascontiguousarray(x[c // 2, :S_]); maps.append(m)
    return maps


NCORES = 8


def kernel(**inputs):
    inputs = {k: np.asarray(v) for k, v in inputs.items()}
    nc = build_fused()
    res = run_bass_kernel_spmd(nc, fused_inputs(inputs), core_ids=list(range(NCORES))).results
    return np.ascontiguousarray(np.stack([np.asarray(res[2 * b]["out"]) for b in range(NB)]), dtype=np.float32)
```

```python
import numpy as np
import ml_dtypes
from contextlib import ExitStack
import concourse.bass as bass
import concourse.mybir as mybir
from concourse.bass_utils import run_bass_kernel_spmd

F32 = mybir.dt.float32
BF16 = mybir.dt.bfloat16
AF = mybir.ActivationFunctionType
ALU = mybir.AluOpType
NPBF = ml_dtypes.bfloat16

SAME_ENG_SYNC = True
RET_STAGE = 99
EXP = ""
EPS = 1e-6

D = 2048
SEQ = 8192
NB = 4
NPROJ = 2824
DFF = 5632
OZ, OX, OB, OC, ODT, OAQ, OAK, OAV, ORQ, ORK, ORV, ORG = 0, 512, 1024, 1152, 1280, 1288, 1544, 1800, 2056, 2184, 2312, 2568


class T:
    __slots__ = ("name", "lw", "rd", "excl")

    def __init__(self, name="", excl=False):
        self.name = name
        self.lw = None
        self.rd = []
        self.excl = excl


class Buf:
    __slots__ = ("t", "d")

    def __init__(self, t, name=""):
        self.t = t
        self.d = T(name)


class Sched:
    ENGS = ("pe", "act", "dve", "pool", "sp")

    def __init__(self, nc, stack, ndma=16):
        self.nc = nc
        self.prog = {e: [] for e in self.ENGS}
        self.sem = {}
        self.cnt = {}
        self.known = {e: {} for e in self.ENGS}
        for e in self.ENGS:
            self.sem[e] = stack.enter_context(nc.semaphore("s_" + e))
            self.cnt[e] = 0
        self.dsem = {}
        self.dcount = {}
        self.ndma = ndma
        for e in ("sp", "pool", "act"):
            self.dsem[e] = [stack.enter_context(nc.semaphore("d_%s_%d" % (e, i))) for i in range(ndma)]
            self.dcount[e] = 0
        self.ninst = 0

    def _deps(self, eng, reads, writes):
        need = {}
        for t in reads:
            if t.lw is not None:
                k, v = t.lw
                if need.get(k, 0) < v:
                    need[k] = v
        for t in writes:
            if t.lw is not None:
                k, v = t.lw
                if need.get(k, 0) < v:
                    need[k] = v
            for k, v in t.rd:
                if need.get(k, 0) < v:
                    need[k] = v
        waits = []
        kn = self.known[eng]
        for k, v in need.items():
            if k == eng and (not SAME_ENG_SYNC or eng == "pe" or eng == "sp"):
                continue
            if kn.get(k, 0) >= v:
                continue
            kn[k] = v
            waits.append((k, v))
        return waits

    def _semof(self, k):
        if isinstance(k, tuple):
            return self.dsem[k[0]][k[1]]
        return self.sem[k]

    @staticmethod
    def _compact(t, key, ticket):
        t.rd = [(k, v) for (k, v) in t.rd if k != key]
        t.rd.append((key, ticket))

    def op(self, eng, fn, reads=(), writes=(), signal=True):
        ex = [t for t in reads if t.excl]
        if ex:
            reads = [t for t in reads if not t.excl]
            writes = list(writes) + ex
        waits = self._deps(eng, reads, writes)
        if signal:
            self.cnt[eng] += 1
            ticket = self.cnt[eng]
        else:
            ticket = self.cnt[eng] + 1
        sem = self.sem[eng]
        wl = [(self._semof(k), v) for k, v in waits]

        def thunk(e, fn=fn, wl=wl, signal=signal, sem=sem):
            for s, v in wl:
                e.wait_ge(s, v)
            ins = fn(e)
            if signal:
                ins.then_inc(sem, 1)
        self.prog[eng].append(thunk)
        for t in reads:
            self._compact(t, eng, ticket)
        for t in writes:
            t.lw = (eng, ticket)
            t.rd = []
        self.ninst += 1

    def dma(self, eng, fn, reads=(), writes=()):
        i = self.dcount[eng]
        self.dcount[eng] += 1
        slot = i % self.ndma
        ticket = 16 * (i // self.ndma + 1)
        key = (eng, slot)
        waits = self._deps(eng, reads, writes)
        if i >= self.ndma:
            kn = self.known[eng]
            if kn.get(key, 0) < ticket - 16:
                kn[key] = ticket - 16
                waits.append((key, ticket - 16))
        sem = self.dsem[eng][slot]
        wl = [(self._semof(k), v) for k, v in waits]

        def thunk(e, fn=fn, wl=wl, sem=sem):
            for s, v in wl:
                e.wait_ge(s, v)
            fn(e).then_inc(sem, 16)
        self.prog[eng].append(thunk)
        for t in reads:
            self._compact(t, key, ticket)
        for t in writes:
            t.lw = (key, ticket)
            t.rd = []
        self.ninst += 1

    def barrier(self):
        targets = [(e, self.cnt[e]) for e in self.ENGS if self.cnt[e] > 0]
        for q in self.dsem:
            n = self.dcount[q]
            for slot in range(min(n, self.ndma)):
                cnt = (n - slot + self.ndma - 1) // self.ndma
                targets.append(((q, slot), 16 * cnt))
        for eng in self.ENGS:
            kn = self.known[eng]
            wl = []
            for k, v in targets:
                if k == eng and eng in ("pe", "sp"):
                    continue
                if kn.get(k, 0) >= v:
                    continue
                kn[k] = v
                wl.append((self._semof(k), v))

            def thunk(e, wl=wl):
                for s_, v in wl:
                    e.wait_ge(s_, v)
            self.prog[eng].append(thunk)

    def finish(self, eng, tiles):
        waits = self._deps(eng, tiles, ())
        wl = [(self._semof(k), v) for k, v in waits]

        def thunk(e, wl=wl):
            for s, v in wl:
                e.wait_ge(s, v)
        self.prog[eng].append(thunk)

    def replay(self, block):
        prog = self.prog

        @block.tensor
        def _(e):
            for th in prog["pe"]:
                th(e)

        @block.scalar
        def _(e):
            for th in prog["act"]:
                th(e)

        @block.vector
        def _(e):
            for th in prog["dve"]:
                th(e)

        @block.gpsimd
        def _(e):
            for th in prog["pool"]:
                th(e)

        @block.sync
        def _(e):
            for th in prog["sp"]:
                th(e)


class Ctx:
    def __init__(self, st):
        self.nc = bass.Bass("TRN2", target_bir_lowering=False)
        self.st = st
        self.S = Sched(self.nc, st)
        self.outs = []
        self.phase_id = 0
        self.tag = ""
        self.bankmap = None
        self.bank = [Buf(st.enter_context(self.nc.psum_tensor("bank%d" % i, [128, 512], F32)), "bank%d" % i)
                     for i in range(8)]
        for b in self.bank:
            b.d.excl = True

    def sb(self, name, shape, dt=F32):
        return Buf(self.st.enter_context(self.nc.sbuf_tensor("sb%d%s_%s" % (self.phase_id, self.tag, name), shape, dt)), name)

    def run_phase(self, fn):
        old = self.st
        self.phase_id += 1
        with ExitStack() as pst:
            self.st = pst
            fn()
            self.S.barrier()
            with self.nc.Block() as block:
                self.S.replay(block)
            self.S.prog = {e: [] for e in self.S.ENGS}
        self.st = old

    def scratch(self, name, shape, dt=F32):
        return self.nc.dram_tensor(name, list(shape), dt, kind="Internal").ap()

    def din(self, name, shape, dt=F32):
        return self.nc.dram_tensor(name, list(shape), dt, kind="ExternalInput").ap()

    def dout(self, name, shape, dt=F32):
        return self.nc.dram_tensor(name, list(shape), dt, kind="ExternalOutput").ap()

    def done(self):
        self.S.finish("pool", self.outs)
        self.S.finish("sp", self.outs)
        with self.nc.Block() as block:
            self.S.replay(block)
        return self.nc


def bfv(bk):
    return bk.t[:, :].bitcast(BF16)


def make_identb(C, CST):
    IDB = C.sb("idb", [128, 128], BF16)
    C.S.op("act", lambda e: e.activation(out=IDB.t[:], in_=CST.t[:, 0:128], func=AF.Copy), reads=[CST.d], writes=[IDB.d])
    return IDB


def const_mats():
    i = np.arange(128)
    ident = (i[:, None] == i[None, :])
    tri = (i[:, None] <= i[None, :])
    strict = (i[:, None] > i[None, :])
    ones = np.ones((128, 128), bool)
    return np.concatenate([ident, tri, strict, ones], axis=1).astype(np.float32)


def ssd_params(conv_w, conv_b, dt_bias, a_log, d_skip, ssd_norm, h):
    ch = np.concatenate([np.arange(h * 512, (h + 1) * 512),
                         1024 + h * 128 + np.arange(128),
                         1280 + h * 128 + np.arange(128)])
    cw = conv_w[:, ch]
    cb = conv_b[ch]
    prm = np.zeros((128, 64), np.float32)
    prm[:, 0:24] = cw.reshape(4, 6, 128).transpose(2, 1, 0).reshape(128, 24)
    prm[:, 24:30] = cb.reshape(6, 128).T
    prm[:, 30:38] = np.broadcast_to(dt_bias[h * 8:(h + 1) * 8], (128, 8))
    prm[:, 38:46] = np.broadcast_to(a_log[h * 8:(h + 1) * 8], (128, 8))
    prm[:, 46:54] = np.broadcast_to(d_skip[h * 8:(h + 1) * 8], (128, 8))
    prm[:, 54:58] = ssd_norm[h * 512:(h + 1) * 512].reshape(4, 128).T
    return prm


def make_ssd(C, NCH, proj, cst_d, prm_d, yT):
    if True:
        nc, S = C.nc, C.S
        bank = [C.bank[i] for i in (C.bankmap or range(8))]
        Stok = NCH * 128
        yT_v = yT.rearrange("(t p) n -> p t n", p=128)

        CST = C.sb("cst", [128, 512]); PRM = C.sb("prm", [128, 64])
        ident = CST.t[:, 0:128]; tri = CST.t[:, 128:256]; strict = CST.t[:, 256:384]; ones = CST.t[:, 384:512]
        PIN = [C.sb("pin%d" % i, [128, 1288]) for i in range(2)]
        XB = [C.sb("xb%d" % i, [128, 6, 131]) for i in range(2)]
        CVX = C.sb("cvx", [128, 4, 128]); CVBC = C.sb("cvbc", [128, 2, 128])
        XA = C.sb("xa", [128, 4, 128]); BCA = C.sb("bca", [128, 2, 128]); BCT = C.sb("bct", [128, 2, 128], BF16)
        BTOK = C.sb("btok", [128, 128], BF16)
        DTV = C.sb("dtv", [128, 8]); DTE = C.sb("dte", [128, 8]); DT = C.sb("dt", [128, 8]); DA = C.sb("da", [128, 8])
        ANEG = C.sb("aneg", [128, 8]); ACS = C.sb("acs", [128, 16]); EAC = C.sb("eac", [128, 8]); CD = C.sb("cd", [128, 8])
        TMP8 = C.sb("tmp8", [128, 8]); DTEND = C.sb("dtend", [128, 8]); DTDTE = C.sb("dtdte", [128, 8])
        XG = C.sb("xg", [128, 512], BF16); XW = C.sb("xw", [128, 512], BF16); SKIP = C.sb("skip", [128, 512])
        U = C.sb("u", [128, 8, 128]); E = C.sb("e", [128, 8, 128]); CBM = C.sb("cbm", [128, 128])
        M = C.sb("m", [128, 8, 128], BF16)
        Y1 = C.sb("y1", [128, 512]); PREV = C.sb("prev", [128, 512]); PREVB = C.sb("prevb", [128, 512], BF16)
        SZ = C.sb("sz", [128, 512]); YZ = C.sb("yz", [128, 512]); SQ = C.sb("sq", [128, 512])
        SS = C.sb("ss", [128, 1]); RSTD = C.sb("rstd", [128, 1]); YN = C.sb("yn", [128, 512])
        YT = [C.sb("yt%d" % i, [128, 4, 512], BF16) for i in range(2)]

        S.dma("sp", lambda e: e.dma_start(out=CST.t[:], in_=cst_d[:, :]), writes=[CST.d])
        S.dma("sp", lambda e: e.dma_start(out=PRM.t[:], in_=prm_d[:, :]), writes=[PRM.d])
        S.op("pool", lambda e: e.memset(XB[0].t[:], 0.0), writes=[XB[0].d])
        S.op("pool", lambda e: e.memset(XB[1].t[:], 0.0), writes=[XB[1].d])
        S.op("pool", lambda e: e.memset(PREV.t[:], 0.0), writes=[PREV.d])
        S.op("pool", lambda e: e.memset(PREVB.t[:], 0.0), writes=[PREVB.d])
        S.op("act", lambda e: e.activation(out=ANEG.t[:], in_=PRM.t[:, 38:46], func=AF.Exp), reads=[PRM.d], writes=[ANEG.d])
        S.op("dve", lambda e: e.tensor_scalar(out=ANEG.t[:], in0=ANEG.t[:], scalar1=-1.0, scalar2=None, op0=ALU.mult),
             reads=[ANEG.d], writes=[ANEG.d])

        def bc8(b):
            return b.t[:, :].unsqueeze(2).broadcast_to([128, 8, 64])

        def v8(ap):
            return ap.rearrange("p (h d) -> p h d", h=8)

        def chunk(c):
            pin = PIN[c % 2]; xb = XB[c % 2]; xbn = XB[(c + 1) % 2]
            S.dma("sp", lambda e, pin=pin, c=c: e.dma_start(out=pin.t[:], in_=proj[c * 128:(c + 1) * 128, 0:1288]),
                  writes=[pin.d])
            for t in range(4):
                S.op("pe", lambda e, t=t, pin=pin: e.transpose(out=bank[0].t[:, t * 128:(t + 1) * 128],
                                                                 in_=pin.t[:, OX + t * 128:OX + (t + 1) * 128], identity=ident),
                     reads=[pin.d, CST.d], writes=[bank[0].d], signal=(t == 3))
            for t in range(2):
                S.op("pe", lambda e, t=t, pin=pin: e.transpose(out=bank[1].t[:, t * 128:(t + 1) * 128],
                                                                 in_=pin.t[:, OB + t * 128:OB + (t + 1) * 128], identity=ident),
                     reads=[pin.d, CST.d], writes=[bank[1].d], signal=(t == 1))
            S.op("act", lambda e, xb=xb: e.activation(out=xb.t[:, 0:4, 3:131], in_=bank[0].t[:, 0:512].rearrange("p (t n) -> p t n", t=4),
                                                       func=AF.Copy), reads=[bank[0].d], writes=[xb.d])
            S.op("act", lambda e, xb=xb: e.activation(out=xb.t[:, 4:6, 3:131], in_=bank[1].t[:, 0:256].rearrange("p (t n) -> p t n", t=2),
                                                       func=AF.Copy), reads=[bank[1].d], writes=[xb.d])
            for t in range(6):
                eng = "dve"
                cv = CVX if t < 4 else CVBC
                tt = t if t < 4 else t - 4
                S.op(eng, lambda e, t=t, tt=tt, cv=cv, xb=xb: e.tensor_scalar(
                    out=cv.t[:, tt, :], in0=xb.t[:, t, 0:128], scalar1=PRM.t[:, t * 4:t * 4 + 1],
                    scalar2=PRM.t[:, 24 + t:25 + t], op0=ALU.mult, op1=ALU.add),
                    reads=[xb.d, PRM.d], writes=[cv.d])
                for i in range(1, 4):
                    S.op(eng, lambda e, t=t, tt=tt, i=i, cv=cv, xb=xb: e.scalar_tensor_tensor(
                        out=cv.t[:, tt, :], in0=xb.t[:, t, i:i + 128], scalar=PRM.t[:, t * 4 + i:t * 4 + i + 1],
                        in1=cv.t[:, tt, :], op0=ALU.mult, op1=ALU.add),
                        reads=[xb.d, PRM.d, cv.d], writes=[cv.d])
            S.op("pool", lambda e, xb=xb, xbn=xbn: e.tensor_copy(out=xbn.t[:, :, 0:3], in_=xb.t[:, :, 128:131]),
                 reads=[xb.d], writes=[xbn.d])
            S.op("act", lambda e: e.activation(out=XA.t[:], in_=CVX.t[:], func=AF.Silu), reads=[CVX.d], writes=[XA.d])
            S.op("act", lambda e: e.activation(out=BCA.t[:], in_=CVBC.t[:], func=AF.Silu), reads=[CVBC.d], writes=[BCA.d])
            S.op("act", lambda e: e.activation(out=BCT.t[:], in_=BCA.t[:], func=AF.Copy), reads=[BCA.d], writes=[BCT.d])
            for t in range(4):
                S.op("pe", lambda e, t=t: e.transpose(out=bank[2].t[:, t * 128:(t + 1) * 128], in_=XA.t[:, t, :], identity=ident),
                     reads=[XA.d, CST.d], writes=[bank[2].d], signal=(t == 3))
            S.op("pe", lambda e: e.transpose(out=bank[3].t[:, 0:128], in_=BCA.t[:, 0, :], identity=ident),
                 reads=[BCA.d, CST.d], writes=[bank[3].d])
            S.op("act", lambda e: e.activation(out=BTOK.t[:], in_=bank[3].t[:, 0:128], func=AF.Copy),
                 reads=[bank[3].d], writes=[BTOK.d])
            S.op("dve", lambda e, pin=pin: e.tensor_tensor(out=DTV.t[:], in0=pin.t[:, ODT:ODT + 8], in1=PRM.t[:, 30:38], op=ALU.add),
                 reads=[pin.d, PRM.d], writes=[DTV.d])
            S.op("act", lambda e: e.activation(out=DTE.t[:], in_=DTV.t[:], func=AF.Exp), reads=[DTV.d], writes=[DTE.d])
            S.op("act", lambda e: e.activation(out=DT.t[:], in_=DTE.t[:], func=AF.Ln, bias=1.0), reads=[DTE.d], writes=[DT.d])
            S.op("dve", lambda e: e.tensor_tensor(out=DA.t[:], in0=DT.t[:], in1=ANEG.t[:], op=ALU.mult),
                 reads=[DT.d, ANEG.d], writes=[DA.d])
            S.op("pe", lambda e: e.matmul(bank[1].t[:, 256:264], lhsT=tri, rhs=DA.t[:], start=True, stop=True),
                 reads=[DA.d, CST.d], writes=[bank[1].d], signal=False)
            S.op("pe", lambda e: e.matmul(bank[1].t[:, 264:272], lhsT=ones, rhs=DA.t[:], start=True, stop=True),
                 reads=[DA.d, CST.d], writes=[bank[1].d])
            S.op("act", lambda e: e.activation(out=ACS.t[:], in_=bank[1].t[:, 256:272], func=AF.Copy),
                 reads=[bank[1].d], writes=[ACS.d])
            S.op("act", lambda e: e.activation(out=EAC.t[:], in_=ACS.t[:, 0:8], func=AF.Exp), reads=[ACS.d], writes=[EAC.d])
            S.op("act", lambda e: e.activation(out=CD.t[:], in_=ACS.t[:, 8:16], func=AF.Exp), reads=[ACS.d], writes=[CD.d])
            S.op("dve", lambda e: e.tensor_tensor(out=TMP8.t[:], in0=ACS.t[:, 8:16], in1=ACS.t[:, 0:8], op=ALU.subtract),
                 reads=[ACS.d], writes=[TMP8.d])
            S.op("act", lambda e: e.activation(out=DTEND.t[:], in_=TMP8.t[:], func=AF.Exp), reads=[TMP8.d], writes=[DTEND.d])
            S.op("dve", lambda e: e.tensor_tensor(out=DTDTE.t[:], in0=DT.t[:], in1=DTEND.t[:], op=ALU.mult),
                 reads=[DT.d, DTEND.d], writes=[DTDTE.d])
            S.op("dve", lambda e: e.tensor_tensor(out=v8(XG.t[:, :]), in0=v8(bank[2].t[:, :]), in1=bc8(DT), op=ALU.mult),
                 reads=[bank[2].d, DT.d], writes=[XG.d])
            S.op("dve", lambda e: e.tensor_tensor(out=v8(XW.t[:, :]), in0=v8(bank[2].t[:, :]), in1=bc8(DTDTE), op=ALU.mult),
                 reads=[bank[2].d, DTDTE.d], writes=[XW.d])
            S.op("dve", lambda e: e.tensor_tensor(out=v8(SKIP.t[:, :]), in0=v8(bank[2].t[:, :]),
                                                  in1=PRM.t[:, 46:54].unsqueeze(2).broadcast_to([128, 8, 64]), op=ALU.mult),
                 reads=[bank[2].d, PRM.d], writes=[SKIP.d])
            S.op("dve", lambda e: e.tensor_tensor(out=U.t[:], in0=strict.unsqueeze(1).broadcast_to([128, 8, 128]),
                                                  in1=DA.t[:, :].unsqueeze(2).broadcast_to([128, 8, 128]), op=ALU.mult),
                 reads=[CST.d, DA.d], writes=[U.d])
            for h in range(8):
                bk = bank[4 + h // 4]
                S.op("pe", lambda e, h=h, bk=bk: e.matmul(bk.t[:, (h % 4) * 128:(h % 4 + 1) * 128], lhsT=U.t[:, h, :], rhs=tri,
                                                          start=True, stop=True),
                     reads=[U.d, CST.d], writes=[bk.d], signal=(h % 4 == 3))
            S.op("act", lambda e: e.activation(out=E.t[:, 0:4, :], in_=bank[4].t[:, :].rearrange("p (h n) -> p h n", h=4), func=AF.Exp),
                 reads=[bank[4].d], writes=[E.d])
            S.op("act", lambda e: e.activation(out=E.t[:, 4:8, :], in_=bank[5].t[:, :].rearrange("p (h n) -> p h n", h=4), func=AF.Exp),
                 reads=[bank[5].d], writes=[E.d])
            S.op("pe", lambda e: e.matmul(bank[3].t[:, 128:256], lhsT=BCT.t[:, 0, :], rhs=BCT.t[:, 1, :], start=True, stop=True),
                 reads=[BCT.d], writes=[bank[3].d])
            S.op("dve", lambda e: e.tensor_tensor(out=CBM.t[:], in0=bank[3].t[:, 128:256], in1=tri, op=ALU.mult),
                 reads=[bank[3].d, CST.d], writes=[CBM.d])
            S.op("dve", lambda e: e.tensor_tensor(out=M.t[:], in0=E.t[:], in1=CBM.t[:, :].unsqueeze(1).broadcast_to([128, 8, 128]),
                                                  op=ALU.mult), reads=[E.d, CBM.d], writes=[M.d])
            for h in range(8):
                S.op("pe", lambda e, h=h: e.matmul(bank[0].t[:, h * 64:(h + 1) * 64], lhsT=M.t[:, h, :], rhs=XG.t[:, h * 64:(h + 1) * 64],
                                                   start=True, stop=True),
                     reads=[M.d, XG.d], writes=[bank[0].d], signal=(h == 7))
            S.op("pe", lambda e: e.matmul(bank[6].t[:, :], lhsT=BCT.t[:, 1, :], rhs=PREVB.t[:, :], start=True, stop=True),
                 reads=[BCT.d, PREVB.d], writes=[bank[6].d])
            S.op("pe", lambda e: e.matmul(bank[7].t[:, :], lhsT=BTOK.t[:, :], rhs=XW.t[:, :], start=True, stop=True),
                 reads=[BTOK.d, XW.d], writes=[bank[7].d])
            S.op("dve", lambda e: e.tensor_tensor(out=v8(Y1.t[:, :]), in0=v8(bank[6].t[:, :]), in1=bc8(EAC), op=ALU.mult),
                 reads=[bank[6].d, EAC.d], writes=[Y1.d])
            S.op("dve", lambda e: e.tensor_tensor(out=Y1.t[:], in0=Y1.t[:], in1=bank[0].t[:, :], op=ALU.add),
                 reads=[Y1.d, bank[0].d], writes=[Y1.d])
            S.op("dve", lambda e: e.tensor_tensor(out=Y1.t[:], in0=Y1.t[:], in1=SKIP.t[:], op=ALU.add),
                 reads=[Y1.d, SKIP.d], writes=[Y1.d])
            S.op("dve", lambda e: e.tensor_tensor(out=v8(PREV.t[:, :]), in0=v8(PREV.t[:, :]), in1=bc8(CD), op=ALU.mult),
                 reads=[PREV.d, CD.d], writes=[PREV.d])
            S.op("dve", lambda e: e.tensor_tensor(out=PREV.t[:], in0=PREV.t[:], in1=bank[7].t[:, :], op=ALU.add),
                 reads=[PREV.d, bank[7].d], writes=[PREV.d])
            S.op("act", lambda e: e.activation(out=PREVB.t[:], in_=PREV.t[:], func=AF.Copy), reads=[PREV.d], writes=[PREVB.d])
            S.op("act", lambda e, pin=pin: e.activation(out=SZ.t[:], in_=pin.t[:, OZ:OZ + 512], func=AF.Silu),
                 reads=[pin.d], writes=[SZ.d])
            S.op("dve", lambda e: e.tensor_tensor(out=YZ.t[:], in0=Y1.t[:], in1=SZ.t[:], op=ALU.mult),
                 reads=[Y1.d, SZ.d], writes=[YZ.d])
            S.op("act", lambda e: e.activation(out=SQ.t[:], in_=YZ.t[:], func=AF.Square, accum_out=SS.t[:]),
                 reads=[YZ.d], writes=[SQ.d, SS.d])
            S.op("act", lambda e: e.activation(out=RSTD.t[:], in_=SS.t[:], func=AF.Sqrt, bias=EPS, scale=1.0 / 512),
                 reads=[SS.d], writes=[RSTD.d])
            S.op("dve", lambda e: e.reciprocal(out=RSTD.t[:], in_=RSTD.t[:]), reads=[RSTD.d], writes=[RSTD.d])
            S.op("act", lambda e: e.activation(out=YN.t[:], in_=YZ.t[:], func=AF.Copy, scale=RSTD.t[:, 0:1]),
                 reads=[YZ.d, RSTD.d], writes=[YN.d])
            for t in range(4):
                S.op("pe", lambda e, t=t: e.transpose(out=bank[2].t[:, t * 128:(t + 1) * 128], in_=YN.t[:, t * 128:(t + 1) * 128],
                                                      identity=ident),
                     reads=[YN.d, CST.d], writes=[bank[2].d], signal=(t == 3))
            yt = YT[(c // 4) % 2]; c4 = c % 4
            S.op("dve", lambda e, yt=yt, c4=c4: e.tensor_tensor(
                out=yt.t[:, :, c4 * 128:(c4 + 1) * 128], in0=bank[2].t[:, :].rearrange("p (t n) -> p t n", t=4),
                in1=PRM.t[:, 54:58].unsqueeze(2).broadcast_to([128, 4, 128]), op=ALU.mult),
                reads=[bank[2].d, PRM.d], writes=[yt.d])
            if c4 == 3:
                c0 = (c - 3) * 128
                o = T("out"); C.outs.append(o)
                S.dma("pool", lambda e, yt=yt, c0=c0: e.dma_start(out=yT_v[:, :, c0:c0 + 512], in_=yt.t[:]),
                      reads=[yt.d], writes=[o])
        return chunk


class Recorder:
    def __init__(self):
        self.l = []

    def op(self, eng, fn, reads=(), writes=(), signal=True):
        self.l.append(("op", eng, fn, tuple(reads), tuple(writes), signal))

    def dma(self, eng, fn, reads=(), writes=()):
        self.l.append(("dma", eng, fn, tuple(reads), tuple(writes)))

    def flush_to(self, S):
        for it in self.l:
            play(S, it)
        self.l = []


def play(S, it):
    if it[0] == "op":
        S.op(it[1], it[2], reads=it[3], writes=it[4], signal=it[5])
    else:
        S.dma(it[1], it[2], reads=it[3], writes=it[4])


def emit_multi(C, NCH, makers, bankmaps, extra=()):
    S = C.S
    recs, chunks = [], []
    makers = list(makers) + list(extra)
    bankmaps = list(bankmaps) + [None] * len(extra)
    for i, mk in enumerate(makers):
        C.tag = "i%d" % i
        C.bankmap = bankmaps[i]
        r = Recorder()
        C.S = r
        chunks.append(mk())
        C.S = S
        C.bankmap = None
        r.flush_to(S)
        recs.append(r)
    C.tag = ""
    for c in range(NCH):
        lists = []
        for r, ch in zip(recs, chunks):
            ch(c)
            lists.append(r.l); r.l = []
        n = max(len(l) for l in lists)
        for k in range(n):
            for l in lists:
                if k < len(l):
                    play(S, l[k])


def half_cols(h):
    r = np.arange
    return np.concatenate([
        h * 512 + r(512),
        1024 + h * 512 + r(512),
        2048 + h * 128 + r(128),
        2304 + h * 128 + r(128),
        2560 + h * 8 + r(8),
        2576 + h * 256 + r(256),
        3088 + h * 256 + r(256),
        3600 + h * 256 + r(256),
        4112 + h * 128 + r(128),
        4368 + h * 128 + r(128),
        4624 + h * 256 + r(256),
        5136 + h * 256 + r(256),
    ])


def rope_table(S):
    pos = np.arange(S, dtype=np.float32)
    inv = (np.float32(10000.0) ** (-np.arange(0, 64, 2, dtype=np.float32) / np.float32(64))).astype(np.float32)
    ang = (pos[:, None] * inv[None, :]).astype(np.float32)
    c = np.cos(ang).astype(np.float32); s = np.sin(ang).astype(np.float32)
    return np.concatenate([c, s, c * np.float32(0.125), s * np.float32(0.125)], axis=1).astype(np.float32)


def ret_consts(ret_norm, h):
    out = np.zeros((128, 1024), np.float32)
    idx = np.arange(128, dtype=np.float64)
    for j in range(2):
        hh = 2 * h + j
        lg = np.log1p(-np.exp2(-5.0 - hh))
        rel = idx[None, :] - idx[:, None]
        dec = np.where(rel >= 0, np.exp(np.maximum(rel, 0) * lg), 0.0)
        out[:, j * 128:(j + 1) * 128] = dec
        out[j * 64:(j + 1) * 64, 256:384] = np.exp((idx + 1.0) * lg)[None, :]
        out[:, 512 + j] = np.exp((127 - idx) * lg)
        out[j * 64:(j + 1) * 64, 514] = np.exp(128 * lg)
    out[:, 516:518] = ret_norm[h * 256:(h + 1) * 256].reshape(2, 128).T
    return out


def make_ret(C, NCH, proj, cst_d, rc_d, rope_d, yT):
    if True:
        nc, S = C.nc, C.S
        bank = [C.bank[i] for i in (C.bankmap or range(8))]
        Stok = NCH * 128
        yT_v = yT.rearrange("(t p) n -> p t n", p=128)

        CST = C.sb("cst", [128, 512]); RC = C.sb("rc", [128, 1024])
        ident = CST.t[:, 0:128]
        PIN = [C.sb("pin%d" % i, [128, 768]) for i in range(2)]
        RP = [C.sb("rp%d" % i, [128, 128]) for i in range(2)]
        TA = C.sb("ta", [128, 4, 32]); TB = C.sb("tb", [128, 4, 32])
        QKR = C.sb("qkr", [128, 4, 64])
        KS = C.sb("ks", [128, 2, 64], BF16); VB = C.sb("vb", [128, 256], BF16)
        QT = C.sb("qt", [128, 128], BF16); KT = C.sb("kt", [128, 128], BF16); QST = C.sb("qst", [128, 128], BF16)
        SC = C.sb("sc", [128, 2, 128], BF16)
        PREV = C.sb("prev", [128, 128]); PREVB = C.sb("prevb", [128, 128], BF16)
        Y = C.sb("y", [128, 256]); SQ = C.sb("sq", [128, 128]); SS = C.sb("ss", [128, 2]); RSTD = C.sb("rstd", [128, 2])
        SG = C.sb("sg", [128, 256]); YN = C.sb("yn", [128, 256])
        YT = [C.sb("yt%d" % i, [128, 2, 512], BF16) for i in range(2)]

        S.dma("sp", lambda e: e.dma_start(out=CST.t[:], in_=cst_d[:, :]), writes=[CST.d])
        S.dma("sp", lambda e: e.dma_start(out=RC.t[:], in_=rc_d[:, :]), writes=[RC.d])
        S.op("pool", lambda e: e.memset(PREV.t[:], 0.0), writes=[PREV.d])
        S.op("pool", lambda e: e.memset(PREVB.t[:], 0.0), writes=[PREVB.d])

        def chunk(c):
            pin = PIN[c % 2]; rp = RP[c % 2]
            S.dma("sp", lambda e, pin=pin, c=c: e.dma_start(out=pin.t[:], in_=proj[c * 128:(c + 1) * 128, ORQ:ORQ + 768]),
                  writes=[pin.d])
            S.dma("sp", lambda e, rp=rp, c=c: e.dma_start(out=rp.t[:], in_=rope_d[c * 128:(c + 1) * 128, :]), writes=[rp.d])
            qk = pin.t[:, 0:256].rearrange("p (a h d) -> p a h d", a=2, h=2)

            def tab(rp, off):
                return rp.t[:, :].rearrange("p (a f) -> p a f", a=2)[:, :, off:off + 32].unsqueeze(2).broadcast_to([128, 2, 2, 32])
            v4 = lambda b: b.t[:, :, :].rearrange("p (a h) d -> p a h d", a=2)
            t1 = qk[:, :, :, 0:32]; t2 = qk[:, :, :, 32:64]
            o1 = QKR.t[:, :, 0:32].rearrange("p (a h) d -> p a h d", a=2)
            o2 = QKR.t[:, :, 32:64].rearrange("p (a h) d -> p a h d", a=2)
            S.op("dve", lambda e, rp=rp, t1=t1: e.tensor_tensor(out=v4(TA), in0=t1, in1=tab(rp, 0), op=ALU.mult),
                 reads=[pin.d, rp.d], writes=[TA.d])
            S.op("dve", lambda e, rp=rp, t2=t2: e.tensor_tensor(out=v4(TB), in0=t2, in1=tab(rp, 32), op=ALU.mult),
                 reads=[pin.d, rp.d], writes=[TB.d])
            S.op("dve", lambda e, o1=o1: e.tensor_tensor(out=o1, in0=v4(TA), in1=v4(TB), op=ALU.subtract),
                 reads=[TA.d, TB.d], writes=[QKR.d])
            S.op("dve", lambda e, rp=rp, t1=t1: e.tensor_tensor(out=v4(TA), in0=t1, in1=tab(rp, 32), op=ALU.mult),
                 reads=[pin.d, rp.d], writes=[TA.d])
            S.op("dve", lambda e, rp=rp, t2=t2: e.tensor_tensor(out=v4(TB), in0=t2, in1=tab(rp, 0), op=ALU.mult),
                 reads=[pin.d, rp.d], writes=[TB.d])
            S.op("dve", lambda e, o2=o2: e.tensor_tensor(out=o2, in0=v4(TA), in1=v4(TB), op=ALU.add),
                 reads=[TA.d, TB.d], writes=[QKR.d])
            if RET_STAGE < 2:
                return
            S.op("dve", lambda e: e.tensor_tensor(out=KS.t[:], in0=QKR.t[:, 2:4, :],
                                                   in1=RC.t[:, 512:514].unsqueeze(2).broadcast_to([128, 2, 64]), op=ALU.mult),
                 reads=[QKR.d, RC.d], writes=[KS.d])
            S.op("act", lambda e, pin=pin: e.activation(out=VB.t[:], in_=pin.t[:, 256:512], func=AF.Copy), reads=[pin.d], writes=[VB.d])
            if RET_STAGE < 3:
                return
            for a in range(2):
                S.op("pe", lambda e, a=a: e.transpose(out=bank[0].t[:, a * 128:(a + 1) * 128],
                                                      in_=QKR.t[:, 2 * a:2 * a + 2, :].rearrange("p h d -> p (h d)"), identity=ident),
                     reads=[QKR.d, CST.d], writes=[bank[0].d], signal=(a == 1))
            S.op("act", lambda e: e.activation(out=QT.t[:], in_=bank[0].t[:, 0:128], func=AF.Copy), reads=[bank[0].d], writes=[QT.d])
            S.op("act", lambda e: e.activation(out=KT.t[:], in_=bank[0].t[:, 128:256], func=AF.Copy), reads=[bank[0].d], writes=[KT.d])
            S.op("dve", lambda e: e.tensor_tensor(out=QST.t[:], in0=bank[0].t[:, 0:128], in1=RC.t[:, 256:384], op=ALU.mult),
                 reads=[bank[0].d, RC.d], writes=[QST.d])
            if RET_STAGE < 4:
                return
            for j in range(2):
                bk = bank[1 + 4 * j]
                S.op("pe", lambda e, j=j, bk=bk: e.matmul(bk.t[:, 0:128], lhsT=KT.t[j * 64:(j + 1) * 64, :],
                                                          rhs=QT.t[j * 64:(j + 1) * 64, :], start=True, stop=True),
                     reads=[KT.d, QT.d], writes=[bk.d])
            for j in range(2):
                bk = bank[1 + 4 * j]
                S.op("dve", lambda e, j=j, bk=bk: e.tensor_tensor(out=SC.t[:, j, :], in0=bk.t[:, 0:128],
                                                                  in1=RC.t[:, j * 128:(j + 1) * 128], op=ALU.mult),
                     reads=[bk.d, RC.d], writes=[SC.d])
            if RET_STAGE < 5:
                return
            for j in range(2):
                bk = bank[2 + 4 * j]
                S.op("pe", lambda e, j=j, bk=bk: e.matmul(bk.t[:, 0:128], lhsT=SC.t[:, j, :], rhs=VB.t[:, j * 128:(j + 1) * 128],
                                                          start=True, stop=False),
                     reads=[SC.d, VB.d], writes=[bk.d], signal=False)
                S.op("pe", lambda e, j=j, bk=bk: e.matmul(bk.t[:, 0:128], lhsT=QST.t[j * 64:(j + 1) * 64, :],
                                                          rhs=PREVB.t[j * 64:(j + 1) * 64, :], start=False, stop=True),
                     reads=[QST.d, PREVB.d], writes=[bk.d])
            if RET_STAGE < 6:
                return
            for j in range(2):
                S.op("pe", lambda e, j=j: e.matmul(bank[3].t[j * 64:(j + 1) * 64, 0:128], lhsT=KS.t[:, j, :], rhs=VB.t[:, j * 128:(j + 1) * 128],
                                                   start=True, stop=True),
                     reads=[KS.d, VB.d], writes=[bank[3].d], signal=(j == 1))
            S.op("dve", lambda e: e.scalar_tensor_tensor(out=PREV.t[:], in0=PREV.t[:], scalar=RC.t[:, 514:515],
                                                         in1=bank[3].t[:, 0:128], op0=ALU.mult, op1=ALU.add),
                 reads=[PREV.d, RC.d, bank[3].d], writes=[PREV.d])
            S.op("act", lambda e: e.activation(out=PREVB.t[:], in_=PREV.t[:], func=AF.Copy), reads=[PREV.d], writes=[PREVB.d])
            if RET_STAGE < 7:
                return
            for j in range(2):
                bk = bank[2 + 4 * j]
                S.op("act", lambda e, j=j, bk=bk: e.activation(out=Y.t[:, j * 128:(j + 1) * 128], in_=bk.t[:, 0:128], func=AF.Copy),
                     reads=[bk.d], writes=[Y.d])
            for j in range(2):
                S.op("act", lambda e, j=j: e.activation(out=SQ.t[:], in_=Y.t[:, j * 128:(j + 1) * 128], func=AF.Square,
                                                        accum_out=SS.t[:, j:j + 1]),
                     reads=[Y.d], writes=[SQ.d, SS.d])
            S.op("act", lambda e: e.activation(out=RSTD.t[:], in_=SS.t[:], func=AF.Sqrt, bias=EPS, scale=1.0 / 128),
                 reads=[SS.d], writes=[RSTD.d])
            S.op("dve", lambda e: e.reciprocal(out=RSTD.t[:], in_=RSTD.t[:]), reads=[RSTD.d], writes=[RSTD.d])
            S.op("act", lambda e, pin=pin: e.activation(out=SG.t[:], in_=pin.t[:, 512:768], func=AF.Silu), reads=[pin.d], writes=[SG.d])
            S.op("dve", lambda e: e.tensor_tensor(out=YN.t[:, :].rearrange("p (h n) -> p h n", h=2),
                                                   in0=Y.t[:, :].rearrange("p (h n) -> p h n", h=2),
                                                   in1=RSTD.t[:, :].unsqueeze(2).broadcast_to([128, 2, 128]), op=ALU.mult),
                 reads=[Y.d, RSTD.d], writes=[YN.d])
            S.op("dve", lambda e: e.tensor_tensor(out=YN.t[:], in0=YN.t[:], in1=SG.t[:], op=ALU.mult),
                 reads=[YN.d, SG.d], writes=[YN.d])
            if RET_STAGE < 8:
                return
            for t in range(2):
                S.op("pe", lambda e, t=t: e.transpose(out=bank[4].t[:, t * 128:(t + 1) * 128], in_=YN.t[:, t * 128:(t + 1) * 128],
                                                      identity=ident),
                     reads=[YN.d, CST.d], writes=[bank[4].d], signal=(t == 1))
            yt = YT[(c // 4) % 2]; c4 = c % 4
            S.op("dve", lambda e, yt=yt, c4=c4: e.tensor_tensor(
                out=yt.t[:, :, c4 * 128:(c4 + 1) * 128], in0=bank[4].t[:, 0:256].rearrange("p (t n) -> p t n", t=2),
                in1=RC.t[:, 516:518].unsqueeze(2).broadcast_to([128, 2, 128]), op=ALU.mult),
                reads=[bank[4].d, RC.d], writes=[yt.d])
            if c4 == 3:
                c0 = (c - 3) * 128
                o = T("out"); C.outs.append(o)
                S.dma("pool", lambda e, yt=yt, c0=c0: e.dma_start(out=yT_v[:, :, c0:c0 + 512], in_=yt.t[:]),
                      reads=[yt.d], writes=[o])


        return chunk


def att_rope_table(S):
    t = rope_table(S)
    return np.ascontiguousarray(np.concatenate([t[:, 64:128], t[:, 0:64]], axis=1))


def att_consts(q_norm, k_norm):
    out = np.zeros((128, 1024), np.float32)
    out[:, 0:256] = np.tile(q_norm, 4)[None, :]
    out[:, 256:512] = np.tile(k_norm, 4)[None, :]
    i = np.arange(128)
    out[:, 512:640] = (i[:, None] <= i[None, :])
    out[:, 640:768] = (i[:, None] >= i[None, :])
    out[64, 768:832] = 1.0
    return out


def emit_att(C, NCH, proj, cst_d, ac_d, rope_d, yT):
    if True:
        nc, S, bank = C.nc, C.S, C.bank
        Stok = NCH * 128

        CST = C.sb("cst", [128, 512]); AC = C.sb("ac", [128, 1024]); MSK = C.sb("msk", [128, 256], BF16)
        ident = CST.t[:, 0:128]
        PIN = [C.sb("pin%d" % i, [128, 512]) for i in range(2)]
        RP = [C.sb("rp%d" % i, [128, 128]) for i in range(2)]
        SQ = C.sb("sq", [128, 512]); SS = C.sb("ss", [128, 8]); RSTD = C.sb("rstd", [128, 8]); QN = C.sb("qn", [128, 512])
        TA = C.sb("ta", [128, 8, 32]); TB = C.sb("tb", [128, 8, 32]); QKR = C.sb("qkr", [128, 8, 64], BF16)
        IDB = make_identb(C, CST)
        QT = C.sb("qt", [128, 2, Stok], BF16); KT = C.sb("kt", [128, 2, Stok], BF16)
        ACC = [C.sb("acc%d" % j, [65, Stok]) for j in range(2)]
        VF = [C.sb("vf%d" % i, [128, 2, 64]) for i in range(2)]
        VE = [C.sb("ve%d" % i, [128, 2, 65], BF16) for i in range(3)]
        PT = [[C.sb("pt%d_%d" % (j, i), [128, 256], BF16) for i in range(2)] for j in range(2)]
        RD = C.sb("rd", [64, 512]); YO = [C.sb("yo%d" % i, [64, 2048], BF16) for i in range(2)]

        S.dma("sp", lambda e: e.dma_start(out=CST.t[:], in_=cst_d[:, :]), writes=[CST.d])
        S.dma("sp", lambda e: e.dma_start(out=AC.t[:], in_=ac_d[:, :]), writes=[AC.d])
        S.op("pool", lambda e: e.tensor_copy(out=MSK.t[:], in_=AC.t[:, 512:768]), reads=[AC.d], writes=[MSK.d])
        for i in range(3):
            S.op("pool", lambda e, i=i: e.memset(VE[i].t[:], 1.0), writes=[VE[i].d])

        for c in range(NCH):
            pin = PIN[c % 2]; rp = RP[c % 2]
            S.dma("sp", lambda e, pin=pin, c=c: e.dma_start(out=pin.t[:], in_=proj[c * 128:(c + 1) * 128, OAQ:OAQ + 512]),
                  writes=[pin.d])
            S.dma("sp", lambda e, rp=rp, c=c: e.dma_start(out=rp.t[:], in_=rope_d[c * 128:(c + 1) * 128, :]), writes=[rp.d])
            S.op("act", lambda e, pin=pin: e.activation(out=SQ.t[:], in_=pin.t[:], func=AF.Square), reads=[pin.d], writes=[SQ.d])
            S.op("dve", lambda e: e.tensor_reduce(out=SS.t[:], in_=SQ.t[:, :].rearrange("p (h d) -> p h d", h=8),
                                                  axis=mybir.AxisListType.X, op=ALU.add), reads=[SQ.d], writes=[SS.d])
            S.op("act", lambda e: e.activation(out=RSTD.t[:], in_=SS.t[:], func=AF.Sqrt, bias=EPS, scale=1.0 / 64),
                 reads=[SS.d], writes=[RSTD.d])
            S.op("dve", lambda e: e.reciprocal(out=RSTD.t[:], in_=RSTD.t[:]), reads=[RSTD.d], writes=[RSTD.d])
            S.op("dve", lambda e, pin=pin: e.tensor_tensor(out=QN.t[:, :].rearrange("p (h d) -> p h d", h=8),
                                                           in0=pin.t[:, :].rearrange("p (h d) -> p h d", h=8),
                                                           in1=RSTD.t[:, :].unsqueeze(2).broadcast_to([128, 8, 64]), op=ALU.mult),
                 reads=[pin.d, RSTD.d], writes=[QN.d])
            S.op("dve", lambda e: e.tensor_tensor(out=QN.t[:], in0=QN.t[:], in1=AC.t[:, 0:512], op=ALU.mult),
                 reads=[QN.d, AC.d], writes=[QN.d])
            qk = QN.t[:, :].rearrange("p (a h d) -> p a h d", a=2, h=4)

            def tab(rp, off):
                return rp.t[:, :].rearrange("p (a f) -> p a f", a=2)[:, :, off:off + 32].unsqueeze(2).broadcast_to([128, 2, 4, 32])
            v4 = lambda b: b.t[:, :, :].rearrange("p (a h) d -> p a h d", a=2)
            t1 = qk[:, :, :, 0:32]; t2 = qk[:, :, :, 32:64]
            o1 = QKR.t[:, :, 0:32].rearrange("p (a h) d -> p a h d", a=2)
            o2 = QKR.t[:, :, 32:64].rearrange("p (a h) d -> p a h d", a=2)
            S.op("dve", lambda e, rp=rp, t1=t1: e.tensor_tensor(out=v4(TA), in0=t1, in1=tab(rp, 0), op=ALU.mult),
                 reads=[QN.d, rp.d], writes=[TA.d])
            S.op("dve", lambda e, rp=rp, t2=t2: e.tensor_tensor(out=v4(TB), in0=t2, in1=tab(rp, 32), op=ALU.mult),
                 reads=[QN.d, rp.d], writes=[TB.d])
            S.op("dve", lambda e, o1=o1: e.tensor_tensor(out=o1, in0=v4(TA), in1=v4(TB), op=ALU.subtract),
                 reads=[TA.d, TB.d], writes=[QKR.d])
            S.op("dve", lambda e, rp=rp, t1=t1: e.tensor_tensor(out=v4(TA), in0=t1, in1=tab(rp, 32), op=ALU.mult),
                 reads=[QN.d, rp.d], writes=[TA.d])
            S.op("dve", lambda e, rp=rp, t2=t2: e.tensor_tensor(out=v4(TB), in0=t2, in1=tab(rp, 0), op=ALU.mult),
                 reads=[QN.d, rp.d], writes=[TB.d])
            S.op("dve", lambda e, o2=o2: e.tensor_tensor(out=o2, in0=v4(TA), in1=v4(TB), op=ALU.add),
                 reads=[TA.d, TB.d], writes=[QKR.d])
            for a in range(4):
                S.op("pe", lambda e, a=a: e.transpose(out=bfv(bank[0])[:, a * 128:(a + 1) * 128],
                                                      in_=QKR.t[:, 2 * a:2 * a + 2, :].rearrange("p h d -> p (h d)"), identity=IDB.t[:]),
                     reads=[QKR.d, IDB.d], writes=[bank[0].d], signal=(a == 3))
            S.op("act", lambda e, c=c: e.activation(out=QT.t[:, :, c * 128:(c + 1) * 128],
                                                    in_=bfv(bank[0])[:, 0:256].rearrange("p (a n) -> p a n", a=2), func=AF.Copy),
                 reads=[bank[0].d], writes=[QT.d])
            S.op("act", lambda e, c=c: e.activation(out=KT.t[:, :, c * 128:(c + 1) * 128],
                                                    in_=bfv(bank[0])[:, 256:512].rearrange("p (a n) -> p a n", a=2), func=AF.Copy),
                 reads=[bank[0].d], writes=[KT.d])

        ti = 0
        for p in range(2):
            for j in range(2):
                S.op("pool", lambda e, j=j: e.memset(ACC[j].t[:], 0.0), writes=[ACC[j].d])
            for d in (1, 4, 16):
                L = Stok // d
                for r in range(d):
                    for blk in range(L // 128):
                        vf = VF[ti % 2]; ve = VE[ti % 3]; vprev = VE[(ti - 1) % 3]
                        t0 = blk * 128 * d + r
                        S.dma("sp", lambda e, vf=vf, t0=t0, d=d, p=p: e.dma_start(
                            out=vf.t[:], in_=proj[t0:t0 + 127 * d + 1:d, OAV + p * 128:OAV + (p + 1) * 128].rearrange("t (h e) -> t h e", h=2)),
                            writes=[vf.d])
                        S.op("act", lambda e, vf=vf, ve=ve: e.activation(out=ve.t[:, :, 0:64], in_=vf.t[:], func=AF.Copy), reads=[vf.d], writes=[ve.d])
                        tok = slice(t0, t0 + 127 * d + 1, d)
                        tokp = slice(t0 - 128 * d, t0 - d + 1, d)
                        for j in range(2):
                            pr = slice(j * 64, (j + 1) * 64)
                            sb_ = bank[1 + j + 2 * (ti % 2)]
                            ob = bank[(5 + j) if ti % 2 == 0 else (0 if j == 0 else 7)]
                            pt = PT[j][ti % 2]
                            ncol = 256 if blk > 0 else 128
                            S.op("pe", lambda e, sb_=sb_, pr=pr, p=p, tok=tok: e.matmul(
                                sb_.t[:, 0:128], lhsT=KT.t[pr, p, tok], rhs=QT.t[pr, p, tok], start=True, stop=True),
                                reads=[KT.d, QT.d], writes=[sb_.d], signal=(blk == 0))
                            if blk > 0:
                                S.op("pe", lambda e, sb_=sb_, pr=pr, p=p, tok=tok, tokp=tokp: e.matmul(
                                    sb_.t[:, 128:256], lhsT=KT.t[pr, p, tokp], rhs=QT.t[pr, p, tok], start=True, stop=True),
                                    reads=[KT.d, QT.d], writes=[sb_.d])
                            S.op("act", lambda e, pt=pt, sb_=sb_, ncol=ncol: e.activation(out=pt.t[:, 0:ncol], in_=sb_.t[:, 0:ncol], func=AF.Exp),
                                 reads=[sb_.d], writes=[pt.d])
                            meng = "dve"
                            S.op(meng, lambda e, pt=pt, ncol=ncol: e.tensor_tensor(out=pt.t[:, 0:ncol], in0=pt.t[:, 0:ncol], in1=MSK.t[:, 0:ncol],
                                                                                   op=ALU.mult),
                                 reads=[pt.d, MSK.d], writes=[pt.d])
                            S.op("pe", lambda e, ob=ob, ve=ve, j=j, pt=pt, blk=blk: e.matmul(
                                ob.t[0:65, 0:128], lhsT=ve.t[:, j, :], rhs=pt.t[:, 0:128], start=True, stop=(blk == 0)),
                                reads=[ve.d, pt.d], writes=[ob.d], signal=(blk == 0))
                            if blk > 0:
                                S.op("pe", lambda e, ob=ob, vprev=vprev, j=j, pt=pt: e.matmul(
                                    ob.t[0:65, 0:128], lhsT=vprev.t[:, j, :], rhs=pt.t[:, 128:256], start=False, stop=True),
                                    reads=[vprev.d, pt.d], writes=[ob.d])
                            S.op("dve", lambda e, j=j, ob=ob, tok=tok: e.tensor_tensor(
                                out=ACC[j].t[0:65, tok], in0=ACC[j].t[0:65, tok], in1=ob.t[0:65, 0:128], op=ALU.add),
                                reads=[ACC[j].d, ob.d], writes=[ACC[j].d])
                        ti += 1
            for j in range(2):
                h = 2 * p + j
                for n0 in range(0, Stok, 512):
                    yo = YO[(n0 // 2048) % 2]
                    S.op("pe", lambda e, j=j, n0=n0: e.matmul(bank[7].t[0:64, :], lhsT=AC.t[0:65, 768:832], rhs=ACC[j].t[0:65, n0:n0 + 512],
                                                              start=True, stop=True),
                         reads=[AC.d, ACC[j].d], writes=[bank[7].d])
                    S.op("dve", lambda e: e.reciprocal(out=RD.t[:], in_=bank[7].t[0:64, :]), reads=[bank[7].d], writes=[RD.d])
                    S.op("dve", lambda e, j=j, n0=n0, yo=yo: e.tensor_tensor(out=yo.t[:, n0 % 2048:n0 % 2048 + 512], in0=ACC[j].t[0:64, n0:n0 + 512],
                                                                             in1=RD.t[:], op=ALU.mult),
                         reads=[ACC[j].d, RD.d], writes=[yo.d])
                    if (n0 + 512) % 2048 == 0 or n0 + 512 == Stok:
                        nb0 = (n0 // 2048) * 2048; nn = n0 + 512 - nb0
                        o = T("out"); C.outs.append(o)
                        S.dma("pool", lambda e, yo=yo, h=h, nb0=nb0, nn=nn: e.dma_start(out=yT[h * 64:(h + 1) * 64, nb0:nb0 + nn], in_=yo.t[:, 0:nn]),
                              reads=[yo.d], writes=[o])


def make_wconv(C, tiles, per_chunk, wf, wb, nbuf=2):
    S = C.S
    TW_ = wf.shape[1]
    FB = [C.sb("f%d" % i, [128, TW_]) for i in range(nbuf)]
    BB = [C.sb("b%d" % i, [128, TW_], BF16) for i in range(nbuf)]
    cnt = [0]

    def chunk(c):
        for i in tiles[c * per_chunk:(c + 1) * per_chunk]:
            k = cnt[0]; cnt[0] += 1
            f = FB[k % nbuf]; b = BB[k % nbuf]
            S.dma("sp", lambda e, f=f, i=i: e.dma_start(out=f.t[:], in_=wf[i * 128:(i + 1) * 128, :]), writes=[f.d])
            if k % 2 == 0:
                S.op("act", lambda e, f=f, b=b: e.activation(out=b.t[:], in_=f.t[:], func=AF.Copy), reads=[f.d], writes=[b.d])
            else:
                S.op("dve", lambda e, f=f, b=b: e.tensor_copy(out=b.t[:], in_=f.t[:]), reads=[f.d], writes=[b.d])
            o = T("out"); C.outs.append(o)
            S.dma("pool", lambda e, b=b, i=i: e.dma_start(out=wb[i * 128:(i + 1) * 128, :], in_=b.t[:]), reads=[b.d], writes=[o])
    return chunk


def emit_wconv(C, tiles, wf, wb):
    ch = make_wconv(C, tiles, len(tiles), wf, wb, nbuf=3)
    ch(0)


def emit_norm_prep(C, xt_ap, xt_dep, XN, SQ, SS, RSTD):
    S = C.S
    S.op("act", lambda e: e.activation(out=SQ.t[:], in_=xt_ap, func=AF.Square, accum_out=SS.t[:]), reads=[xt_dep], writes=[SQ.d, SS.d])
    S.op("act", lambda e: e.activation(out=RSTD.t[:], in_=SS.t[:], func=AF.Sqrt, bias=EPS, scale=1.0 / D), reads=[SS.d], writes=[RSTD.d])
    S.op("dve", lambda e: e.reciprocal(out=RSTD.t[:], in_=RSTD.t[:]), reads=[RSTD.d], writes=[RSTD.d])
    S.op("act", lambda e: e.activation(out=XN.t[:], in_=xt_ap, func=AF.Copy, scale=RSTD.t[:, 0:1]),
         reads=[xt_dep, RSTD.d], writes=[XN.d])


def emit_norm_tr(C, XN, GAIN, ident, CST, HNT, col0, banks):
    S, bank = C.S, C.bank
    for g in range(4):
        bk = bank[banks[g % len(banks)]]
        for q in range(4):
            k = 4 * g + q
            S.op("pe", lambda e, bk=bk, q=q, k=k: e.transpose(out=bfv(bk)[:, q * 128:(q + 1) * 128], in_=XN.t[:, k * 128:(k + 1) * 128],
                                                              identity=ident.t[:]),
                 reads=[XN.d, ident.d], writes=[bk.d], signal=(q == 3))
        S.op("dve", lambda e, bk=bk, g=g: e.tensor_tensor(out=HNT.t[:, 4 * g:4 * g + 4, col0:col0 + 128],
                                                          in0=bfv(bk)[:, 0:512].rearrange("p (q n) -> p q n", q=4),
                                                          in1=GAIN.t[:, 4 * g:4 * g + 4].unsqueeze(2).broadcast_to([128, 4, 128]), op=ALU.mult),
             reads=[bk.d, GAIN.d], writes=[HNT.d])


def emit_norm_transpose(C, xt_ap, xt_dep, XN, SQ, SS, RSTD, GAIN, ident, CST, HNT, col0, banks):
    emit_norm_prep(C, xt_ap, xt_dep, XN, SQ, SS, RSTD)
    emit_norm_tr(C, XN, GAIN, ident, CST, HNT, col0, banks)


def emit_inproj(C, NCH, x, cst_d, g_d, w_d, proj):
    if True:
        nc, S, bank = C.nc, C.S, C.bank
        Stok = NCH * 128
        CST = C.sb("cst", [128, 512]); GAIN = C.sb("gain", [128, 16])
        ident = CST.t[:, 0:128]
        WB = C.sb("wb", [128, 16, NPROJ], BF16)
        XT = [C.sb("xt%d" % i, [128, D]) for i in range(3)]
        XN = C.sb("xn", [128, D], BF16); SQ = C.sb("sq", [128, D]); SS = C.sb("ss", [128, 1]); RSTD = C.sb("rstd", [128, 1])
        HNT = [C.sb("hnt%d" % i, [128, 16, 128], BF16) for i in range(2)]
        OUT = [C.sb("out%d" % i, [128, NPROJ]) for i in range(2)]
        S.dma("sp", lambda e: e.dma_start(out=CST.t[:], in_=cst_d[:, :]), writes=[CST.d])
        S.dma("sp", lambda e: e.dma_start(out=GAIN.t[:], in_=g_d[:, :]), writes=[GAIN.d])
        ident = make_identb(C, CST)
        wv = w_d.rearrange("(k p) n -> p k n", p=128)
        for k in range(16):
            S.dma("act" if k % 2 else "sp", lambda e, k=k: e.dma_start(out=WB.t[:, k, :], in_=wv[:, k, :]), writes=[WB.d])
        ncols = [(i * 512, min(512, NPROJ - i * 512)) for i in range(6)]

        def load_x(c):
            xt = XT[c % 3]
            S.dma("sp", lambda e, xt=xt, c=c: e.dma_start(out=xt.t[:], in_=x[c * 128:(c + 1) * 128, :]), writes=[xt.d])

        def mm_groups(c, groups):
            hnt = HNT[c % 2]; out = OUT[c % 2]
            for i in groups:
                n0, nw = ncols[i]
                bk = bank[2 + (c * 6 + i) % 6]
                for k in range(16):
                    S.op("pe", lambda e, bk=bk, k=k, n0=n0, nw=nw, hnt=hnt: e.matmul(bk.t[:, 0:nw], lhsT=hnt.t[:, k, :], rhs=WB.t[:, k, n0:n0 + nw],
                                                                                  start=(k == 0), stop=(k == 15)),
                         reads=[hnt.d, WB.d], writes=[bk.d], signal=(k == 15))
                if i % 2 == 0:
                    S.op("act", lambda e, bk=bk, n0=n0, nw=nw, out=out: e.activation(out=out.t[:, n0:n0 + nw], in_=bk.t[:, 0:nw], func=AF.Copy),
                         reads=[bk.d], writes=[out.d])
                else:
                    S.op("dve", lambda e, bk=bk, n0=n0, nw=nw, out=out: e.tensor_copy(out=out.t[:, n0:n0 + nw], in_=bk.t[:, 0:nw]),
                         reads=[bk.d], writes=[out.d])

        load_x(0)
        if NCH > 1:
            load_x(1)
        emit_norm_prep(C, XT[0].t[:], XT[0].d, XN, SQ, SS, RSTD)
        emit_norm_tr(C, XN, GAIN, ident, CST, HNT[0], 0, (0, 1))
        for c in range(NCH):
            if c + 2 < NCH:
                load_x(c + 2)
            if c + 1 < NCH and EXP != "noprep":
                emit_norm_prep(C, XT[(c + 1) % 3].t[:], XT[(c + 1) % 3].d, XN, SQ, SS, RSTD)
            mm_groups(c, (0, 1, 2))
            if c + 1 < NCH and EXP != "notr":
                emit_norm_tr(C, XN, GAIN, ident, CST, HNT[(c + 1) % 2], 0, (0, 1))
            mm_groups(c, (3, 4, 5))
            out = OUT[c % 2]
            o = T("out"); C.outs.append(o)
            S.dma("pool", lambda e, out=out, c=c: e.dma_start(out=proj[c * 128:(c + 1) * 128, :], in_=out.t[:]), reads=[out.d], writes=[o])


def emit_outffn(C, NT, x, yT, cst_d, g_d, wo_d, wg_d, wu_d, wd_d, xo, gidx_d=None):
    if True:
        nc, S, bank = C.nc, C.S, C.bank
        NBLK = NT // 512
        NF = DFF // 128
        CST = C.sb("cst", [128, 512]); GAIN = C.sb("gain", [128, 16])
        ident = CST.t[:, 0:128]
        X1 = C.sb("x1", [128, 4, D])
        YT = C.sb("yt", [128, 16, 512], BF16)
        HNT = C.sb("hnt", [128, 16, 512], BF16)
        HT = C.sb("ht", [128, NF, 512], BF16)
        XN = C.sb("xn", [128, D], BF16); SQ = C.sb("sq", [128, D]); SS = C.sb("ss", [128, 1]); RSTD = C.sb("rstd", [128, 1])
        WS = [C.sb("ws%d" % i, [128, 1024], BF16) for i in range(4)]
        WG = [C.sb("wg%d" % i, [128, 16, 128], BF16) for i in range(3)]
        WU = [C.sb("wu%d" % i, [128, 16, 128], BF16) for i in range(3)]
        SG = [C.sb("sg%d" % i, [128, 512]) for i in range(2)]
        S.dma("sp", lambda e: e.dma_start(out=CST.t[:], in_=cst_d[:, :]), writes=[CST.d])
        S.dma("sp", lambda e: e.dma_start(out=GAIN.t[:], in_=g_d[:, :]), writes=[GAIN.d])
        ident = make_identb(C, CST)
        yT_v = yT.rearrange("(k p) n -> p k n", p=128)
        if gidx_d is not None:
            IDX = C.sb("gidx", [128, NBLK * 20], mybir.dt.uint32)
            S.dma("sp", lambda e: e.dma_start(out=IDX.t[:], in_=gidx_d[:, :]), writes=[IDX.d])
            yT_rows = yT.rearrange("f (b n) -> (f b) n", n=512)
        wsi = [0]

        def gemm_res(lhs, lhs_dep, KCH, wdram):
            for half in range(2):
                for k in range(KCH):
                    ws = WS[wsi[0] % 4]; q = "sp" if wsi[0] % 2 == 0 else "act"; wsi[0] += 1
                    S.dma(q, lambda e, ws=ws, k=k, half=half: e.dma_start(out=ws.t[:], in_=wdram[k * 128:(k + 1) * 128, half * 1024:(half + 1) * 1024]),
                          writes=[ws.d])
                    for t in range(4):
                        for nn in range(2):
                            bk = bank[t * 2 + nn]
                            S.op("pe", lambda e, bk=bk, k=k, t=t, nn=nn, ws=ws: e.matmul(bk.t[:, :], lhsT=lhs(k, t), rhs=ws.t[:, nn * 512:(nn + 1) * 512],
                                                                                      start=(k == 0), stop=(k == KCH - 1)),
                                 reads=[lhs_dep, ws.d], writes=[bk.d], signal=(k == KCH - 1 or (t == 3 and nn == 1)))
                for t in range(4):
                    for nn in range(2):
                        bk = bank[t * 2 + nn]; c0 = half * 1024 + nn * 512
                        S.op("dve", lambda e, bk=bk, t=t, c0=c0: e.tensor_tensor(out=X1.t[:, t, c0:c0 + 512], in0=X1.t[:, t, c0:c0 + 512],
                                                                                 in1=bk.t[:, :], op=ALU.add),
                             reads=[X1.d, bk.d], writes=[X1.d])

        for b in range(NBLK):
            t0 = b * 512
            if gidx_d is None:
                S.dma("sp", lambda e, t0=t0: e.dma_start(out=X1.t[:], in_=x[t0:t0 + 512, :].rearrange("(t p) n -> p t n", p=128)), writes=[X1.d])
                S.dma("act", lambda e, t0=t0: e.dma_start(out=YT.t[:], in_=yT_v[:, :, t0:t0 + 512]), writes=[YT.d])
            else:
                for t in range(4):
                    col = b * 20 + t
                    S.dma("pool", lambda e, t=t, col=col: e.indirect_dma_start(
                        out=X1.t[:, t, :], out_offset=None, in_=x[:, :],
                        in_offset=bass.IndirectOffsetOnAxis(ap=IDX.t[:, col:col + 1], axis=0)), reads=[IDX.d], writes=[X1.d])
                for k in range(16):
                    col = b * 20 + 4 + k
                    S.dma("pool", lambda e, k=k, col=col: e.indirect_dma_start(
                        out=YT.t[:, k, :], out_offset=None, in_=yT_rows[:, :],
                        in_offset=bass.IndirectOffsetOnAxis(ap=IDX.t[:, col:col + 1], axis=0)), reads=[IDX.d], writes=[YT.d])
            gemm_res(lambda k, t: YT.t[:, k, t * 128:(t + 1) * 128], YT.d, 16, wo_d)
            for t in range(4):
                emit_norm_transpose(C, X1.t[:, t, :], X1.d, XN, SQ, SS, RSTD, GAIN, ident, CST, HNT, t * 128, (0, 1, 2, 3))
            for f in range(NF):
                wg = WG[f % 3]; wu = WU[f % 3]; sg = SG[f % 2]
                S.dma("sp", lambda e, wg=wg, f=f: e.dma_start(out=wg.t[:], in_=wg_d[f].rearrange("p (k c) -> p k c", k=16)), writes=[wg.d])
                S.dma("act", lambda e, wu=wu, f=f: e.dma_start(out=wu.t[:], in_=wu_d[f].rearrange("p (k c) -> p k c", k=16)), writes=[wu.d])
                ba = bank[4 + (f % 2) * 2]; bb = bank[5 + (f % 2) * 2]
                for k in range(16):
                    S.op("pe", lambda e, ba=ba, wg=wg, k=k: e.matmul(ba.t[:, :], lhsT=wg.t[:, k, :], rhs=HNT.t[:, k, :], start=(k == 0), stop=(k == 15)),
                         reads=[wg.d, HNT.d], writes=[ba.d], signal=(k == 15))
                for k in range(16):
                    S.op("pe", lambda e, bb=bb, wu=wu, k=k: e.matmul(bb.t[:, :], lhsT=wu.t[:, k, :], rhs=HNT.t[:, k, :], start=(k == 0), stop=(k == 15)),
                         reads=[wu.d, HNT.d], writes=[bb.d], signal=(k == 15))
                S.op("act", lambda e, ba=ba, sg=sg: e.activation(out=sg.t[:], in_=ba.t[:, :], func=AF.Silu), reads=[ba.d], writes=[sg.d])
                S.op("dve", lambda e, bb=bb, sg=sg, f=f: e.tensor_tensor(out=HT.t[:, f, :], in0=bb.t[:, :], in1=sg.t[:], op=ALU.mult),
                     reads=[bb.d, sg.d], writes=[HT.d])
            gemm_res(lambda k, t: HT.t[:, k, t * 128:(t + 1) * 128], HT.d, NF, wd_d)
            o = T("out"); C.outs.append(o)
            S.dma("pool", lambda e, t0=t0: e.dma_start(out=xo[t0:t0 + 512, :].rearrange("(t p) n -> p t n", p=128), in_=X1.t[:]),
                  reads=[X1.d], writes=[o])


def gate_layout(w):
    return np.ascontiguousarray(w.reshape(16, 128, DFF // 128, 128).transpose(2, 1, 0, 3)).reshape(DFF // 128, 128, 2048)


NL = 2
W_SHAPES = [(D, NPROJ), (D, NPROJ), (D, D), (DFF // 128, 128, 2048), (DFF // 128, 128, 2048), (DFF, D)]
W_SIZES = [int(np.prod(sh)) for sh in W_SHAPES]
LW = sum(W_SIZES) // 128
TW = 3392


def build_fused(NCH=SEQ // 128):
    st = ExitStack()
    with st:
        C = Ctx(st)
        nc, S = C.nc, C.S
        Stok = NCH * 128
        x_in = C.din("x", [Stok, D])
        cst_d = C.din("cst", [128, 512])
        rope_d = C.din("rope", [Stok, 128])
        arope_d = C.din("arope", [Stok, 128])
        sprm_d = C.din("ssd_prm", [NL * 2 * 128, 64])
        retc_d = C.din("ret_c", [NL * 2 * 128, 1024])
        attc_d = C.din("att_c", [NL * 128, 1024])
        gains_d = C.din("gains", [NL * 2 * 128, 16])
        NTILE = LW * 128 // (128 * TW)
        wf = [C.din("wf%d" % l, [NTILE * 128, TW]) for l in range(NL)]
        out = C.dout("out", [Stok // 2, D])
        gidx_d = C.din("gidx", [128, (Stok // 1024) * 20], mybir.dt.uint32)
        wbs = [C.scratch("wb%d" % l, [NTILE * 128, TW], BF16) for l in range(NL)]
        projs = [C.scratch("proj_s%d" % h, [Stok, NPROJ]) for h in range(2)]
        yT = C.scratch("yT_s", [D, Stok], BF16)
        xs = C.scratch("xs", [Stok, D])
        wvs = []
        for l in range(NL):
            wbf = wbs[l].rearrange("p n -> (p n)")
            wv, off = [], 0
            for sh, n in zip(W_SHAPES, W_SIZES):
                v = wbf[off:off + n]
                if len(sh) == 2:
                    v = v.rearrange("(r c) -> r c", c=sh[1])
                else:
                    v = v.rearrange("(f p c) -> f p c", p=sh[1], c=sh[2])
                wv.append(v); off += n
            wvs.append(wv)
        n_in = -(-(2 * W_SIZES[0]) // (128 * TW))
        C.run_phase(lambda: emit_wconv(C, list(range(n_in)), wf[0], wbs[0]))
        for l in range(NL):
            xl = x_in if l == 0 else xs
            xo = out if l == NL - 1 else xs
            wv = wvs[l]
            g_mix = gains_d[(l * 2) * 128:(l * 2 + 1) * 128, :]
            for h in range(2):
                C.run_phase(lambda: emit_inproj(C, NCH, xl, cst_d, g_mix, wv[h], projs[h]))
            rr = [(l * 2 + h) * 128 for h in range(2)]
            C.run_phase(lambda: emit_multi(C, NCH, [
                (lambda h=h: make_ssd(C, NCH, projs[h], cst_d, sprm_d[rr[h]:rr[h] + 128, :], yT[h * 512:(h + 1) * 512, :])) for h in range(2)],
                [[0, 1, 2, 3, 0, 1, 3, 2], [4, 5, 6, 7, 4, 5, 7, 6]],
                extra=([lambda: make_wconv(C, list(range(n_in, NTILE)), -(-(NTILE - n_in) // NCH), wf[0], wbs[0])] if l == 0 else [])))
            for h in range(2):
                C.run_phase(lambda: emit_att(C, NCH, projs[h], cst_d, attc_d[l * 128:(l + 1) * 128, :], arope_d,
                                             yT[1024 + h * 256:1024 + (h + 1) * 256, :]))
            C.run_phase(lambda: emit_multi(C, NCH, [
                (lambda h=h: make_ret(C, NCH, projs[h], cst_d, retc_d[rr[h]:rr[h] + 128, :], rope_d,
                                      yT[1536 + h * 256:1536 + (h + 1) * 256, :])) for h in range(2)],
                [[0, 1, 3, 1, 2, 2, 0, 0], [4, 5, 7, 5, 6, 6, 4, 4]],
                extra=([lambda: make_wconv(C, list(range(NTILE)), -(-NTILE // NCH), wf[l + 1], wbs[l + 1])] if l + 1 < NL else [])))
            g_ffn = gains_d[(l * 2 + 1) * 128:(l * 2 + 2) * 128, :]
            if l == NL - 1:
                C.run_phase(lambda: emit_outffn(C, Stok // 2, xl, yT, cst_d, g_ffn, wv[2], wv[3], wv[4], wv[5], out, gidx_d=gidx_d))
            else:
                C.run_phase(lambda: emit_outffn(C, Stok, xl, yT, cst_d, g_ffn, wv[2], wv[3], wv[4], wv[5], xo))
        return nc


def fused_inputs(inputs, S_=SEQ):
    x = np.ascontiguousarray(inputs["x"], dtype=np.float32)
    common = {"cst": const_mats(), "rope": rope_table(S_), "arope": att_rope_table(S_)}
    sprm = np.concatenate([ssd_params(inputs["conv_w"][l], inputs["conv_b"][l], inputs["dt_bias"][l], inputs["a_log"][l],
                                      inputs["d_skip"][l], inputs["ssd_norm"][l], h) for l in range(NL) for h in range(2)], axis=0)
    retc = np.concatenate([ret_consts(inputs["ret_norm"][l], h) for l in range(NL) for h in range(2)], axis=0)
    attc = np.concatenate([att_consts(inputs["q_norm"][l], inputs["k_norm"][l]) for l in range(NL)], axis=0)
    gains = np.concatenate([np.ascontiguousarray(inputs[k][l].reshape(16, 128).T) for l in range(NL) for k in ("ln_mix", "ln_ffn")], axis=0)
    common.update({"ssd_prm": sprm, "ret_c": retc, "att_c": attc, "gains": gains.astype(np.float32)})
    for l in range(NL):
        w_in = inputs["w_in"][l]
        lay = [w_in[:, half_cols(0)], w_in[:, half_cols(1)], inputs["w_out"][l], gate_layout(inputs["w_gate"][l]),
               gate_layout(inputs["w_up"][l]), inputs["w_down"][l]]
        common["wf%d" % l] = np.concatenate([np.asarray(a, np.float32).reshape(-1) for a in lay]).reshape(-1, TW)
    maps = []
    nbt = S_ // 512; nbh = nbt // 2
    p = np.arange(128, dtype=np.int64)
    for c in range(NCORES):
        m = dict(common); m["x"] = np.ascontiguousarray(x[c // 2, :S_])
        half = c % 2
        g = np.zeros((128, nbh * 20), np.int64)
        for b in range(nbh):
            for t in range(4):
                g[:, b * 20 + t] = half * (S_ // 2) + b * 512 + t * 128 + p
            for k in range(16):
                g[:, b * 20 + 4 + k] = (k * 128 + p) * nbt + (half * nbh + b)
        m["gidx"] = g.astype(np.uint32)
        maps.append(m)
    return maps


NCORES = 8


def kernel(**inputs):
    inputs = {k: np.asarray(v) for k, v in inputs.items()}
    nc = build_fused()
    res = run_bass_kernel_spmd(nc, fused_inputs(inputs), core_ids=list(range(NCORES))).results
    return np.ascontiguousarray(np.stack([np.concatenate([np.asarray(res[2 * b]["out"]), np.asarray(res[2 * b + 1]["out"])], axis=0)
                                          for b in range(NB)]), dtype=np.float32)
```

```python
import numpy as np
import ml_dtypes
from contextlib import ExitStack
import concourse.bass as bass
import concourse.mybir as mybir
from concourse.bass_utils import run_bass_kernel_spmd

F32 = mybir.dt.float32
BF16 = mybir.dt.bfloat16
AF = mybir.ActivationFunctionType
ALU = mybir.AluOpType
NPBF = ml_dtypes.bfloat16

SAME_ENG_SYNC = True
RET_STAGE = 99
EXP = ""
EPS = 1e-6

D = 2048
SEQ = 8192
NB = 4
NPROJ = 2824
DFF = 5632
OZ, OX, OB, OC, ODT, OAQ, OAK, OAV, ORQ, ORK, ORV, ORG = 0, 512, 1024, 1152, 1280, 1288, 1544, 1800, 2056, 2184, 2312, 2568


class T:
    __slots__ = ("name", "lw", "rd", "excl")

    def __init__(self, name="", excl=False):
        self.name = name
        self.lw = None
        self.rd = []
        self.excl = excl


class Buf:
    __slots__ = ("t", "d")

    def __init__(self, t, name=""):
        self.t = t
        self.d = T(name)


class Sched:
    ENGS = ("pe", "act", "dve", "pool", "sp")

    def __init__(self, nc, stack, ndma=16):
        self.nc = nc
        self.prog = {e: [] for e in self.ENGS}
        self.sem = {}
        self.cnt = {}
        self.known = {e: {} for e in self.ENGS}
        for e in self.ENGS:
            self.sem[e] = stack.enter_context(nc.semaphore("s_" + e))
            self.cnt[e] = 0
        self.dsem = {}
        self.dcount = {}
        self.ndma = ndma
        for e in ("sp", "pool", "act"):
            self.dsem[e] = [stack.enter_context(nc.semaphore("d_%s_%d" % (e, i))) for i in range(ndma)]
            self.dcount[e] = 0
        self.ninst = 0

    def _deps(self, eng, reads, writes):
        need = {}
        for t in reads:
            if t.lw is not None:
                k, v = t.lw
                if need.get(k, 0) < v:
                    need[k] = v
        for t in writes:
            if t.lw is not None:
                k, v = t.lw
                if need.get(k, 0) < v:
                    need[k] = v
            for k, v in t.rd:
                if need.get(k, 0) < v:
                    need[k] = v
        waits = []
        kn = self.known[eng]
        for k, v in need.items():
            if k == eng and (not SAME_ENG_SYNC or eng == "pe" or eng == "sp"):
                continue
            if kn.get(k, 0) >= v:
                continue
            kn[k] = v
            waits.append((k, v))
        return waits

    def _semof(self, k):
        if isinstance(k, tuple):
            return self.dsem[k[0]][k[1]]
        return self.sem[k]

    @staticmethod
    def _compact(t, key, ticket):
        t.rd = [(k, v) for (k, v) in t.rd if k != key]
        t.rd.append((key, ticket))

    def op(self, eng, fn, reads=(), writes=(), signal=True):
        ex = [t for t in reads if t.excl]
        if ex:
            reads = [t for t in reads if not t.excl]
            writes = list(writes) + ex
        waits = self._deps(eng, reads, writes)
        if signal:
            self.cnt[eng] += 1
            ticket = self.cnt[eng]
        else:
            ticket = self.cnt[eng] + 1
        sem = self.sem[eng]
        wl = [(self._semof(k), v) for k, v in waits]

        def thunk(e, fn=fn, wl=wl, signal=signal, sem=sem):
            for s, v in wl:
                e.wait_ge(s, v)
            ins = fn(e)
            if signal:
                ins.then_inc(sem, 1)
        self.prog[eng].append(thunk)
        for t in reads:
            self._compact(t, eng, ticket)
        for t in writes:
            t.lw = (eng, ticket)
            t.rd = []
        self.ninst += 1

    def dma(self, eng, fn, reads=(), writes=()):
        i = self.dcount[eng]
        self.dcount[eng] += 1
        slot = i % self.ndma
        ticket = 16 * (i // self.ndma + 1)
        key = (eng, slot)
        waits = self._deps(eng, reads, writes)
        if i >= self.ndma:
            kn = self.known[eng]
            if kn.get(key, 0) < ticket - 16:
                kn[key] = ticket - 16
                waits.append((key, ticket - 16))
        sem = self.dsem[eng][slot]
        wl = [(self._semof(k), v) for k, v in waits]

        def thunk(e, fn=fn, wl=wl, sem=sem):
            for s, v in wl:
                e.wait_ge(s, v)
            fn(e).then_inc(sem, 16)
        self.prog[eng].append(thunk)
        for t in reads:
            self._compact(t, key, ticket)
        for t in writes:
            t.lw = (key, ticket)
            t.rd = []
        self.ninst += 1

    def barrier(self):
        targets = [(e, self.cnt[e]) for e in self.ENGS if self.cnt[e] > 0]
        for q in self.dsem:
            n = self.dcount[q]
            for slot in range(min(n, self.ndma)):
                cnt = (n - slot + self.ndma - 1) // self.ndma
                targets.append(((q, slot), 16 * cnt))
        for eng in self.ENGS:
            kn = self.known[eng]
            wl = []
            for k, v in targets:
                if k == eng and eng in ("pe", "sp"):
                    continue
                if kn.get(k, 0) >= v:
                    continue
                kn[k] = v
                wl.append((self._semof(k), v))

            def thunk(e, wl=wl):
                for s_, v in wl:
                    e.wait_ge(s_, v)
            self.prog[eng].append(thunk)

    def finish(self, eng, tiles):
        waits = self._deps(eng, tiles, ())
        wl = [(self._semof(k), v) for k, v in waits]

        def thunk(e, wl=wl):
            for s, v in wl:
                e.wait_ge(s, v)
        self.prog[eng].append(thunk)

    def replay(self, block):
        prog = self.prog

        @block.tensor
        def _(e):
            for th in prog["pe"]:
                th(e)

        @block.scalar
        def _(e):
            for th in prog["act"]:
                th(e)

        @block.vector
        def _(e):
            for th in prog["dve"]:
                th(e)

        @block.gpsimd
        def _(e):
            for th in prog["pool"]:
                th(e)

        @block.sync
        def _(e):
            for th in prog["sp"]:
                th(e)


class Ctx:
    def __init__(self, st):
        self.nc = bass.Bass("TRN2", target_bir_lowering=False)
        self.st = st
        self.S = Sched(self.nc, st)
        self.outs = []
        self.phase_id = 0
        self.tag = ""
        self.bankmap = None
        self.bank = [Buf(st.enter_context(self.nc.psum_tensor("bank%d" % i, [128, 512], F32)), "bank%d" % i)
                     for i in range(8)]
        for b in self.bank:
            b.d.excl = True

    def sb(self, name, shape, dt=F32):
        return Buf(self.st.enter_context(self.nc.sbuf_tensor("sb%d%s_%s" % (self.phase_id, self.tag, name), shape, dt)), name)

    def run_phase(self, fn):
        old = self.st
        self.phase_id += 1
        with ExitStack() as pst:
            self.st = pst
            fn()
            self.S.barrier()
            with self.nc.Block() as block:
                self.S.replay(block)
            self.S.prog = {e: [] for e in self.S.ENGS}
        self.st = old

    def scratch(self, name, shape, dt=F32):
        return self.nc.dram_tensor(name, list(shape), dt, kind="Internal").ap()

    def din(self, name, shape, dt=F32):
        return self.nc.dram_tensor(name, list(shape), dt, kind="ExternalInput").ap()

    def dout(self, name, shape, dt=F32):
        return self.nc.dram_tensor(name, list(shape), dt, kind="ExternalOutput").ap()

    def done(self):
        self.S.finish("pool", self.outs)
        self.S.finish("sp", self.outs)
        with self.nc.Block() as block:
            self.S.replay(block)
        return self.nc


def bfv(bk):
    return bk.t[:, :].bitcast(BF16)


def make_identb(C, CST):
    IDB = C.sb("idb", [128, 128], BF16)
    C.S.op("act", lambda e: e.activation(out=IDB.t[:], in_=CST.t[:, 0:128], func=AF.Copy), reads=[CST.d], writes=[IDB.d])
    return IDB


def const_mats():
    i = np.arange(128)
    ident = (i[:, None] == i[None, :])
    tri = (i[:, None] <= i[None, :])
    strict = (i[:, None] > i[None, :])
    ones = np.ones((128, 128), bool)
    return np.concatenate([ident, tri, strict, ones], axis=1).astype(np.float32)


def ssd_params(conv_w, conv_b, dt_bias, a_log, d_skip, ssd_norm, h):
    ch = np.concatenate([np.arange(h * 512, (h + 1) * 512),
                         1024 + h * 128 + np.arange(128),
                         1280 + h * 128 + np.arange(128)])
    cw = conv_w[:, ch]
    cb = conv_b[ch]
    prm = np.zeros((128, 64), np.float32)
    prm[:, 0:24] = cw.reshape(4, 6, 128).transpose(2, 1, 0).reshape(128, 24)
    prm[:, 24:30] = cb.reshape(6, 128).T
    prm[:, 30:38] = np.broadcast_to(dt_bias[h * 8:(h + 1) * 8], (128, 8))
    prm[:, 38:46] = np.broadcast_to(a_log[h * 8:(h + 1) * 8], (128, 8))
    prm[:, 46:54] = np.broadcast_to(d_skip[h * 8:(h + 1) * 8], (128, 8))
    prm[:, 54:58] = ssd_norm[h * 512:(h + 1) * 512].reshape(4, 128).T
    return prm


def make_ssd(C, NCH, proj, cst_d, prm_d, yT):
    if True:
        nc, S = C.nc, C.S
        bank = [C.bank[i] for i in (C.bankmap or range(8))]
        Stok = NCH * 128
        yT_v = yT.rearrange("(t p) n -> p t n", p=128)

        CST = C.sb("cst", [128, 512]); PRM = C.sb("prm", [128, 64])
        ident = CST.t[:, 0:128]; tri = CST.t[:, 128:256]; strict = CST.t[:, 256:384]; ones = CST.t[:, 384:512]
        PIN = [C.sb("pin%d" % i, [128, 1288]) for i in range(2)]
        XB = [C.sb("xb%d" % i, [128, 6, 131]) for i in range(2)]
        CVX = C.sb("cvx", [128, 4, 128]); CVBC = C.sb("cvbc", [128, 2, 128])
        XA = C.sb("xa", [128, 4, 128]); BCA = C.sb("bca", [128, 2, 128]); BCT = C.sb("bct", [128, 2, 128], BF16)
        BTOK = C.sb("btok", [128, 128], BF16)
        DTV = C.sb("dtv", [128, 8]); DTE = C.sb("dte", [128, 8]); DT = C.sb("dt", [128, 8]); DA = C.sb("da", [128, 8])
        ANEG = C.sb("aneg", [128, 8]); ACS = C.sb("acs", [128, 16]); EAC = C.sb("eac", [128, 8]); CD = C.sb("cd", [128, 8])
        TMP8 = C.sb("tmp8", [128, 8]); DTEND = C.sb("dtend", [128, 8]); DTDTE = C.sb("dtdte", [128, 8])
        XG = C.sb("xg", [128, 512], BF16); XW = C.sb("xw", [128, 512], BF16); SKIP = C.sb("skip", [128, 512])
        U = C.sb("u", [128, 8, 128]); E = C.sb("e", [128, 8, 128]); CBM = C.sb("cbm", [128, 128])
        M = C.sb("m", [128, 8, 128], BF16)
        Y1 = C.sb("y1", [128, 512]); PREV = C.sb("prev", [128, 512]); PREVB = C.sb("prevb", [128, 512], BF16)
        SZ = C.sb("sz", [128, 512]); YZ = C.sb("yz", [128, 512]); SQ = C.sb("sq", [128, 512])
        SS = C.sb("ss", [128, 1]); RSTD = C.sb("rstd", [128, 1]); YN = C.sb("yn", [128, 512])
        YT = [C.sb("yt%d" % i, [128, 4, 512], BF16) for i in range(2)]

        S.dma("sp", lambda e: e.dma_start(out=CST.t[:], in_=cst_d[:, :]), writes=[CST.d])
        S.dma("sp", lambda e: e.dma_start(out=PRM.t[:], in_=prm_d[:, :]), writes=[PRM.d])
        S.op("pool", lambda e: e.memset(XB[0].t[:], 0.0), writes=[XB[0].d])
        S.op("pool", lambda e: e.memset(XB[1].t[:], 0.0), writes=[XB[1].d])
        S.op("pool", lambda e: e.memset(PREV.t[:], 0.0), writes=[PREV.d])
        S.op("pool", lambda e: e.memset(PREVB.t[:], 0.0), writes=[PREVB.d])
        S.op("act", lambda e: e.activation(out=ANEG.t[:], in_=PRM.t[:, 38:46], func=AF.Exp), reads=[PRM.d], writes=[ANEG.d])
        S.op("dve", lambda e: e.tensor_scalar(out=ANEG.t[:], in0=ANEG.t[:], scalar1=-1.0, scalar2=None, op0=ALU.mult),
             reads=[ANEG.d], writes=[ANEG.d])

        def bc8(b):
            return b.t[:, :].unsqueeze(2).broadcast_to([128, 8, 64])

        def v8(ap):
            return ap.rearrange("p (h d) -> p h d", h=8)

        def chunk(c):
            pin = PIN[c % 2]; xb = XB[c % 2]; xbn = XB[(c + 1) % 2]
            S.dma("sp", lambda e, pin=pin, c=c: e.dma_start(out=pin.t[:], in_=proj[c * 128:(c + 1) * 128, 0:1288]),
                  writes=[pin.d])
            for t in range(4):
                S.op("pe", lambda e, t=t, pin=pin: e.transpose(out=bank[0].t[:, t * 128:(t + 1) * 128],
                                                                 in_=pin.t[:, OX + t * 128:OX + (t + 1) * 128], identity=ident),
                     reads=[pin.d, CST.d], writes=[bank[0].d], signal=(t == 3))
            for t in range(2):
                S.op("pe", lambda e, t=t, pin=pin: e.transpose(out=bank[1].t[:, t * 128:(t + 1) * 128],
                                                                 in_=pin.t[:, OB + t * 128:OB + (t + 1) * 128], identity=ident),
                     reads=[pin.d, CST.d], writes=[bank[1].d], signal=(t == 1))
            S.op("act", lambda e, xb=xb: e.activation(out=xb.t[:, 0:4, 3:131], in_=bank[0].t[:, 0:512].rearrange("p (t n) -> p t n", t=4),
                                                       func=AF.Copy), reads=[bank[0].d], writes=[xb.d])
            S.op("act", lambda e, xb=xb: e.activation(out=xb.t[:, 4:6, 3:131], in_=bank[1].t[:, 0:256].rearrange("p (t n) -> p t n", t=2),
                                                       func=AF.Copy), reads=[bank[1].d], writes=[xb.d])
            for t in range(6):
                eng = "dve"
                cv = CVX if t < 4 else CVBC
                tt = t if t < 4 else t - 4
                S.op(eng, lambda e, t=t, tt=tt, cv=cv, xb=xb: e.tensor_scalar(
                    out=cv.t[:, tt, :], in0=xb.t[:, t, 0:128], scalar1=PRM.t[:, t * 4:t * 4 + 1],
                    scalar2=PRM.t[:, 24 + t:25 + t], op0=ALU.mult, op1=ALU.add),
                    reads=[xb.d, PRM.d], writes=[cv.d])
                for i in range(1, 4):
                    S.op(eng, lambda e, t=t, tt=tt, i=i, cv=cv, xb=xb: e.scalar_tensor_tensor(
                        out=cv.t[:, tt, :], in0=xb.t[:, t, i:i + 128], scalar=PRM.t[:, t * 4 + i:t * 4 + i + 1],
                        in1=cv.t[:, tt, :], op0=ALU.mult, op1=ALU.add),
                        reads=[xb.d, PRM.d, cv.d], writes=[cv.d])
            S.op("pool", lambda e, xb=xb, xbn=xbn: e.tensor_copy(out=xbn.t[:, :, 0:3], in_=xb.t[:, :, 128:131]),
                 reads=[xb.d], writes=[xbn.d])
            S.op("act", lambda e: e.activation(out=XA.t[:], in_=CVX.t[:], func=AF.Silu), reads=[CVX.d], writes=[XA.d])
            S.op("act", lambda e: e.activation(out=BCA.t[:], in_=CVBC.t[:], func=AF.Silu), reads=[CVBC.d], writes=[BCA.d])
            S.op("act", lambda e: e.activation(out=BCT.t[:], in_=BCA.t[:], func=AF.Copy), reads=[BCA.d], writes=[BCT.d])
            for t in range(4):
                S.op("pe", lambda e, t=t: e.transpose(out=bank[2].t[:, t * 128:(t + 1) * 128], in_=XA.t[:, t, :], identity=ident),
                     reads=[XA.d, CST.d], writes=[bank[2].d], signal=(t == 3))
            S.op("pe", lambda e: e.transpose(out=bank[3].t[:, 0:128], in_=BCA.t[:, 0, :], identity=ident),
                 reads=[BCA.d, CST.d], writes=[bank[3].d])
            S.op("act", lambda e: e.activation(out=BTOK.t[:], in_=bank[3].t[:, 0:128], func=AF.Copy),
                 reads=[bank[3].d], writes=[BTOK.d])
            S.op("dve", lambda e, pin=pin: e.tensor_tensor(out=DTV.t[:], in0=pin.t[:, ODT:ODT + 8], in1=PRM.t[:, 30:38], op=ALU.add),
                 reads=[pin.d, PRM.d], writes=[DTV.d])
            S.op("act", lambda e: e.activation(out=DTE.t[:], in_=DTV.t[:], func=AF.Exp), reads=[DTV.d], writes=[DTE.d])
            S.op("act", lambda e: e.activation(out=DT.t[:], in_=DTE.t[:], func=AF.Ln, bias=1.0), reads=[DTE.d], writes=[DT.d])
            S.op("dve", lambda e: e.tensor_tensor(out=DA.t[:], in0=DT.t[:], in1=ANEG.t[:], op=ALU.mult),
                 reads=[DT.d, ANEG.d], writes=[DA.d])
            S.op("pe", lambda e: e.matmul(bank[1].t[:, 256:264], lhsT=tri, rhs=DA.t[:], start=True, stop=True),
                 reads=[DA.d, CST.d], writes=[bank[1].d], signal=False)
            S.op("pe", lambda e: e.matmul(bank[1].t[:, 264:272], lhsT=ones, rhs=DA.t[:], start=True, stop=True),
                 reads=[DA.d, CST.d], writes=[bank[1].d])
            S.op("act", lambda e: e.activation(out=ACS.t[:], in_=bank[1].t[:, 256:272], func=AF.Copy),
                 reads=[bank[1].d], writes=[ACS.d])
            S.op("act", lambda e: e.activation(out=EAC.t[:], in_=ACS.t[:, 0:8], func=AF.Exp), reads=[ACS.d], writes=[EAC.d])
            S.op("act", lambda e: e.activation(out=CD.t[:], in_=ACS.t[:, 8:16], func=AF.Exp), reads=[ACS.d], writes=[CD.d])
            S.op("dve", lambda e: e.tensor_tensor(out=TMP8.t[:], in0=ACS.t[:, 8:16], in1=ACS.t[:, 0:8], op=ALU.subtract),
                 reads=[ACS.d], writes=[TMP8.d])
            S.op("act", lambda e: e.activation(out=DTEND.t[:], in_=TMP8.t[:], func=AF.Exp), reads=[TMP8.d], writes=[DTEND.d])
            S.op("dve", lambda e: e.tensor_tensor(out=DTDTE.t[:], in0=DT.t[:], in1=DTEND.t[:], op=ALU.mult),
                 reads=[DT.d, DTEND.d], writes=[DTDTE.d])
            S.op("dve", lambda e: e.tensor_tensor(out=v8(XG.t[:, :]), in0=v8(bank[2].t[:, :]), in1=bc8(DT), op=ALU.mult),
                 reads=[bank[2].d, DT.d], writes=[XG.d])
            S.op("dve", lambda e: e.tensor_tensor(out=v8(XW.t[:, :]), in0=v8(bank[2].t[:, :]), in1=bc8(DTDTE), op=ALU.mult),
                 reads=[bank[2].d, DTDTE.d], writes=[XW.d])
            S.op("dve", lambda e: e.tensor_tensor(out=v8(SKIP.t[:, :]), in0=v8(bank[2].t[:, :]),
                                                  in1=PRM.t[:, 46:54].unsqueeze(2).broadcast_to([128, 8, 64]), op=ALU.mult),
                 reads=[bank[2].d, PRM.d], writes=[SKIP.d])
            S.op("dve", lambda e: e.tensor_tensor(out=U.t[:], in0=strict.unsqueeze(1).broadcast_to([128, 8, 128]),
                                                  in1=DA.t[:, :].unsqueeze(2).broadcast_to([128, 8, 128]), op=ALU.mult),
                 reads=[CST.d, DA.d], writes=[U.d])
            for h in range(8):
                bk = bank[4 + h // 4]
                S.op("pe", lambda e, h=h, bk=bk: e.matmul(bk.t[:, (h % 4) * 128:(h % 4 + 1) * 128], lhsT=U.t[:, h, :], rhs=tri,
                                                          start=True, stop=True),
                     reads=[U.d, CST.d], writes=[bk.d], signal=(h % 4 == 3))
            S.op("act", lambda e: e.activation(out=E.t[:, 0:4, :], in_=bank[4].t[:, :].rearrange("p (h n) -> p h n", h=4), func=AF.Exp),
                 reads=[bank[4].d], writes=[E.d])
            S.op("act", lambda e: e.activation(out=E.t[:, 4:8, :], in_=bank[5].t[:, :].rearrange("p (h n) -> p h n", h=4), func=AF.Exp),
                 reads=[bank[5].d], writes=[E.d])
            S.op("pe", lambda e: e.matmul(bank[3].t[:, 128:256], lhsT=BCT.t[:, 0, :], rhs=BCT.t[:, 1, :], start=True, stop=True),
                 reads=[BCT.d], writes=[bank[3].d])
            S.op("dve", lambda e: e.tensor_tensor(out=CBM.t[:], in0=bank[3].t[:, 128:256], in1=tri, op=ALU.mult),
                 reads=[bank[3].d, CST.d], writes=[CBM.d])
            S.op("dve", lambda e: e.tensor_tensor(out=M.t[:], in0=E.t[:], in1=CBM.t[:, :].unsqueeze(1).broadcast_to([128, 8, 128]),
                                                  op=ALU.mult), reads=[E.d, CBM.d], writes=[M.d])
            for h in range(8):
                S.op("pe", lambda e, h=h: e.matmul(bank[0].t[:, h * 64:(h + 1) * 64], lhsT=M.t[:, h, :], rhs=XG.t[:, h * 64:(h + 1) * 64],
                                                   start=True, stop=True),
                     reads=[M.d, XG.d], writes=[bank[0].d], signal=(h == 7))
            S.op("pe", lambda e: e.matmul(bank[6].t[:, :], lhsT=BCT.t[:, 1, :], rhs=PREVB.t[:, :], start=True, stop=True),
                 reads=[BCT.d, PREVB.d], writes=[bank[6].d])
            S.op("pe", lambda e: e.matmul(bank[7].t[:, :], lhsT=BTOK.t[:, :], rhs=XW.t[:, :], start=True, stop=True),
                 reads=[BTOK.d, XW.d], writes=[bank[7].d])
            S.op("dve", lambda e: e.tensor_tensor(out=v8(Y1.t[:, :]), in0=v8(bank[6].t[:, :]), in1=bc8(EAC), op=ALU.mult),
                 reads=[bank[6].d, EAC.d], writes=[Y1.d])
            S.op("dve", lambda e: e.tensor_tensor(out=Y1.t[:], in0=Y1.t[:], in1=bank[0].t[:, :], op=ALU.add),
                 reads=[Y1.d, bank[0].d], writes=[Y1.d])
            S.op("dve", lambda e: e.tensor_tensor(out=Y1.t[:], in0=Y1.t[:], in1=SKIP.t[:], op=ALU.add),
                 reads=[Y1.d, SKIP.d], writes=[Y1.d])
            S.op("dve", lambda e: e.tensor_tensor(out=v8(PREV.t[:, :]), in0=v8(PREV.t[:, :]), in1=bc8(CD), op=ALU.mult),
                 reads=[PREV.d, CD.d], writes=[PREV.d])
            S.op("dve", lambda e: e.tensor_tensor(out=PREV.t[:], in0=PREV.t[:], in1=bank[7].t[:, :], op=ALU.add),
                 reads=[PREV.d, bank[7].d], writes=[PREV.d])
            S.op("act", lambda e: e.activation(out=PREVB.t[:], in_=PREV.t[:], func=AF.Copy), reads=[PREV.d], writes=[PREVB.d])
            S.op("act", lambda e, pin=pin: e.activation(out=SZ.t[:], in_=pin.t[:, OZ:OZ + 512], func=AF.Silu),
                 reads=[pin.d], writes=[SZ.d])
            S.op("dve", lambda e: e.tensor_tensor(out=YZ.t[:], in0=Y1.t[:], in1=SZ.t[:], op=ALU.mult),
                 reads=[Y1.d, SZ.d], writes=[YZ.d])
            S.op("act", lambda e: e.activation(out=SQ.t[:], in_=YZ.t[:], func=AF.Square, accum_out=SS.t[:]),
                 reads=[YZ.d], writes=[SQ.d, SS.d])
            S.op("act", lambda e: e.activation(out=RSTD.t[:], in_=SS.t[:], func=AF.Sqrt, bias=EPS, scale=1.0 / 512),
                 reads=[SS.d], writes=[RSTD.d])
            S.op("dve", lambda e: e.reciprocal(out=RSTD.t[:], in_=RSTD.t[:]), reads=[RSTD.d], writes=[RSTD.d])
            S.op("act", lambda e: e.activation(out=YN.t[:], in_=YZ.t[:], func=AF.Copy, scale=RSTD.t[:, 0:1]),
                 reads=[YZ.d, RSTD.d], writes=[YN.d])
            for t in range(4):
                S.op("pe", lambda e, t=t: e.transpose(out=bank[2].t[:, t * 128:(t + 1) * 128], in_=YN.t[:, t * 128:(t + 1) * 128],
                                                      identity=ident),
                     reads=[YN.d, CST.d], writes=[bank[2].d], signal=(t == 3))
            yt = YT[(c // 4) % 2]; c4 = c % 4
            S.op("dve", lambda e, yt=yt, c4=c4: e.tensor_tensor(
                out=yt.t[:, :, c4 * 128:(c4 + 1) * 128], in0=bank[2].t[:, :].rearrange("p (t n) -> p t n", t=4),
                in1=PRM.t[:, 54:58].unsqueeze(2).broadcast_to([128, 4, 128]), op=ALU.mult),
                reads=[bank[2].d, PRM.d], writes=[yt.d])
            if c4 == 3:
                c0 = (c - 3) * 128
                o = T("out"); C.outs.append(o)
                S.dma("pool", lambda e, yt=yt, c0=c0: e.dma_start(out=yT_v[:, :, c0:c0 + 512], in_=yt.t[:]),
                      reads=[yt.d], writes=[o])
        return chunk


class Recorder:
    def __init__(self):
        self.l = []

    def op(self, eng, fn, reads=(), writes=(), signal=True):
        self.l.append(("op", eng, fn, tuple(reads), tuple(writes), signal))

    def dma(self, eng, fn, reads=(), writes=()):
        self.l.append(("dma", eng, fn, tuple(reads), tuple(writes)))

    def flush_to(self, S):
        for it in self.l:
            play(S, it)
        self.l = []


def play(S, it):
    if it[0] == "op":
        S.op(it[1], it[2], reads=it[3], writes=it[4], signal=it[5])
    else:
        S.dma(it[1], it[2], reads=it[3], writes=it[4])


def emit_multi(C, NCH, makers, bankmaps, extra=()):
    S = C.S
    recs, chunks = [], []
    makers = list(makers) + list(extra)
    bankmaps = list(bankmaps) + [None] * len(extra)
    for i, mk in enumerate(makers):
        C.tag = "i%d" % i
        C.bankmap = bankmaps[i]
        r = Recorder()
        C.S = r
        chunks.append(mk())
        C.S = S
        C.bankmap = None
        r.flush_to(S)
        recs.append(r)
    C.tag = ""
    for c in range(NCH):
        lists = []
        for r, ch in zip(recs, chunks):
            ch(c)
            lists.append(r.l); r.l = []
        n = max(len(l) for l in lists)
        for k in range(n):
            for l in lists:
                if k < len(l):
                    play(S, l[k])


def half_cols(h):
    r = np.arange
    return np.concatenate([
        h * 512 + r(512),
        1024 + h * 512 + r(512),
        2048 + h * 128 + r(128),
        2304 + h * 128 + r(128),
        2560 + h * 8 + r(8),
        2576 + h * 256 + r(256),
        3088 + h * 256 + r(256),
        3600 + h * 256 + r(256),
        4112 + h * 128 + r(128),
        4368 + h * 128 + r(128),
        4624 + h * 256 + r(256),
        5136 + h * 256 + r(256),
    ])


def rope_table(S):
    pos = np.arange(S, dtype=np.float32)
    inv = (np.float32(10000.0) ** (-np.arange(0, 64, 2, dtype=np.float32) / np.float32(64))).astype(np.float32)
    ang = (pos[:, None] * inv[None, :]).astype(np.float32)
    c = np.cos(ang).astype(np.float32); s = np.sin(ang).astype(np.float32)
    return np.concatenate([c, s, c * np.float32(0.125), s * np.float32(0.125)], axis=1).astype(np.float32)


def ret_consts(ret_norm, h):
    out = np.zeros((128, 1024), np.float32)
    idx = np.arange(128, dtype=np.float64)
    for j in range(2):
        hh = 2 * h + j
        lg = np.log1p(-np.exp2(-5.0 - hh))
        rel = idx[None, :] - idx[:, None]
        dec = np.where(rel >= 0, np.exp(np.maximum(rel, 0) * lg), 0.0)
        out[:, j * 128:(j + 1) * 128] = dec
        out[j * 64:(j + 1) * 64, 256:384] = np.exp((idx + 1.0) * lg)[None, :]
        out[:, 512 + j] = np.exp((127 - idx) * lg)
        out[j * 64:(j + 1) * 64, 514] = np.exp(128 * lg)
    out[:, 516:518] = ret_norm[h * 256:(h + 1) * 256].reshape(2, 128).T
    return out


def make_ret(C, NCH, proj, cst_d, rc_d, rope_d, yT):
    if True:
        nc, S = C.nc, C.S
        bank = [C.bank[i] for i in (C.bankmap or range(8))]
        Stok = NCH * 128
        yT_v = yT.rearrange("(t p) n -> p t n", p=128)

        CST = C.sb("cst", [128, 512]); RC = C.sb("rc", [128, 1024])
        ident = CST.t[:, 0:128]
        PIN = [C.sb("pin%d" % i, [128, 768]) for i in range(2)]
        RP = [C.sb("rp%d" % i, [128, 128]) for i in range(2)]
        TA = C.sb("ta", [128, 4, 32]); TB = C.sb("tb", [128, 4, 32])
        QKR = C.sb("qkr", [128, 4, 64])
        KS = C.sb("ks", [128, 2, 64], BF16); VB = C.sb("vb", [128, 256], BF16)
        QT = C.sb("qt", [128, 128], BF16); KT = C.sb("kt", [128, 128], BF16); QST = C.sb("qst", [128, 128], BF16)
        SC = C.sb("sc", [128, 2, 128], BF16)
        PREV = C.sb("prev", [128, 128]); PREVB = C.sb("prevb", [128, 128], BF16)
        Y = C.sb("y", [128, 256]); SQ = C.sb("sq", [128, 128]); SS = C.sb("ss", [128, 2]); RSTD = C.sb("rstd", [128, 2])
        SG = C.sb("sg", [128, 256]); YN = C.sb("yn", [128, 256])
        YT = [C.sb("yt%d" % i, [128, 2, 512], BF16) for i in range(2)]

        S.dma("sp", lambda e: e.dma_start(out=CST.t[:], in_=cst_d[:, :]), writes=[CST.d])
        S.dma("sp", lambda e: e.dma_start(out=RC.t[:], in_=rc_d[:, :]), writes=[RC.d])
        S.op("pool", lambda e: e.memset(PREV.t[:], 0.0), writes=[PREV.d])
        S.op("pool", lambda e: e.memset(PREVB.t[:], 0.0), writes=[PREVB.d])

        def chunk(c):
            pin = PIN[c % 2]; rp = RP[c % 2]
            S.dma("sp", lambda e, pin=pin, c=c: e.dma_start(out=pin.t[:], in_=proj[c * 128:(c + 1) * 128, ORQ:ORQ + 768]),
                  writes=[pin.d])
            S.dma("sp", lambda e, rp=rp, c=c: e.dma_start(out=rp.t[:], in_=rope_d[c * 128:(c + 1) * 128, :]), writes=[rp.d])
            qk = pin.t[:, 0:256].rearrange("p (a h d) -> p a h d", a=2, h=2)

            def tab(rp, off):
                return rp.t[:, :].rearrange("p (a f) -> p a f", a=2)[:, :, off:off + 32].unsqueeze(2).broadcast_to([128, 2, 2, 32])
            v4 = lambda b: b.t[:, :, :].rearrange("p (a h) d -> p a h d", a=2)
            t1 = qk[:, :, :, 0:32]; t2 = qk[:, :, :, 32:64]
            o1 = QKR.t[:, :, 0:32].rearrange("p (a h) d -> p a h d", a=2)
            o2 = QKR.t[:, :, 32:64].rearrange("p (a h) d -> p a h d", a=2)
            S.op("dve", lambda e, rp=rp, t1=t1: e.tensor_tensor(out=v4(TA), in0=t1, in1=tab(rp, 0), op=ALU.mult),
                 reads=[pin.d, rp.d], writes=[TA.d])
            S.op("dve", lambda e, rp=rp, t2=t2: e.tensor_tensor(out=v4(TB), in0=t2, in1=tab(rp, 32), op=ALU.mult),
                 reads=[pin.d, rp.d], writes=[TB.d])
            S.op("dve", lambda e, o1=o1: e.tensor_tensor(out=o1, in0=v4(TA), in1=v4(TB), op=ALU.subtract),
                 reads=[TA.d, TB.d], writes=[QKR.d])
            S.op("dve", lambda e, rp=rp, t1=t1: e.tensor_tensor(out=v4(TA), in0=t1, in1=tab(rp, 32), op=ALU.mult),
                 reads=[pin.d, rp.d], writes=[TA.d])
            S.op("dve", lambda e, rp=rp, t2=t2: e.tensor_tensor(out=v4(TB), in0=t2, in1=tab(rp, 0), op=ALU.mult),
                 reads=[pin.d, rp.d], writes=[TB.d])
            S.op("dve", lambda e, o2=o2: e.tensor_tensor(out=o2, in0=v4(TA), in1=v4(TB), op=ALU.add),
                 reads=[TA.d, TB.d], writes=[QKR.d])
            if RET_STAGE < 2:
                return
            S.op("dve", lambda e: e.tensor_tensor(out=KS.t[:], in0=QKR.t[:, 2:4, :],
                                                   in1=RC.t[:, 512:514].unsqueeze(2).broadcast_to([128, 2, 64]), op=ALU.mult),
                 reads=[QKR.d, RC.d], writes=[KS.d])
            S.op("act", lambda e, pin=pin: e.activation(out=VB.t[:], in_=pin.t[:, 256:512], func=AF.Copy), reads=[pin.d], writes=[VB.d])
            if RET_STAGE < 3:
                return
            for a in range(2):
                S.op("pe", lambda e, a=a: e.transpose(out=bank[0].t[:, a * 128:(a + 1) * 128],
                                                      in_=QKR.t[:, 2 * a:2 * a + 2, :].rearrange("p h d -> p (h d)"), identity=ident),
                     reads=[QKR.d, CST.d], writes=[bank[0].d], signal=(a == 1))
            S.op("act", lambda e: e.activation(out=QT.t[:], in_=bank[0].t[:, 0:128], func=AF.Copy), reads=[bank[0].d], writes=[QT.d])
            S.op("act", lambda e: e.activation(out=KT.t[:], in_=bank[0].t[:, 128:256], func=AF.Copy), reads=[bank[0].d], writes=[KT.d])
            S.op("dve", lambda e: e.tensor_tensor(out=QST.t[:], in0=bank[0].t[:, 0:128], in1=RC.t[:, 256:384], op=ALU.mult),
                 reads=[bank[0].d, RC.d], writes=[QST.d])
            if RET_STAGE < 4:
                return
            for j in range(2):
                bk = bank[1 + 4 * j]
                S.op("pe", lambda e, j=j, bk=bk: e.matmul(bk.t[:, 0:128], lhsT=KT.t[j * 64:(j + 1) * 64, :],
                                                          rhs=QT.t[j * 64:(j + 1) * 64, :], start=True, stop=True),
                     reads=[KT.d, QT.d], writes=[bk.d])
            for j in range(2):
                bk = bank[1 + 4 * j]
                S.op("dve", lambda e, j=j, bk=bk: e.tensor_tensor(out=SC.t[:, j, :], in0=bk.t[:, 0:128],
                                                                  in1=RC.t[:, j * 128:(j + 1) * 128], op=ALU.mult),
                     reads=[bk.d, RC.d], writes=[SC.d])
            if RET_STAGE < 5:
                return
            for j in range(2):
                bk = bank[2 + 4 * j]
                S.op("pe", lambda e, j=j, bk=bk: e.matmul(bk.t[:, 0:128], lhsT=SC.t[:, j, :], rhs=VB.t[:, j * 128:(j + 1) * 128],
                                                          start=True, stop=False),
                     reads=[SC.d, VB.d], writes=[bk.d], signal=False)
                S.op("pe", lambda e, j=j, bk=bk: e.matmul(bk.t[:, 0:128], lhsT=QST.t[j * 64:(j + 1) * 64, :],
                                                          rhs=PREVB.t[j * 64:(j + 1) * 64, :], start=False, stop=True),
                     reads=[QST.d, PREVB.d], writes=[bk.d])
            if RET_STAGE < 6:
                return
            for j in range(2):
                S.op("pe", lambda e, j=j: e.matmul(bank[3].t[j * 64:(j + 1) * 64, 0:128], lhsT=KS.t[:, j, :], rhs=VB.t[:, j * 128:(j + 1) * 128],
                                                   start=True, stop=True),
                     reads=[KS.d, VB.d], writes=[bank[3].d], signal=(j == 1))
            S.op("dve", lambda e: e.scalar_tensor_tensor(out=PREV.t[:], in0=PREV.t[:], scalar=RC.t[:, 514:515],
                                                         in1=bank[3].t[:, 0:128], op0=ALU.mult, op1=ALU.add),
                 reads=[PREV.d, RC.d, bank[3].d], writes=[PREV.d])
            S.op("act", lambda e: e.activation(out=PREVB.t[:], in_=PREV.t[:], func=AF.Copy), reads=[PREV.d], writes=[PREVB.d])
            if RET_STAGE < 7:
                return
            for j in range(2):
                bk = bank[2 + 4 * j]
                S.op("act", lambda e, j=j, bk=bk: e.activation(out=Y.t[:, j * 128:(j + 1) * 128], in_=bk.t[:, 0:128], func=AF.Copy),
                     reads=[bk.d], writes=[Y.d])
            for j in range(2):
                S.op("act", lambda e, j=j: e.activation(out=SQ.t[:], in_=Y.t[:, j * 128:(j + 1) * 128], func=AF.Square,
                                                        accum_out=SS.t[:, j:j + 1]),
                     reads=[Y.d], writes=[SQ.d, SS.d])
            S.op("act", lambda e: e.activation(out=RSTD.t[:], in_=SS.t[:], func=AF.Sqrt, bias=EPS, scale=1.0 / 128),
                 reads=[SS.d], writes=[RSTD.d])
            S.op("dve", lambda e: e.reciprocal(out=RSTD.t[:], in_=RSTD.t[:]), reads=[RSTD.d], writes=[RSTD.d])
            S.op("act", lambda e, pin=pin: e.activation(out=SG.t[:], in_=pin.t[:, 512:768], func=AF.Silu), reads=[pin.d], writes=[SG.d])
            S.op("dve", lambda e: e.tensor_tensor(out=YN.t[:, :].rearrange("p (h n) -> p h n", h=2),
                                                   in0=Y.t[:, :].rearrange("p (h n) -> p h n", h=2),
                                                   in1=RSTD.t[:, :].unsqueeze(2).broadcast_to([128, 2, 128]), op=ALU.mult),
                 reads=[Y.d, RSTD.d], writes=[YN.d])
            S.op("dve", lambda e: e.tensor_tensor(out=YN.t[:], in0=YN.t[:], in1=SG.t[:], op=ALU.mult),
                 reads=[YN.d, SG.d], writes=[YN.d])
            if RET_STAGE < 8:
                return
            for t in range(2):
                S.op("pe", lambda e, t=t: e.transpose(out=bank[4].t[:, t * 128:(t + 1) * 128], in_=YN.t[:, t * 128:(t + 1) * 128],
                                                      identity=ident),
                     reads=[YN.d, CST.d], writes=[bank[4].d], signal=(t == 1))
            yt = YT[(c // 4) % 2]; c4 = c % 4
            S.op("dve", lambda e, yt=yt, c4=c4: e.tensor_tensor(
                out=yt.t[:, :, c4 * 128:(c4 + 1) * 128], in0=bank[4].t[:, 0:256].rearrange("p (t n) -> p t n", t=2),
                in1=RC.t[:, 516:518].unsqueeze(2).broadcast_to([128, 2, 128]), op=ALU.mult),
                reads=[bank[4].d, RC.d], writes=[yt.d])
            if c4 == 3:
                c0 = (c - 3) * 128
                o = T("out"); C.outs.append(o)
                S.dma("pool", lambda e, yt=yt, c0=c0: e.dma_start(out=yT_v[:, :, c0:c0 + 512], in_=yt.t[:]),
                      reads=[yt.d], writes=[o])


        return chunk


def att_rope_table(S):
    t = rope_table(S)
    return np.ascontiguousarray(np.concatenate([t[:, 64:128], t[:, 0:64]], axis=1))


def att_consts(q_norm, k_norm):
    out = np.zeros((128, 1024), np.float32)
    out[:, 0:256] = np.tile(q_norm, 4)[None, :]
    out[:, 256:512] = np.tile(k_norm, 4)[None, :]
    i = np.arange(128)
    out[:, 512:640] = (i[:, None] <= i[None, :])
    out[:, 640:768] = (i[:, None] >= i[None, :])
    out[64, 768:832] = 1.0
    return out


def emit_att(C, NCH, proj, cst_d, ac_d, rope_d, yT):
    if True:
        nc, S, bank = C.nc, C.S, C.bank
        Stok = NCH * 128

        CST = C.sb("cst", [128, 512]); AC = C.sb("ac", [128, 1024]); MSK = C.sb("msk", [128, 256], BF16)
        ident = CST.t[:, 0:128]
        PIN = [C.sb("pin%d" % i, [128, 512]) for i in range(2)]
        RP = [C.sb("rp%d" % i, [128, 128]) for i in range(2)]
        SQ = C.sb("sq", [128, 512]); SS = C.sb("ss", [128, 8]); RSTD = C.sb("rstd", [128, 8]); QN = C.sb("qn", [128, 512])
        TA = C.sb("ta", [128, 8, 32]); TB = C.sb("tb", [128, 8, 32]); QKR = C.sb("qkr", [128, 8, 64], BF16)
        IDB = make_identb(C, CST)
        QT = C.sb("qt", [128, 2, Stok], BF16); KT = C.sb("kt", [128, 2, Stok], BF16)
        ACC = [C.sb("acc%d" % j, [65, Stok]) for j in range(2)]
        VF = [C.sb("vf%d" % i, [128, 2, 64]) for i in range(2)]
        VE = [C.sb("ve%d" % i, [128, 2, 65], BF16) for i in range(3)]
        PT = [[C.sb("pt%d_%d" % (j, i), [128, 256], BF16) for i in range(2)] for j in range(2)]
        RD = C.sb("rd", [64, 512]); YO = [C.sb("yo%d" % i, [64, 2048], BF16) for i in range(2)]

        S.dma("sp", lambda e: e.dma_start(out=CST.t[:], in_=cst_d[:, :]), writes=[CST.d])
        S.dma("sp", lambda e: e.dma_start(out=AC.t[:], in_=ac_d[:, :]), writes=[AC.d])
        S.op("pool", lambda e: e.tensor_copy(out=MSK.t[:], in_=AC.t[:, 512:768]), reads=[AC.d], writes=[MSK.d])
        for i in range(3):
            S.op("pool", lambda e, i=i: e.memset(VE[i].t[:], 1.0), writes=[VE[i].d])

        for c in range(NCH):
            pin = PIN[c % 2]; rp = RP[c % 2]
            S.dma("sp", lambda e, pin=pin, c=c: e.dma_start(out=pin.t[:], in_=proj[c * 128:(c + 1) * 128, OAQ:OAQ + 512]),
                  writes=[pin.d])
            S.dma("sp", lambda e, rp=rp, c=c: e.dma_start(out=rp.t[:], in_=rope_d[c * 128:(c + 1) * 128, :]), writes=[rp.d])
            S.op("act", lambda e, pin=pin: e.activation(out=SQ.t[:], in_=pin.t[:], func=AF.Square), reads=[pin.d], writes=[SQ.d])
            S.op("dve", lambda e: e.tensor_reduce(out=SS.t[:], in_=SQ.t[:, :].rearrange("p (h d) -> p h d", h=8),
                                                  axis=mybir.AxisListType.X, op=ALU.add), reads=[SQ.d], writes=[SS.d])
            S.op("act", lambda e: e.activation(out=RSTD.t[:], in_=SS.t[:], func=AF.Sqrt, bias=EPS, scale=1.0 / 64),
                 reads=[SS.d], writes=[RSTD.d])
            S.op("dve", lambda e: e.reciprocal(out=RSTD.t[:], in_=RSTD.t[:]), reads=[RSTD.d], writes=[RSTD.d])
            S.op("dve", lambda e, pin=pin: e.tensor_tensor(out=QN.t[:, :].rearrange("p (h d) -> p h d", h=8),
                                                           in0=pin.t[:, :].rearrange("p (h d) -> p h d", h=8),
                                                           in1=RSTD.t[:, :].unsqueeze(2).broadcast_to([128, 8, 64]), op=ALU.mult),
                 reads=[pin.d, RSTD.d], writes=[QN.d])
            S.op("dve", lambda e: e.tensor_tensor(out=QN.t[:], in0=QN.t[:], in1=AC.t[:, 0:512], op=ALU.mult),
                 reads=[QN.d, AC.d], writes=[QN.d])
            qk = QN.t[:, :].rearrange("p (a h d) -> p a h d", a=2, h=4)

            def tab(rp, off):
                return rp.t[:, :].rearrange("p (a f) -> p a f", a=2)[:, :, off:off + 32].unsqueeze(2).broadcast_to([128, 2, 4, 32])
            v4 = lambda b: b.t[:, :, :].rearrange("p (a h) d -> p a h d", a=2)
            t1 = qk[:, :, :, 0:32]; t2 = qk[:, :, :, 32:64]
            o1 = QKR.t[:, :, 0:32].rearrange("p (a h) d -> p a h d", a=2)
            o2 = QKR.t[:, :, 32:64].rearrange("p (a h) d -> p a h d", a=2)
            S.op("dve", lambda e, rp=rp, t1=t1: e.tensor_tensor(out=v4(TA), in0=t1, in1=tab(rp, 0), op=ALU.mult),
                 reads=[QN.d, rp.d], writes=[TA.d])
            S.op("dve", lambda e, rp=rp, t2=t2: e.tensor_tensor(out=v4(TB), in0=t2, in1=tab(rp, 32), op=ALU.mult),
                 reads=[QN.d, rp.d], writes=[TB.d])
            S.op("dve", lambda e, o1=o1: e.tensor_tensor(out=o1, in0=v4(TA), in1=v4(TB), op=ALU.subtract),
                 reads=[TA.d, TB.d], writes=[QKR.d])
            S.op("dve", lambda e, rp=rp, t1=t1: e.tensor_tensor(out=v4(TA), in0=t1, in1=tab(rp, 32), op=ALU.mult),
                 reads=[QN.d, rp.d], writes=[TA.d])
            S.op("dve", lambda e, rp=rp, t2=t2: e.tensor_tensor(out=v4(TB), in0=t2, in1=tab(rp, 0), op=ALU.mult),
                 reads=[QN.d, rp.d], writes=[TB.d])
            S.op("dve", lambda e, o2=o2: e.tensor_tensor(out=o2, in0=v4(TA), in1=v4(TB), op=ALU.add),
                 reads=[TA.d, TB.d], writes=[QKR.d])
            for a in range(4):
                S.op("pe", lambda e, a=a: e.transpose(out=bfv(bank[0])[:, a * 128:(a + 1) * 128],
                                                      in_=QKR.t[:, 2 * a:2 * a + 2, :].rearrange("p h d -> p (h d)"), identity=IDB.t[:]),
                     reads=[QKR.d, IDB.d], writes=[bank[0].d], signal=(a == 3))
            S.op("act", lambda e, c=c: e.activation(out=QT.t[:, :, c * 128:(c + 1) * 128],
                                                    in_=bfv(bank[0])[:, 0:256].rearrange("p (a n) -> p a n", a=2), func=AF.Copy),
                 reads=[bank[0].d], writes=[QT.d])
            S.op("act", lambda e, c=c: e.activation(out=KT.t[:, :, c * 128:(c + 1) * 128],
                                                    in_=bfv(bank[0])[:, 256:512].rearrange("p (a n) -> p a n", a=2), func=AF.Copy),
                 reads=[bank[0].d], writes=[KT.d])

        ti = 0
        realS = S
        pend = []

        def flush():
            n = max([len(l) for l in pend] + [0])
            for k in range(n):
                for l in pend:
                    if k < len(l):
                        play(realS, l[k])
            del pend[:]

        for p in range(2):
            for j in range(2):
                S.op("pool", lambda e, j=j: e.memset(ACC[j].t[:], 0.0), writes=[ACC[j].d])
            for d in (1, 4, 16):
                L = Stok // d
                for r in range(d):
                    for blk in range(L // 128):
                        vf = VF[ti % 2]; ve = VE[ti % 3]; vprev = VE[(ti - 1) % 3]
                        t0 = blk * 128 * d + r
                        rec0 = Recorder(); S = rec0
                        S.dma("sp", lambda e, vf=vf, t0=t0, d=d, p=p: e.dma_start(
                            out=vf.t[:], in_=proj[t0:t0 + 127 * d + 1:d, OAV + p * 128:OAV + (p + 1) * 128].rearrange("t (h e) -> t h e", h=2)),
                            writes=[vf.d])
                        S.op("act", lambda e, vf=vf, ve=ve: e.activation(out=ve.t[:, :, 0:64], in_=vf.t[:], func=AF.Copy), reads=[vf.d], writes=[ve.d])
                        tok = slice(t0, t0 + 127 * d + 1, d)
                        tokp = slice(t0 - 128 * d, t0 - d + 1, d)
                        for j in range(2):
                            if j == 1:
                                S = Recorder()
                            pend.append(S.l)
                            pr = slice(j * 64, (j + 1) * 64)
                            sb_ = bank[1 + j + 2 * (ti % 2)]
                            ob = bank[(5 + j) if ti % 2 == 0 else (0 if j == 0 else 7)]
                            pt = PT[j][ti % 2]
                            ncol = 256 if blk > 0 else 128
                            S.op("pe", lambda e, sb_=sb_, pr=pr, p=p, tok=tok: e.matmul(
                                sb_.t[:, 0:128], lhsT=KT.t[pr, p, tok], rhs=QT.t[pr, p, tok], start=True, stop=True),
                                reads=[KT.d, QT.d], writes=[sb_.d], signal=(blk == 0))
                            if blk > 0:
                                S.op("pe", lambda e, sb_=sb_, pr=pr, p=p, tok=tok, tokp=tokp: e.matmul(
                                    sb_.t[:, 128:256], lhsT=KT.t[pr, p, tokp], rhs=QT.t[pr, p, tok], start=True, stop=True),
                                    reads=[KT.d, QT.d], writes=[sb_.d])
                            S.op("act", lambda e, pt=pt, sb_=sb_, ncol=ncol: e.activation(out=pt.t[:, 0:ncol], in_=sb_.t[:, 0:ncol], func=AF.Exp),
                                 reads=[sb_.d], writes=[pt.d])
                            meng = "dve"
                            S.op(meng, lambda e, pt=pt, ncol=ncol: e.tensor_tensor(out=pt.t[:, 0:ncol], in0=pt.t[:, 0:ncol], in1=MSK.t[:, 0:ncol],
                                                                                   op=ALU.mult),
                                 reads=[pt.d, MSK.d], writes=[pt.d])
                            S.op("pe", lambda e, ob=ob, ve=ve, j=j, pt=pt, blk=blk: e.matmul(
                                ob.t[0:65, 0:128], lhsT=ve.t[:, j, :], rhs=pt.t[:, 0:128], start=True, stop=(blk == 0)),
                                reads=[ve.d, pt.d], writes=[ob.d], signal=(blk == 0))
                            if blk > 0:
                                S.op("pe", lambda e, ob=ob, vprev=vprev, j=j, pt=pt: e.matmul(
                                    ob.t[0:65, 0:128], lhsT=vprev.t[:, j, :], rhs=pt.t[:, 128:256], start=False, stop=True),
                                    reads=[vprev.d, pt.d], writes=[ob.d])
                            S.op("dve", lambda e, j=j, ob=ob, tok=tok: e.tensor_tensor(
                                out=ACC[j].t[0:65, tok], in0=ACC[j].t[0:65, tok], in1=ob.t[0:65, 0:128], op=ALU.add),
                                reads=[ACC[j].d, ob.d], writes=[ACC[j].d])
                        ti += 1
                        S = realS
                        if len(pend) >= 4:
                            flush()
            flush()
            for j in range(2):
                h = 2 * p + j
                for n0 in range(0, Stok, 512):
                    yo = YO[(n0 // 2048) % 2]
                    S.op("pe", lambda e, j=j, n0=n0: e.matmul(bank[7].t[0:64, :], lhsT=AC.t[0:65, 768:832], rhs=ACC[j].t[0:65, n0:n0 + 512],
                                                              start=True, stop=True),
                         reads=[AC.d, ACC[j].d], writes=[bank[7].d])
                    S.op("dve", lambda e: e.reciprocal(out=RD.t[:], in_=bank[7].t[0:64, :]), reads=[bank[7].d], writes=[RD.d])
                    S.op("dve", lambda e, j=j, n0=n0, yo=yo: e.tensor_tensor(out=yo.t[:, n0 % 2048:n0 % 2048 + 512], in0=ACC[j].t[0:64, n0:n0 + 512],
                                                                             in1=RD.t[:], op=ALU.mult),
                         reads=[ACC[j].d, RD.d], writes=[yo.d])
                    if (n0 + 512) % 2048 == 0 or n0 + 512 == Stok:
                        nb0 = (n0 // 2048) * 2048; nn = n0 + 512 - nb0
                        o = T("out"); C.outs.append(o)
                        S.dma("pool", lambda e, yo=yo, h=h, nb0=nb0, nn=nn: e.dma_start(out=yT[h * 64:(h + 1) * 64, nb0:nb0 + nn], in_=yo.t[:, 0:nn]),
                              reads=[yo.d], writes=[o])


def make_wconv(C, tiles, per_chunk, wf, wb, nbuf=2):
    S = C.S
    TW_ = wf.shape[1]
    FB = [C.sb("f%d" % i, [128, TW_]) for i in range(nbuf)]
    BB = [C.sb("b%d" % i, [128, TW_], BF16) for i in range(nbuf)]
    cnt = [0]

    def chunk(c):
        for i in tiles[c * per_chunk:(c + 1) * per_chunk]:
            k = cnt[0]; cnt[0] += 1
            f = FB[k % nbuf]; b = BB[k % nbuf]
            S.dma("sp", lambda e, f=f, i=i: e.dma_start(out=f.t[:], in_=wf[i * 128:(i + 1) * 128, :]), writes=[f.d])
            if k % 2 == 0:
                S.op("act", lambda e, f=f, b=b: e.activation(out=b.t[:], in_=f.t[:], func=AF.Copy), reads=[f.d], writes=[b.d])
            else:
                S.op("dve", lambda e, f=f, b=b: e.tensor_copy(out=b.t[:], in_=f.t[:]), reads=[f.d], writes=[b.d])
            o = T("out"); C.outs.append(o)
            S.dma("pool", lambda e, b=b, i=i: e.dma_start(out=wb[i * 128:(i + 1) * 128, :], in_=b.t[:]), reads=[b.d], writes=[o])
    return chunk


def emit_wconv(C, tiles, wf, wb):
    ch = make_wconv(C, tiles, len(tiles), wf, wb, nbuf=3)
    ch(0)


def emit_norm_prep(C, xt_ap, xt_dep, XN, SQ, SS, RSTD):
    S = C.S
    S.op("act", lambda e: e.activation(out=SQ.t[:], in_=xt_ap, func=AF.Square, accum_out=SS.t[:]), reads=[xt_dep], writes=[SQ.d, SS.d])
    S.op("act", lambda e: e.activation(out=RSTD.t[:], in_=SS.t[:], func=AF.Sqrt, bias=EPS, scale=1.0 / D), reads=[SS.d], writes=[RSTD.d])
    S.op("dve", lambda e: e.reciprocal(out=RSTD.t[:], in_=RSTD.t[:]), reads=[RSTD.d], writes=[RSTD.d])
    S.op("act", lambda e: e.activation(out=XN.t[:], in_=xt_ap, func=AF.Copy, scale=RSTD.t[:, 0:1]),
         reads=[xt_dep, RSTD.d], writes=[XN.d])


def emit_norm_tr(C, XN, GAIN, ident, CST, HNT, col0, banks):
    S, bank = C.S, C.bank
    for g in range(4):
        bk = bank[banks[g % len(banks)]]
        for q in range(4):
            k = 4 * g + q
            S.op("pe", lambda e, bk=bk, q=q, k=k: e.transpose(out=bfv(bk)[:, q * 128:(q + 1) * 128], in_=XN.t[:, k * 128:(k + 1) * 128],
                                                              identity=ident.t[:]),
                 reads=[XN.d, ident.d], writes=[bk.d], signal=(q == 3))
        S.op("dve", lambda e, bk=bk, g=g: e.tensor_tensor(out=HNT.t[:, 4 * g:4 * g + 4, col0:col0 + 128],
                                                          in0=bfv(bk)[:, 0:512].rearrange("p (q n) -> p q n", q=4),
                                                          in1=GAIN.t[:, 4 * g:4 * g + 4].unsqueeze(2).broadcast_to([128, 4, 128]), op=ALU.mult),
             reads=[bk.d, GAIN.d], writes=[HNT.d])


def emit_norm_transpose(C, xt_ap, xt_dep, XN, SQ, SS, RSTD, GAIN, ident, CST, HNT, col0, banks):
    emit_norm_prep(C, xt_ap, xt_dep, XN, SQ, SS, RSTD)
    emit_norm_tr(C, XN, GAIN, ident, CST, HNT, col0, banks)


def emit_inproj(C, NCH, x, cst_d, g_d, w_d, proj):
    if True:
        nc, S, bank = C.nc, C.S, C.bank
        Stok = NCH * 128
        CST = C.sb("cst", [128, 512]); GAIN = C.sb("gain", [128, 16])
        ident = CST.t[:, 0:128]
        WB = C.sb("wb", [128, 16, NPROJ], BF16)
        XT = [C.sb("xt%d" % i, [128, D]) for i in range(3)]
        XN = C.sb("xn", [128, D], BF16); SQ = C.sb("sq", [128, D]); SS = C.sb("ss", [128, 1]); RSTD = C.sb("rstd", [128, 1])
        HNT = [C.sb("hnt%d" % i, [128, 16, 128], BF16) for i in range(2)]
        OUT = [C.sb("out%d" % i, [128, NPROJ]) for i in range(2)]
        S.dma("sp", lambda e: e.dma_start(out=CST.t[:], in_=cst_d[:, :]), writes=[CST.d])
        S.dma("sp", lambda e: e.dma_start(out=GAIN.t[:], in_=g_d[:, :]), writes=[GAIN.d])
        ident = make_identb(C, CST)
        wv = w_d.rearrange("(k p) n -> p k n", p=128)
        for k in range(16):
            S.dma("act" if k % 2 else "sp", lambda e, k=k: e.dma_start(out=WB.t[:, k, :], in_=wv[:, k, :]), writes=[WB.d])
        ncols = [(i * 512, min(512, NPROJ - i * 512)) for i in range(6)]

        def load_x(c):
            xt = XT[c % 3]
            S.dma("sp", lambda e, xt=xt, c=c: e.dma_start(out=xt.t[:], in_=x[c * 128:(c + 1) * 128, :]), writes=[xt.d])

        def mm_groups(c, groups):
            hnt = HNT[c % 2]; out = OUT[c % 2]
            for i in groups:
                n0, nw = ncols[i]
                bk = bank[2 + (c * 6 + i) % 6]
                for k in range(16):
                    S.op("pe", lambda e, bk=bk, k=k, n0=n0, nw=nw, hnt=hnt: e.matmul(bk.t[:, 0:nw], lhsT=hnt.t[:, k, :], rhs=WB.t[:, k, n0:n0 + nw],
                                                                                  start=(k == 0), stop=(k == 15)),
                         reads=[hnt.d, WB.d], writes=[bk.d], signal=(k == 15))
                if i % 2 == 0:
                    S.op("act", lambda e, bk=bk, n0=n0, nw=nw, out=out: e.activation(out=out.t[:, n0:n0 + nw], in_=bk.t[:, 0:nw], func=AF.Copy),
                         reads=[bk.d], writes=[out.d])
                else:
                    S.op("dve", lambda e, bk=bk, n0=n0, nw=nw, out=out: e.tensor_copy(out=out.t[:, n0:n0 + nw], in_=bk.t[:, 0:nw]),
                         reads=[bk.d], writes=[out.d])

        load_x(0)
        if NCH > 1:
            load_x(1)
        emit_norm_prep(C, XT[0].t[:], XT[0].d, XN, SQ, SS, RSTD)
        emit_norm_tr(C, XN, GAIN, ident, CST, HNT[0], 0, (0, 1))
        for c in range(NCH):
            if c + 2 < NCH:
                load_x(c + 2)
            if c + 1 < NCH and EXP != "noprep":
                emit_norm_prep(C, XT[(c + 1) % 3].t[:], XT[(c + 1) % 3].d, XN, SQ, SS, RSTD)
            mm_groups(c, (0, 1, 2))
            if c + 1 < NCH and EXP != "notr":
                emit_norm_tr(C, XN, GAIN, ident, CST, HNT[(c + 1) % 2], 0, (0, 1))
            mm_groups(c, (3, 4, 5))
            out = OUT[c % 2]
            o = T("out"); C.outs.append(o)
            S.dma("pool", lambda e, out=out, c=c: e.dma_start(out=proj[c * 128:(c + 1) * 128, :], in_=out.t[:]), reads=[out.d], writes=[o])


def emit_outffn(C, NT, x, yT, cst_d, g_d, wo_d, wg_d, wu_d, wd_d, xo, gidx_d=None):
    if True:
        nc, S, bank = C.nc, C.S, C.bank
        NBLK = NT // 512
        NF = DFF // 128
        CST = C.sb("cst", [128, 512]); GAIN = C.sb("gain", [128, 16])
        ident = CST.t[:, 0:128]
        X1 = C.sb("x1", [128, 4, D])
        YT = C.sb("yt", [128, 16, 512], BF16)
        HNT = C.sb("hnt", [128, 16, 512], BF16)
        HT = C.sb("ht", [128, NF, 512], BF16)
        XN = C.sb("xn", [128, D], BF16); SQ = C.sb("sq", [128, D]); SS = C.sb("ss", [128, 1]); RSTD = C.sb("rstd", [128, 1])
        WS = [C.sb("ws%d" % i, [128, 1024], BF16) for i in range(4)]
        WG = [C.sb("wg%d" % i, [128, 16, 128], BF16) for i in range(3)]
        WU = [C.sb("wu%d" % i, [128, 16, 128], BF16) for i in range(3)]
        SG = [C.sb("sg%d" % i, [128, 512]) for i in range(2)]
        S.dma("sp", lambda e: e.dma_start(out=CST.t[:], in_=cst_d[:, :]), writes=[CST.d])
        S.dma("sp", lambda e: e.dma_start(out=GAIN.t[:], in_=g_d[:, :]), writes=[GAIN.d])
        ident = make_identb(C, CST)
        yT_v = yT.rearrange("(k p) n -> p k n", p=128)
        if gidx_d is not None:
            IDX = C.sb("gidx", [128, NBLK * 20], mybir.dt.uint32)
            S.dma("sp", lambda e: e.dma_start(out=IDX.t[:], in_=gidx_d[:, :]), writes=[IDX.d])
            yT_rows = yT.rearrange("f (b n) -> (f b) n", n=512)
        wsi = [0]

        def gemm_res(lhs, lhs_dep, KCH, wdram):
            for half in range(2):
                for k in range(KCH):
                    ws = WS[wsi[0] % 4]; q = "sp" if wsi[0] % 2 == 0 else "act"; wsi[0] += 1
                    S.dma(q, lambda e, ws=ws, k=k, half=half: e.dma_start(out=ws.t[:], in_=wdram[k * 128:(k + 1) * 128, half * 1024:(half + 1) * 1024]),
                          writes=[ws.d])
                    for t in range(4):
                        for nn in range(2):
                            bk = bank[t * 2 + nn]
                            S.op("pe", lambda e, bk=bk, k=k, t=t, nn=nn, ws=ws: e.matmul(bk.t[:, :], lhsT=lhs(k, t), rhs=ws.t[:, nn * 512:(nn + 1) * 512],
                                                                                      start=(k == 0), stop=(k == KCH - 1)),
                                 reads=[lhs_dep, ws.d], writes=[bk.d], signal=(k == KCH - 1 or (t == 3 and nn == 1)))
                for t in range(4):
                    for nn in range(2):
                        bk = bank[t * 2 + nn]; c0 = half * 1024 + nn * 512
                        S.op("dve", lambda e, bk=bk, t=t, c0=c0: e.tensor_tensor(out=X1.t[:, t, c0:c0 + 512], in0=X1.t[:, t, c0:c0 + 512],
                                                                                 in1=bk.t[:, :], op=ALU.add),
                             reads=[X1.d, bk.d], writes=[X1.d])

        for b in range(NBLK):
            t0 = b * 512
            if gidx_d is None:
                S.dma("sp", lambda e, t0=t0: e.dma_start(out=X1.t[:], in_=x[t0:t0 + 512, :].rearrange("(t p) n -> p t n", p=128)), writes=[X1.d])
                S.dma("act", lambda e, t0=t0: e.dma_start(out=YT.t[:], in_=yT_v[:, :, t0:t0 + 512]), writes=[YT.d])
            else:
                for t in range(4):
                    col = b * 20 + t
                    S.dma("pool", lambda e, t=t, col=col: e.indirect_dma_start(
                        out=X1.t[:, t, :], out_offset=None, in_=x[:, :],
                        in_offset=bass.IndirectOffsetOnAxis(ap=IDX.t[:, col:col + 1], axis=0)), reads=[IDX.d], writes=[X1.d])
                for k in range(16):
                    col = b * 20 + 4 + k
                    S.dma("pool", lambda e, k=k, col=col: e.indirect_dma_start(
                        out=YT.t[:, k, :], out_offset=None, in_=yT_rows[:, :],
                        in_offset=bass.IndirectOffsetOnAxis(ap=IDX.t[:, col:col + 1], axis=0)), reads=[IDX.d], writes=[YT.d])
            gemm_res(lambda k, t: YT.t[:, k, t * 128:(t + 1) * 128], YT.d, 16, wo_d)
            for t in range(4):
                emit_norm_transpose(C, X1.t[:, t, :], X1.d, XN, SQ, SS, RSTD, GAIN, ident, CST, HNT, t * 128, (0, 1, 2, 3))
            for f in range(NF):
                wg = WG[f % 3]; wu = WU[f % 3]; sg = SG[f % 2]
                S.dma("sp", lambda e, wg=wg, f=f: e.dma_start(out=wg.t[:], in_=wg_d[f].rearrange("p (k c) -> p k c", k=16)), writes=[wg.d])
                S.dma("act", lambda e, wu=wu, f=f: e.dma_start(out=wu.t[:], in_=wu_d[f].rearrange("p (k c) -> p k c", k=16)), writes=[wu.d])
                ba = bank[4 + (f % 2) * 2]; bb = bank[5 + (f % 2) * 2]
                for k in range(16):
                    S.op("pe", lambda e, ba=ba, wg=wg, k=k: e.matmul(ba.t[:, :], lhsT=wg.t[:, k, :], rhs=HNT.t[:, k, :], start=(k == 0), stop=(k == 15)),
                         reads=[wg.d, HNT.d], writes=[ba.d], signal=(k == 15))
                for k in range(16):
                    S.op("pe", lambda e, bb=bb, wu=wu, k=k: e.matmul(bb.t[:, :], lhsT=wu.t[:, k, :], rhs=HNT.t[:, k, :], start=(k == 0), stop=(k == 15)),
                         reads=[wu.d, HNT.d], writes=[bb.d], signal=(k == 15))
                S.op("act", lambda e, ba=ba, sg=sg: e.activation(out=sg.t[:], in_=ba.t[:, :], func=AF.Silu), reads=[ba.d], writes=[sg.d])
                S.op("dve", lambda e, bb=bb, sg=sg, f=f: e.tensor_tensor(out=HT.t[:, f, :], in0=bb.t[:, :], in1=sg.t[:], op=ALU.mult),
                     reads=[bb.d, sg.d], writes=[HT.d])
            gemm_res(lambda k, t: HT.t[:, k, t * 128:(t + 1) * 128], HT.d, NF, wd_d)
            o = T("out"); C.outs.append(o)
            S.dma("pool", lambda e, t0=t0: e.dma_start(out=xo[t0:t0 + 512, :].rearrange("(t p) n -> p t n", p=128), in_=X1.t[:]),
                  reads=[X1.d], writes=[o])


def gate_layout(w):
    return np.ascontiguousarray(w.reshape(16, 128, DFF // 128, 128).transpose(2, 1, 0, 3)).reshape(DFF // 128, 128, 2048)


NL = 2
W_SHAPES = [(D, NPROJ), (D, NPROJ), (D, D), (DFF // 128, 128, 2048), (DFF // 128, 128, 2048), (DFF, D)]
W_SIZES = [int(np.prod(sh)) for sh in W_SHAPES]
LW = sum(W_SIZES) // 128
TW = 3392


def build_fused(NCH=SEQ // 128):
    st = ExitStack()
    with st:
        C = Ctx(st)
        nc, S = C.nc, C.S
        Stok = NCH * 128
        x_in = C.din("x", [Stok, D])
        cst_d = C.din("cst", [128, 512])
        rope_d = C.din("rope", [Stok, 128])
        arope_d = C.din("arope", [Stok, 128])
        sprm_d = C.din("ssd_prm", [NL * 2 * 128, 64])
        retc_d = C.din("ret_c", [NL * 2 * 128, 1024])
        attc_d = C.din("att_c", [NL * 128, 1024])
        gains_d = C.din("gains", [NL * 2 * 128, 16])
        NTILE = LW * 128 // (128 * TW)
        wf = [C.din("wf%d" % l, [NTILE * 128, TW]) for l in range(NL)]
        out = C.dout("out", [Stok // 2, D])
        gidx_d = C.din("gidx", [128, (Stok // 1024) * 20], mybir.dt.uint32)
        wbs = [C.scratch("wb%d" % l, [NTILE * 128, TW], BF16) for l in range(NL)]
        projs = [C.scratch("proj_s%d" % h, [Stok, NPROJ]) for h in range(2)]
        yT = C.scratch("yT_s", [D, Stok], BF16)
        xs = C.scratch("xs", [Stok, D])
        wvs = []
        for l in range(NL):
            wbf = wbs[l].rearrange("p n -> (p n)")
            wv, off = [], 0
            for sh, n in zip(W_SHAPES, W_SIZES):
                v = wbf[off:off + n]
                if len(sh) == 2:
                    v = v.rearrange("(r c) -> r c", c=sh[1])
                else:
                    v = v.rearrange("(f p c) -> f p c", p=sh[1], c=sh[2])
                wv.append(v); off += n
            wvs.append(wv)
        n_in = -(-(2 * W_SIZES[0]) // (128 * TW))
        C.run_phase(lambda: emit_wconv(C, list(range(n_in)), wf[0], wbs[0]))
        for l in range(NL):
            xl = x_in if l == 0 else xs
            xo = out if l == NL - 1 else xs
            wv = wvs[l]
            g_mix = gains_d[(l * 2) * 128:(l * 2 + 1) * 128, :]
            for h in range(2):
                C.run_phase(lambda: emit_inproj(C, NCH, xl, cst_d, g_mix, wv[h], projs[h]))
            rr = [(l * 2 + h) * 128 for h in range(2)]
            C.run_phase(lambda: emit_multi(C, NCH, [
                (lambda h=h: make_ssd(C, NCH, projs[h], cst_d, sprm_d[rr[h]:rr[h] + 128, :], yT[h * 512:(h + 1) * 512, :])) for h in range(2)],
                [[0, 1, 2, 3, 0, 1, 3, 2], [4, 5, 6, 7, 4, 5, 7, 6]],
                extra=([lambda: make_wconv(C, list(range(n_in, NTILE)), -(-(NTILE - n_in) // NCH), wf[0], wbs[0])] if l == 0 else [])))
            for h in range(2):
                C.run_phase(lambda: emit_att(C, NCH, projs[h], cst_d, attc_d[l * 128:(l + 1) * 128, :], arope_d,
                                             yT[1024 + h * 256:1024 + (h + 1) * 256, :]))
            C.run_phase(lambda: emit_multi(C, NCH, [
                (lambda h=h: make_ret(C, NCH, projs[h], cst_d, retc_d[rr[h]:rr[h] + 128, :], rope_d,
                                      yT[1536 + h * 256:1536 + (h + 1) * 256, :])) for h in range(2)],
                [[0, 1, 3, 1, 2, 2, 0, 0], [4, 5, 7, 5, 6, 6, 4, 4]],
                extra=([lambda: make_wconv(C, list(range(NTILE)), -(-NTILE // NCH), wf[l + 1], wbs[l + 1])] if l + 1 < NL else [])))
            g_ffn = gains_d[(l * 2 + 1) * 128:(l * 2 + 2) * 128, :]
            if l == NL - 1:
                C.run_phase(lambda: emit_outffn(C, Stok // 2, xl, yT, cst_d, g_ffn, wv[2], wv[3], wv[4], wv[5], out, gidx_d=gidx_d))
            else:
                C.run_phase(lambda: emit_outffn(C, Stok, xl, yT, cst_d, g_ffn, wv[2], wv[3], wv[4], wv[5], xo))
        return nc


def fused_inputs(inputs, S_=SEQ):
    x = np.ascontiguousarray(inputs["x"], dtype=np.float32)
    common = {"cst": const_mats(), "rope": rope_table(S_), "arope": att_rope_table(S_)}
    sprm = np.concatenate([ssd_params(inputs["conv_w"][l], inputs["conv_b"][l], inputs["dt_bias"][l], inputs["a_log"][l],
                                      inputs["d_skip"][l], inputs["ssd_norm"][l], h) for l in range(NL) for h in range(2)], axis=0)
    retc = np.concatenate([ret_consts(inputs["ret_norm"][l], h) for l in range(NL) for h in range(2)], axis=0)
    attc = np.concatenate([att_consts(inputs["q_norm"][l], inputs["k_norm"][l]) for l in range(NL)], axis=0)
    gains = np.concatenate([np.ascontiguousarray(inputs[k][l].reshape(16, 128).T) for l in range(NL) for k in ("ln_mix", "ln_ffn")], axis=0)
    common.update({"ssd_prm": sprm, "ret_c": retc, "att_c": attc, "gains": gains.astype(np.float32)})
    for l in range(NL):
        w_in = inputs["w_in"][l]
        lay = [w_in[:, half_cols(0)], w_in[:, half_cols(1)], inputs["w_out"][l], gate_layout(inputs["w_gate"][l]),
               gate_layout(inputs["w_up"][l]), inputs["w_down"][l]]
        common["wf%d" % l] = np.concatenate([np.asarray(a, np.float32).reshape(-1) for a in lay]).reshape(-1, TW)
    maps = []
    nbt = S_ // 512; nbh = nbt // 2
    p = np.arange(128, dtype=np.int64)
    for c in range(NCORES):
        m = dict(common); m["x"] = np.ascontiguousarray(x[c // 2, :S_])
        half = c % 2
        g = np.zeros((128, nbh * 20), np.int64)
        for b in range(nbh):
            for t in range(4):
                g[:, b * 20 + t] = half * (S_ // 2) + b * 512 + t * 128 + p
            for k in range(16):
                g[:, b * 20 + 4 + k] = (k * 128 + p) * nbt + (half * nbh + b)
        m["gidx"] = g.astype(np.uint32)
        maps.append(m)
    return maps


NCORES = 8


def kernel(**inputs):
    inputs = {k: np.asarray(v) for k, v in inputs.items()}
    nc = build_fused()
    res = run_bass_kernel_spmd(nc, fused_inputs(inputs), core_ids=list(range(NCORES))).results
    return np.ascontiguousarray(np.stack([np.concatenate([np.asarray(res[2 * b]["out"]), np.asarray(res[2 * b + 1]["out"])], axis=0)
                                          for b in range(NB)]), dtype=np.float32)
```

```python
import numpy as np
import ml_dtypes
from contextlib import ExitStack
import concourse.bass as bass
import concourse.mybir as mybir
from concourse.bass_utils import run_bass_kernel_spmd

F32 = mybir.dt.float32
BF16 = mybir.dt.bfloat16
AF = mybir.ActivationFunctionType
ALU = mybir.AluOpType
NPBF = ml_dtypes.bfloat16

SAME_ENG_SYNC = True
RET_STAGE = 99
EXP = ""
EPS = 1e-6

D = 2048
SEQ = 8192
NB = 4
NPROJ = 2824
DFF = 5632
OZ, OX, OB, OC, ODT, OAQ, OAK, OAV, ORQ, ORK, ORV, ORG = 0, 512, 1024, 1152, 1280, 1288, 1544, 1800, 2056, 2184, 2312, 2568


class T:
    __slots__ = ("name", "lw", "rd", "excl")

    def __init__(self, name="", excl=False):
        self.name = name
        self.lw = None
        self.rd = []
        self.excl = excl


class Buf:
    __slots__ = ("t", "d")

    def __init__(self, t, name=""):
        self.t = t
        self.d = T(name)


class Sched:
    ENGS = ("pe", "act", "dve", "pool", "sp")

    def __init__(self, nc, stack, ndma=16):
        self.nc = nc
        self.prog = {e: [] for e in self.ENGS}
        self.sem = {}
        self.cnt = {}
        self.known = {e: {} for e in self.ENGS}
        for e in self.ENGS:
            self.sem[e] = stack.enter_context(nc.semaphore("s_" + e))
            self.cnt[e] = 0
        self.dsem = {}
        self.dcount = {}
        self.ndma = ndma
        for e in ("sp", "pool", "act"):
            self.dsem[e] = [stack.enter_context(nc.semaphore("d_%s_%d" % (e, i))) for i in range(ndma)]
            self.dcount[e] = 0
        self.ninst = 0

    def _deps(self, eng, reads, writes):
        need = {}
        for t in reads:
            if t.lw is not None:
                k, v = t.lw
                if need.get(k, 0) < v:
                    need[k] = v
        for t in writes:
            if t.lw is not None:
                k, v = t.lw
                if need.get(k, 0) < v:
                    need[k] = v
            for k, v in t.rd:
                if need.get(k, 0) < v:
                    need[k] = v
        waits = []
        kn = self.known[eng]
        for k, v in need.items():
            if k == eng and (not SAME_ENG_SYNC or eng == "pe" or eng == "sp"):
                continue
            if kn.get(k, 0) >= v:
                continue
            kn[k] = v
            waits.append((k, v))
        return waits

    def _semof(self, k):
        if isinstance(k, tuple):
            return self.dsem[k[0]][k[1]]
        return self.sem[k]

    @staticmethod
    def _compact(t, key, ticket):
        t.rd = [(k, v) for (k, v) in t.rd if k != key]
        t.rd.append((key, ticket))

    def op(self, eng, fn, reads=(), writes=(), signal=True):
        ex = [t for t in reads if t.excl]
        if ex:
            reads = [t for t in reads if not t.excl]
            writes = list(writes) + ex
        waits = self._deps(eng, reads, writes)
        if signal:
            self.cnt[eng] += 1
            ticket = self.cnt[eng]
        else:
            ticket = self.cnt[eng] + 1
        sem = self.sem[eng]
        wl = [(self._semof(k), v) for k, v in waits]

        def thunk(e, fn=fn, wl=wl, signal=signal, sem=sem):
            for s, v in wl:
                e.wait_ge(s, v)
            ins = fn(e)
            if signal:
                ins.then_inc(sem, 1)
        self.prog[eng].append(thunk)
        for t in reads:
            self._compact(t, eng, ticket)
        for t in writes:
            t.lw = (eng, ticket)
            t.rd = []
        self.ninst += 1

    def dma(self, eng, fn, reads=(), writes=()):
        i = self.dcount[eng]
        self.dcount[eng] += 1
        slot = i % self.ndma
        ticket = 16 * (i // self.ndma + 1)
        key = (eng, slot)
        waits = self._deps(eng, reads, writes)
        if i >= self.ndma:
            kn = self.known[eng]
            if kn.get(key, 0) < ticket - 16:
                kn[key] = ticket - 16
                waits.append((key, ticket - 16))
        sem = self.dsem[eng][slot]
        wl = [(self._semof(k), v) for k, v in waits]

        def thunk(e, fn=fn, wl=wl, sem=sem):
            for s, v in wl:
                e.wait_ge(s, v)
            fn(e).then_inc(sem, 16)
        self.prog[eng].append(thunk)
        for t in reads:
            self._compact(t, key, ticket)
        for t in writes:
            t.lw = (key, ticket)
            t.rd = []
        self.ninst += 1

    def barrier(self):
        targets = [(e, self.cnt[e]) for e in self.ENGS if self.cnt[e] > 0]
        for q in self.dsem:
            n = self.dcount[q]
            for slot in range(min(n, self.ndma)):
                cnt = (n - slot + self.ndma - 1) // self.ndma
                targets.append(((q, slot), 16 * cnt))
        for eng in self.ENGS:
            kn = self.known[eng]
            wl = []
            for k, v in targets:
                if k == eng and eng in ("pe", "sp"):
                    continue
                if kn.get(k, 0) >= v:
                    continue
                kn[k] = v
                wl.append((self._semof(k), v))

            def thunk(e, wl=wl):
                for s_, v in wl:
                    e.wait_ge(s_, v)
            self.prog[eng].append(thunk)

    def finish(self, eng, tiles):
        waits = self._deps(eng, tiles, ())
        wl = [(self._semof(k), v) for k, v in waits]

        def thunk(e, wl=wl):
            for s, v in wl:
                e.wait_ge(s, v)
        self.prog[eng].append(thunk)

    def replay(self, block):
        prog = self.prog

        @block.tensor
        def _(e):
            for th in prog["pe"]:
                th(e)

        @block.scalar
        def _(e):
            for th in prog["act"]:
                th(e)

        @block.vector
        def _(e):
            for th in prog["dve"]:
                th(e)

        @block.gpsimd
        def _(e):
            for th in prog["pool"]:
                th(e)

        @block.sync
        def _(e):
            for th in prog["sp"]:
                th(e)


class Ctx:
    def __init__(self, st):
        self.nc = bass.Bass("TRN2", target_bir_lowering=False)
        self.st = st
        self.S = Sched(self.nc, st)
        self.outs = []
        self.phase_id = 0
        self.tag = ""
        self.bankmap = None
        self.bank = [Buf(st.enter_context(self.nc.psum_tensor("bank%d" % i, [128, 512], F32)), "bank%d" % i)
                     for i in range(8)]
        for b in self.bank:
            b.d.excl = True

    def sb(self, name, shape, dt=F32):
        return Buf(self.st.enter_context(self.nc.sbuf_tensor("sb%d%s_%s" % (self.phase_id, self.tag, name), shape, dt)), name)

    def run_phase(self, fn):
        old = self.st
        self.phase_id += 1
        with ExitStack() as pst:
            self.st = pst
            fn()
            self.S.barrier()
            with self.nc.Block() as block:
                self.S.replay(block)
            self.S.prog = {e: [] for e in self.S.ENGS}
        self.st = old

    def scratch(self, name, shape, dt=F32):
        return self.nc.dram_tensor(name, list(shape), dt, kind="Internal").ap()

    def din(self, name, shape, dt=F32):
        return self.nc.dram_tensor(name, list(shape), dt, kind="ExternalInput").ap()

    def dout(self, name, shape, dt=F32):
        return self.nc.dram_tensor(name, list(shape), dt, kind="ExternalOutput").ap()

    def done(self):
        self.S.finish("pool", self.outs)
        self.S.finish("sp", self.outs)
        with self.nc.Block() as block:
            self.S.replay(block)
        return self.nc


def bfv(bk):
    return bk.t[:, :].bitcast(BF16)


def make_identb(C, CST):
    IDB = C.sb("idb", [128, 128], BF16)
    C.S.op("act", lambda e: e.activation(out=IDB.t[:], in_=CST.t[:, 0:128], func=AF.Copy), reads=[CST.d], writes=[IDB.d])
    return IDB


def const_mats():
    i = np.arange(128)
    ident = (i[:, None] == i[None, :])
    tri = (i[:, None] <= i[None, :])
    strict = (i[:, None] > i[None, :])
    ones = np.ones((128, 128), bool)
    return np.concatenate([ident, tri, strict, ones], axis=1).astype(np.float32)


def ssd_params(conv_w, conv_b, dt_bias, a_log, d_skip, ssd_norm, h):
    ch = np.concatenate([np.arange(h * 512, (h + 1) * 512),
                         1024 + h * 128 + np.arange(128),
                         1280 + h * 128 + np.arange(128)])
    cw = conv_w[:, ch]
    cb = conv_b[ch]
    prm = np.zeros((128, 64), np.float32)
    prm[:, 0:24] = cw.reshape(4, 6, 128).transpose(2, 1, 0).reshape(128, 24)
    prm[:, 24:30] = cb.reshape(6, 128).T
    prm[:, 30:38] = np.broadcast_to(dt_bias[h * 8:(h + 1) * 8], (128, 8))
    prm[:, 38:46] = np.broadcast_to(a_log[h * 8:(h + 1) * 8], (128, 8))
    prm[:, 46:54] = np.broadcast_to(d_skip[h * 8:(h + 1) * 8], (128, 8))
    prm[:, 54:58] = ssd_norm[h * 512:(h + 1) * 512].reshape(4, 128).T
    return prm


def make_ssd(C, NCH, proj, cst_d, prm_d, yT):
    if True:
        nc, S = C.nc, C.S
        bank = [C.bank[i] for i in (C.bankmap or range(8))]
        Stok = NCH * 128
        yT_v = yT.rearrange("(t p) n -> p t n", p=128)

        CST = C.sb("cst", [128, 512]); PRM = C.sb("prm", [128, 64])
        ident = CST.t[:, 0:128]; tri = CST.t[:, 128:256]; strict = CST.t[:, 256:384]; ones = CST.t[:, 384:512]
        PIN = [C.sb("pin%d" % i, [128, 1288]) for i in range(2)]
        XB = [C.sb("xb%d" % i, [128, 6, 131]) for i in range(2)]
        CVX = C.sb("cvx", [128, 4, 128]); CVBC = C.sb("cvbc", [128, 2, 128])
        XA = C.sb("xa", [128, 4, 128]); BCA = C.sb("bca", [128, 2, 128]); BCT = C.sb("bct", [128, 2, 128], BF16)
        BTOK = C.sb("btok", [128, 128], BF16)
        DTV = C.sb("dtv", [128, 8]); DTE = C.sb("dte", [128, 8]); DT = C.sb("dt", [128, 8]); DA = C.sb("da", [128, 8])
        ANEG = C.sb("aneg", [128, 8]); ACS = C.sb("acs", [128, 16]); EAC = C.sb("eac", [128, 8]); CD = C.sb("cd", [128, 8])
        TMP8 = C.sb("tmp8", [128, 8]); DTEND = C.sb("dtend", [128, 8]); DTDTE = C.sb("dtdte", [128, 8])
        XG = C.sb("xg", [128, 512], BF16); XW = C.sb("xw", [128, 512], BF16); SKIP = C.sb("skip", [128, 512])
        U = C.sb("u", [128, 8, 128]); E = C.sb("e", [128, 8, 128]); CBM = C.sb("cbm", [128, 128])
        M = C.sb("m", [128, 8, 128], BF16)
        Y1 = C.sb("y1", [128, 512]); PREV = C.sb("prev", [128, 512]); PREVB = C.sb("prevb", [128, 512], BF16)
        SZ = C.sb("sz", [128, 512]); YZ = C.sb("yz", [128, 512]); SQ = C.sb("sq", [128, 512])
        SS = C.sb("ss", [128, 1]); RSTD = C.sb("rstd", [128, 1]); YN = C.sb("yn", [128, 512])
        YT = [C.sb("yt%d" % i, [128, 4, 512], BF16) for i in range(2)]

        S.dma("sp", lambda e: e.dma_start(out=CST.t[:], in_=cst_d[:, :]), writes=[CST.d])
        S.dma("sp", lambda e: e.dma_start(out=PRM.t[:], in_=prm_d[:, :]), writes=[PRM.d])
        S.op("pool", lambda e: e.memset(XB[0].t[:], 0.0), writes=[XB[0].d])
        S.op("pool", lambda e: e.memset(XB[1].t[:], 0.0), writes=[XB[1].d])
        S.op("pool", lambda e: e.memset(PREV.t[:], 0.0), writes=[PREV.d])
        S.op("pool", lambda e: e.memset(PREVB.t[:], 0.0), writes=[PREVB.d])
        DW = C.sb("dw", [128, 24, 128])
        for q in range(24):
            S.op("dve", lambda e, q=q: e.tensor_scalar(out=DW.t[:, q, :], in0=ident, scalar1=PRM.t[:, q:q + 1], scalar2=None, op0=ALU.mult),
                 reads=[CST.d, PRM.d], writes=[DW.d])
        S.op("act", lambda e: e.activation(out=ANEG.t[:], in_=PRM.t[:, 38:46], func=AF.Exp), reads=[PRM.d], writes=[ANEG.d])
        S.op("dve", lambda e: e.tensor_scalar(out=ANEG.t[:], in0=ANEG.t[:], scalar1=-1.0, scalar2=None, op0=ALU.mult),
             reads=[ANEG.d], writes=[ANEG.d])

        def bc8(b):
            return b.t[:, :].unsqueeze(2).broadcast_to([128, 8, 64])

        def v8(ap):
            return ap.rearrange("p (h d) -> p h d", h=8)

        def chunk(c):
            pin = PIN[c % 2]; xb = XB[c % 2]; xbn = XB[(c + 1) % 2]
            S.dma("sp", lambda e, pin=pin, c=c: e.dma_start(out=pin.t[:], in_=proj[c * 128:(c + 1) * 128, 0:1288]),
                  writes=[pin.d])
            for t in range(4):
                S.op("pe", lambda e, t=t, pin=pin: e.transpose(out=bank[0].t[:, t * 128:(t + 1) * 128],
                                                                 in_=pin.t[:, OX + t * 128:OX + (t + 1) * 128], identity=ident),
                     reads=[pin.d, CST.d], writes=[bank[0].d], signal=(t == 3))
            for t in range(2):
                S.op("pe", lambda e, t=t, pin=pin: e.transpose(out=bank[1].t[:, t * 128:(t + 1) * 128],
                                                                 in_=pin.t[:, OB + t * 128:OB + (t + 1) * 128], identity=ident),
                     reads=[pin.d, CST.d], writes=[bank[1].d], signal=(t == 1))
            S.op("act", lambda e, xb=xb: e.activation(out=xb.t[:, 0:4, 3:131], in_=bank[0].t[:, 0:512].rearrange("p (t n) -> p t n", t=4),
                                                       func=AF.Copy), reads=[bank[0].d], writes=[xb.d])
            S.op("act", lambda e, xb=xb: e.activation(out=xb.t[:, 4:6, 3:131], in_=bank[1].t[:, 0:256].rearrange("p (t n) -> p t n", t=2),
                                                       func=AF.Copy), reads=[bank[1].d], writes=[xb.d])
            for t in range(6):
                cbk = bank[0] if t < 4 else bank[1]
                tt = t if t < 4 else t - 4
                for i in range(4):
                    S.op("pe", lambda e, t=t, tt=tt, i=i, cbk=cbk, xb=xb: e.matmul(
                        cbk.t[:, tt * 128:(tt + 1) * 128], lhsT=DW.t[:, t * 4 + i, :], rhs=xb.t[:, t, i:i + 128],
                        start=(i == 0), stop=(i == 3)),
                        reads=[DW.d, xb.d], writes=[cbk.d], signal=(i == 3))
            S.op("pool", lambda e, xb=xb, xbn=xbn: e.tensor_copy(out=xbn.t[:, :, 0:3], in_=xb.t[:, :, 128:131]),
                 reads=[xb.d], writes=[xbn.d])
            for t in range(6):
                cbk = bank[0] if t < 4 else bank[1]
                tt = t if t < 4 else t - 4
                dst = XA if t < 4 else BCA
                S.op("act", lambda e, t=t, tt=tt, cbk=cbk, dst=dst: e.activation(
                    out=dst.t[:, tt, :], in_=cbk.t[:, tt * 128:(tt + 1) * 128], func=AF.Silu, bias=PRM.t[:, 24 + t:25 + t]),
                    reads=[cbk.d, PRM.d], writes=[dst.d])
            S.op("act", lambda e: e.activation(out=BCT.t[:], in_=BCA.t[:], func=AF.Copy), reads=[BCA.d], writes=[BCT.d])
            for t in range(4):
                S.op("pe", lambda e, t=t: e.transpose(out=bank[2].t[:, t * 128:(t + 1) * 128], in_=XA.t[:, t, :], identity=ident),
                     reads=[XA.d, CST.d], writes=[bank[2].d], signal=(t == 3))
            S.op("pe", lambda e: e.transpose(out=bank[3].t[:, 0:128], in_=BCA.t[:, 0, :], identity=ident),
                 reads=[BCA.d, CST.d], writes=[bank[3].d])
            S.op("act", lambda e: e.activation(out=BTOK.t[:], in_=bank[3].t[:, 0:128], func=AF.Copy),
                 reads=[bank[3].d], writes=[BTOK.d])
            S.op("dve", lambda e, pin=pin: e.tensor_tensor(out=DTV.t[:], in0=pin.t[:, ODT:ODT + 8], in1=PRM.t[:, 30:38], op=ALU.add),
                 reads=[pin.d, PRM.d], writes=[DTV.d])
            S.op("act", lambda e: e.activation(out=DTE.t[:], in_=DTV.t[:], func=AF.Exp), reads=[DTV.d], writes=[DTE.d])
            S.op("act", lambda e: e.activation(out=DT.t[:], in_=DTE.t[:], func=AF.Ln, bias=1.0), reads=[DTE.d], writes=[DT.d])
            S.op("dve", lambda e: e.tensor_tensor(out=DA.t[:], in0=DT.t[:], in1=ANEG.t[:], op=ALU.mult),
                 reads=[DT.d, ANEG.d], writes=[DA.d])
            S.op("pe", lambda e: e.matmul(bank[1].t[:, 256:264], lhsT=tri, rhs=DA.t[:], start=True, stop=True),
                 reads=[DA.d, CST.d], writes=[bank[1].d], signal=False)
            S.op("pe", lambda e: e.matmul(bank[1].t[:, 264:272], lhsT=ones, rhs=DA.t[:], start=True, stop=True),
                 reads=[DA.d, CST.d], writes=[bank[1].d])
            S.op("act", lambda e: e.activation(out=ACS.t[:], in_=bank[1].t[:, 256:272], func=AF.Copy),
                 reads=[bank[1].d], writes=[ACS.d])
            S.op("act", lambda e: e.activation(out=EAC.t[:], in_=ACS.t[:, 0:8], func=AF.Exp), reads=[ACS.d], writes=[EAC.d])
            S.op("act", lambda e: e.activation(out=CD.t[:], in_=ACS.t[:, 8:16], func=AF.Exp), reads=[ACS.d], writes=[CD.d])
            S.op("dve", lambda e: e.tensor_tensor(out=TMP8.t[:], in0=ACS.t[:, 8:16], in1=ACS.t[:, 0:8], op=ALU.subtract),
                 reads=[ACS.d], writes=[TMP8.d])
            S.op("act", lambda e: e.activation(out=DTEND.t[:], in_=TMP8.t[:], func=AF.Exp), reads=[TMP8.d], writes=[DTEND.d])
            S.op("dve", lambda e: e.tensor_tensor(out=DTDTE.t[:], in0=DT.t[:], in1=DTEND.t[:], op=ALU.mult),
                 reads=[DT.d, DTEND.d], writes=[DTDTE.d])
            S.op("dve", lambda e: e.tensor_tensor(out=v8(XG.t[:, :]), in0=v8(bank[2].t[:, :]), in1=bc8(DT), op=ALU.mult),
                 reads=[bank[2].d, DT.d], writes=[XG.d])
            S.op("dve", lambda e: e.tensor_tensor(out=v8(XW.t[:, :]), in0=v8(bank[2].t[:, :]), in1=bc8(DTDTE), op=ALU.mult),
                 reads=[bank[2].d, DTDTE.d], writes=[XW.d])
            S.op("dve", lambda e: e.tensor_tensor(out=v8(SKIP.t[:, :]), in0=v8(bank[2].t[:, :]),
                                                  in1=PRM.t[:, 46:54].unsqueeze(2).broadcast_to([128, 8, 64]), op=ALU.mult),
                 reads=[bank[2].d, PRM.d], writes=[SKIP.d])
            S.op("dve", lambda e: e.tensor_tensor(out=U.t[:], in0=strict.unsqueeze(1).broadcast_to([128, 8, 128]),
                                                  in1=DA.t[:, :].unsqueeze(2).broadcast_to([128, 8, 128]), op=ALU.mult),
                 reads=[CST.d, DA.d], writes=[U.d])
            for h in range(8):
                bk = bank[4 + h // 4]
                S.op("pe", lambda e, h=h, bk=bk: e.matmul(bk.t[:, (h % 4) * 128:(h % 4 + 1) * 128], lhsT=U.t[:, h, :], rhs=tri,
                                                          start=True, stop=True),
                     reads=[U.d, CST.d], writes=[bk.d], signal=(h % 4 == 3))
            S.op("act", lambda e: e.activation(out=E.t[:, 0:4, :], in_=bank[4].t[:, :].rearrange("p (h n) -> p h n", h=4), func=AF.Exp),
                 reads=[bank[4].d], writes=[E.d])
            S.op("act", lambda e: e.activation(out=E.t[:, 4:8, :], in_=bank[5].t[:, :].rearrange("p (h n) -> p h n", h=4), func=AF.Exp),
                 reads=[bank[5].d], writes=[E.d])
            S.op("pe", lambda e: e.matmul(bank[3].t[:, 128:256], lhsT=BCT.t[:, 0, :], rhs=BCT.t[:, 1, :], start=True, stop=True),
                 reads=[BCT.d], writes=[bank[3].d])
            S.op("dve", lambda e: e.tensor_tensor(out=CBM.t[:], in0=bank[3].t[:, 128:256], in1=tri, op=ALU.mult),
                 reads=[bank[3].d, CST.d], writes=[CBM.d])
            S.op("dve", lambda e: e.tensor_tensor(out=M.t[:], in0=E.t[:], in1=CBM.t[:, :].unsqueeze(1).broadcast_to([128, 8, 128]),
                                                  op=ALU.mult), reads=[E.d, CBM.d], writes=[M.d])
            for h in range(8):
                S.op("pe", lambda e, h=h: e.matmul(bank[0].t[:, h * 64:(h + 1) * 64], lhsT=M.t[:, h, :], rhs=XG.t[:, h * 64:(h + 1) * 64],
                                                   start=True, stop=True),
                     reads=[M.d, XG.d], writes=[bank[0].d], signal=(h == 7))
            S.op("pe", lambda e: e.matmul(bank[6].t[:, :], lhsT=BCT.t[:, 1, :], rhs=PREVB.t[:, :], start=True, stop=True),
                 reads=[BCT.d, PREVB.d], writes=[bank[6].d])
            S.op("pe", lambda e: e.matmul(bank[7].t[:, :], lhsT=BTOK.t[:, :], rhs=XW.t[:, :], start=True, stop=True),
                 reads=[BTOK.d, XW.d], writes=[bank[7].d])
            S.op("dve", lambda e: e.tensor_tensor(out=v8(Y1.t[:, :]), in0=v8(bank[6].t[:, :]), in1=bc8(EAC), op=ALU.mult),
                 reads=[bank[6].d, EAC.d], writes=[Y1.d])
            S.op("dve", lambda e: e.tensor_tensor(out=Y1.t[:], in0=Y1.t[:], in1=bank[0].t[:, :], op=ALU.add),
                 reads=[Y1.d, bank[0].d], writes=[Y1.d])
            S.op("dve", lambda e: e.tensor_tensor(out=Y1.t[:], in0=Y1.t[:], in1=SKIP.t[:], op=ALU.add),
                 reads=[Y1.d, SKIP.d], writes=[Y1.d])
            S.op("dve", lambda e: e.tensor_tensor(out=v8(PREV.t[:, :]), in0=v8(PREV.t[:, :]), in1=bc8(CD), op=ALU.mult),
                 reads=[PREV.d, CD.d], writes=[PREV.d])
            S.op("dve", lambda e: e.tensor_tensor(out=PREV.t[:], in0=PREV.t[:], in1=bank[7].t[:, :], op=ALU.add),
                 reads=[PREV.d, bank[7].d], writes=[PREV.d])
            S.op("act", lambda e: e.activation(out=PREVB.t[:], in_=PREV.t[:], func=AF.Copy), reads=[PREV.d], writes=[PREVB.d])
            S.op("act", lambda e, pin=pin: e.activation(out=SZ.t[:], in_=pin.t[:, OZ:OZ + 512], func=AF.Silu),
                 reads=[pin.d], writes=[SZ.d])
            S.op("dve", lambda e: e.tensor_tensor(out=YZ.t[:], in0=Y1.t[:], in1=SZ.t[:], op=ALU.mult),
                 reads=[Y1.d, SZ.d], writes=[YZ.d])
            S.op("act", lambda e: e.activation(out=SQ.t[:], in_=YZ.t[:], func=AF.Square, accum_out=SS.t[:]),
                 reads=[YZ.d], writes=[SQ.d, SS.d])
            S.op("act", lambda e: e.activation(out=RSTD.t[:], in_=SS.t[:], func=AF.Sqrt, bias=EPS, scale=1.0 / 512),
                 reads=[SS.d], writes=[RSTD.d])
            S.op("dve", lambda e: e.reciprocal(out=RSTD.t[:], in_=RSTD.t[:]), reads=[RSTD.d], writes=[RSTD.d])
            S.op("act", lambda e: e.activation(out=YN.t[:], in_=YZ.t[:], func=AF.Copy, scale=RSTD.t[:, 0:1]),
                 reads=[YZ.d, RSTD.d], writes=[YN.d])
            for t in range(4):
                S.op("pe", lambda e, t=t: e.transpose(out=bank[2].t[:, t * 128:(t + 1) * 128], in_=YN.t[:, t * 128:(t + 1) * 128],
                                                      identity=ident),
                     reads=[YN.d, CST.d], writes=[bank[2].d], signal=(t == 3))
            yt = YT[(c // 4) % 2]; c4 = c % 4
            S.op("dve", lambda e, yt=yt, c4=c4: e.tensor_tensor(
                out=yt.t[:, :, c4 * 128:(c4 + 1) * 128], in0=bank[2].t[:, :].rearrange("p (t n) -> p t n", t=4),
                in1=PRM.t[:, 54:58].unsqueeze(2).broadcast_to([128, 4, 128]), op=ALU.mult),
                reads=[bank[2].d, PRM.d], writes=[yt.d])
            if c4 == 3:
                c0 = (c - 3) * 128
                o = T("out"); C.outs.append(o)
                S.dma("pool", lambda e, yt=yt, c0=c0: e.dma_start(out=yT_v[:, :, c0:c0 + 512], in_=yt.t[:]),
                      reads=[yt.d], writes=[o])
        return chunk


class Recorder:
    def __init__(self):
        self.l = []

    def op(self, eng, fn, reads=(), writes=(), signal=True):
        self.l.append(("op", eng, fn, tuple(reads), tuple(writes), signal))

    def dma(self, eng, fn, reads=(), writes=()):
        self.l.append(("dma", eng, fn, tuple(reads), tuple(writes)))

    def flush_to(self, S):
        for it in self.l:
            play(S, it)
        self.l = []


def play(S, it):
    if it[0] == "op":
        S.op(it[1], it[2], reads=it[3], writes=it[4], signal=it[5])
    else:
        S.dma(it[1], it[2], reads=it[3], writes=it[4])


def emit_multi(C, NCH, makers, bankmaps, extra=()):
    S = C.S
    recs, chunks = [], []
    makers = list(makers) + list(extra)
    bankmaps = list(bankmaps) + [None] * len(extra)
    for i, mk in enumerate(makers):
        C.tag = "i%d" % i
        C.bankmap = bankmaps[i]
        r = Recorder()
        C.S = r
        chunks.append(mk())
        C.S = S
        C.bankmap = None
        r.flush_to(S)
        recs.append(r)
    C.tag = ""
    for c in range(NCH):
        lists = []
        for r, ch in zip(recs, chunks):
            ch(c)
            lists.append(r.l); r.l = []
        n = max(len(l) for l in lists)
        for k in range(n):
            for l in lists:
                if k < len(l):
                    play(S, l[k])


def half_cols(h):
    r = np.arange
    return np.concatenate([
        h * 512 + r(512),
        1024 + h * 512 + r(512),
        2048 + h * 128 + r(128),
        2304 + h * 128 + r(128),
        2560 + h * 8 + r(8),
        2576 + h * 256 + r(256),
        3088 + h * 256 + r(256),
        3600 + h * 256 + r(256),
        4112 + h * 128 + r(128),
        4368 + h * 128 + r(128),
        4624 + h * 256 + r(256),
        5136 + h * 256 + r(256),
    ])


def rope_table(S):
    pos = np.arange(S, dtype=np.float32)
    inv = (np.float32(10000.0) ** (-np.arange(0, 64, 2, dtype=np.float32) / np.float32(64))).astype(np.float32)
    ang = (pos[:, None] * inv[None, :]).astype(np.float32)
    c = np.cos(ang).astype(np.float32); s = np.sin(ang).astype(np.float32)
    return np.concatenate([c, s, c * np.float32(0.125), s * np.float32(0.125)], axis=1).astype(np.float32)


def ret_consts(ret_norm, h):
    out = np.zeros((128, 1024), np.float32)
    idx = np.arange(128, dtype=np.float64)
    for j in range(2):
        hh = 2 * h + j
        lg = np.log1p(-np.exp2(-5.0 - hh))
        rel = idx[None, :] - idx[:, None]
        dec = np.where(rel >= 0, np.exp(np.maximum(rel, 0) * lg), 0.0)
        out[:, j * 128:(j + 1) * 128] = dec
        out[j * 64:(j + 1) * 64, 256:384] = np.exp((idx + 1.0) * lg)[None, :]
        out[:, 512 + j] = np.exp((127 - idx) * lg)
        out[j * 64:(j + 1) * 64, 514] = np.exp(128 * lg)
    out[:, 516:518] = ret_norm[h * 256:(h + 1) * 256].reshape(2, 128).T
    return out


def make_ret(C, NCH, proj, cst_d, rc_d, rope_d, yT):
    if True:
        nc, S = C.nc, C.S
        bank = [C.bank[i] for i in (C.bankmap or range(8))]
        Stok = NCH * 128
        yT_v = yT.rearrange("(t p) n -> p t n", p=128)

        CST = C.sb("cst", [128, 512]); RC = C.sb("rc", [128, 1024])
        ident = CST.t[:, 0:128]
        PIN = [C.sb("pin%d" % i, [128, 768]) for i in range(2)]
        RP = [C.sb("rp%d" % i, [128, 128]) for i in range(2)]
        TA = C.sb("ta", [128, 4, 32]); TB = C.sb("tb", [128, 4, 32])
        QKR = C.sb("qkr", [128, 4, 64])
        KS = C.sb("ks", [128, 2, 64], BF16); VB = C.sb("vb", [128, 256], BF16)
        QT = C.sb("qt", [128, 128], BF16); KT = C.sb("kt", [128, 128], BF16); QST = C.sb("qst", [128, 128], BF16)
        SC = C.sb("sc", [128, 2, 128], BF16)
        PREV = C.sb("prev", [128, 128]); PREVB = C.sb("prevb", [128, 128], BF16)
        Y = C.sb("y", [128, 256]); SQ = C.sb("sq", [128, 128]); SS = C.sb("ss", [128, 2]); RSTD = C.sb("rstd", [128, 2])
        SG = C.sb("sg", [128, 256]); YN = C.sb("yn", [128, 256])
        YT = [C.sb("yt%d" % i, [128, 2, 512], BF16) for i in range(2)]

        S.dma("sp", lambda e: e.dma_start(out=CST.t[:], in_=cst_d[:, :]), writes=[CST.d])
        S.dma("sp", lambda e: e.dma_start(out=RC.t[:], in_=rc_d[:, :]), writes=[RC.d])
        S.op("pool", lambda e: e.memset(PREV.t[:], 0.0), writes=[PREV.d])
        S.op("pool", lambda e: e.memset(PREVB.t[:], 0.0), writes=[PREVB.d])

        def chunk(c):
            pin = PIN[c % 2]; rp = RP[c % 2]
            S.dma("sp", lambda e, pin=pin, c=c: e.dma_start(out=pin.t[:], in_=proj[c * 128:(c + 1) * 128, ORQ:ORQ + 768]),
                  writes=[pin.d])
            S.dma("sp", lambda e, rp=rp, c=c: e.dma_start(out=rp.t[:], in_=rope_d[c * 128:(c + 1) * 128, :]), writes=[rp.d])
            qk = pin.t[:, 0:256].rearrange("p (a h d) -> p a h d", a=2, h=2)

            def tab(rp, off):
                return rp.t[:, :].rearrange("p (a f) -> p a f", a=2)[:, :, off:off + 32].unsqueeze(2).broadcast_to([128, 2, 2, 32])
            v4 = lambda b: b.t[:, :, :].rearrange("p (a h) d -> p a h d", a=2)
            t1 = qk[:, :, :, 0:32]; t2 = qk[:, :, :, 32:64]
            o1 = QKR.t[:, :, 0:32].rearrange("p (a h) d -> p a h d", a=2)
            o2 = QKR.t[:, :, 32:64].rearrange("p (a h) d -> p a h d", a=2)
            S.op("dve", lambda e, rp=rp, t1=t1: e.tensor_tensor(out=v4(TA), in0=t1, in1=tab(rp, 0), op=ALU.mult),
                 reads=[pin.d, rp.d], writes=[TA.d])
            S.op("dve", lambda e, rp=rp, t2=t2: e.tensor_tensor(out=v4(TB), in0=t2, in1=tab(rp, 32), op=ALU.mult),
                 reads=[pin.d, rp.d], writes=[TB.d])
            S.op("dve", lambda e, o1=o1: e.tensor_tensor(out=o1, in0=v4(TA), in1=v4(TB), op=ALU.subtract),
                 reads=[TA.d, TB.d], writes=[QKR.d])
            S.op("dve", lambda e, rp=rp, t1=t1: e.tensor_tensor(out=v4(TA), in0=t1, in1=tab(rp, 32), op=ALU.mult),
                 reads=[pin.d, rp.d], writes=[TA.d])
            S.op("dve", lambda e, rp=rp, t2=t2: e.tensor_tensor(out=v4(TB), in0=t2, in1=tab(rp, 0), op=ALU.mult),
                 reads=[pin.d, rp.d], writes=[TB.d])
            S.op("dve", lambda e, o2=o2: e.tensor_tensor(out=o2, in0=v4(TA), in1=v4(TB), op=ALU.add),
                 reads=[TA.d, TB.d], writes=[QKR.d])
            if RET_STAGE < 2:
                return
            S.op("dve", lambda e: e.tensor_tensor(out=KS.t[:], in0=QKR.t[:, 2:4, :],
                                                   in1=RC.t[:, 512:514].unsqueeze(2).broadcast_to([128, 2, 64]), op=ALU.mult),
                 reads=[QKR.d, RC.d], writes=[KS.d])
            S.op("act", lambda e, pin=pin: e.activation(out=VB.t[:], in_=pin.t[:, 256:512], func=AF.Copy), reads=[pin.d], writes=[VB.d])
            if RET_STAGE < 3:
                return
            for a in range(2):
                S.op("pe", lambda e, a=a: e.transpose(out=bank[0].t[:, a * 128:(a + 1) * 128],
                                                      in_=QKR.t[:, 2 * a:2 * a + 2, :].rearrange("p h d -> p (h d)"), identity=ident),
                     reads=[QKR.d, CST.d], writes=[bank[0].d], signal=(a == 1))
            S.op("act", lambda e: e.activation(out=QT.t[:], in_=bank[0].t[:, 0:128], func=AF.Copy), reads=[bank[0].d], writes=[QT.d])
            S.op("act", lambda e: e.activation(out=KT.t[:], in_=bank[0].t[:, 128:256], func=AF.Copy), reads=[bank[0].d], writes=[KT.d])
            S.op("dve", lambda e: e.tensor_tensor(out=QST.t[:], in0=bank[0].t[:, 0:128], in1=RC.t[:, 256:384], op=ALU.mult),
                 reads=[bank[0].d, RC.d], writes=[QST.d])
            if RET_STAGE < 4:
                return
            for j in range(2):
                bk = bank[1 + 4 * j]
                S.op("pe", lambda e, j=j, bk=bk: e.matmul(bk.t[:, 0:128], lhsT=KT.t[j * 64:(j + 1) * 64, :],
                                                          rhs=QT.t[j * 64:(j + 1) * 64, :], start=True, stop=True),
                     reads=[KT.d, QT.d], writes=[bk.d])
            for j in range(2):
                bk = bank[1 + 4 * j]
                S.op("dve", lambda e, j=j, bk=bk: e.tensor_tensor(out=SC.t[:, j, :], in0=bk.t[:, 0:128],
                                                                  in1=RC.t[:, j * 128:(j + 1) * 128], op=ALU.mult),
                     reads=[bk.d, RC.d], writes=[SC.d])
            if RET_STAGE < 5:
                return
            for j in range(2):
                bk = bank[2 + 4 * j]
                S.op("pe", lambda e, j=j, bk=bk: e.matmul(bk.t[:, 0:128], lhsT=SC.t[:, j, :], rhs=VB.t[:, j * 128:(j + 1) * 128],
                                                          start=True, stop=False),
                     reads=[SC.d, VB.d], writes=[bk.d], signal=False)
                S.op("pe", lambda e, j=j, bk=bk: e.matmul(bk.t[:, 0:128], lhsT=QST.t[j * 64:(j + 1) * 64, :],
                                                          rhs=PREVB.t[j * 64:(j + 1) * 64, :], start=False, stop=True),
                     reads=[QST.d, PREVB.d], writes=[bk.d])
            if RET_STAGE < 6:
                return
            for j in range(2):
                S.op("pe", lambda e, j=j: e.matmul(bank[3].t[j * 64:(j + 1) * 64, 0:128], lhsT=KS.t[:, j, :], rhs=VB.t[:, j * 128:(j + 1) * 128],
                                                   start=True, stop=True),
                     reads=[KS.d, VB.d], writes=[bank[3].d], signal=(j == 1))
            S.op("dve", lambda e: e.scalar_tensor_tensor(out=PREV.t[:], in0=PREV.t[:], scalar=RC.t[:, 514:515],
                                                         in1=bank[3].t[:, 0:128], op0=ALU.mult, op1=ALU.add),
                 reads=[PREV.d, RC.d, bank[3].d], writes=[PREV.d])
            S.op("act", lambda e: e.activation(out=PREVB.t[:], in_=PREV.t[:], func=AF.Copy), reads=[PREV.d], writes=[PREVB.d])
            if RET_STAGE < 7:
                return
            for j in range(2):
                bk = bank[2 + 4 * j]
                S.op("act", lambda e, j=j, bk=bk: e.activation(out=Y.t[:, j * 128:(j + 1) * 128], in_=bk.t[:, 0:128], func=AF.Copy),
                     reads=[bk.d], writes=[Y.d])
            for j in range(2):
                S.op("act", lambda e, j=j: e.activation(out=SQ.t[:], in_=Y.t[:, j * 128:(j + 1) * 128], func=AF.Square,
                                                        accum_out=SS.t[:, j:j + 1]),
                     reads=[Y.d], writes=[SQ.d, SS.d])
            S.op("act", lambda e: e.activation(out=RSTD.t[:], in_=SS.t[:], func=AF.Sqrt, bias=EPS, scale=1.0 / 128),
                 reads=[SS.d], writes=[RSTD.d])
            S.op("dve", lambda e: e.reciprocal(out=RSTD.t[:], in_=RSTD.t[:]), reads=[RSTD.d], writes=[RSTD.d])
            S.op("act", lambda e, pin=pin: e.activation(out=SG.t[:], in_=pin.t[:, 512:768], func=AF.Silu), reads=[pin.d], writes=[SG.d])
            S.op("dve", lambda e: e.tensor_tensor(out=YN.t[:, :].rearrange("p (h n) -> p h n", h=2),
                                                   in0=Y.t[:, :].rearrange("p (h n) -> p h n", h=2),
                                                   in1=RSTD.t[:, :].unsqueeze(2).broadcast_to([128, 2, 128]), op=ALU.mult),
                 reads=[Y.d, RSTD.d], writes=[YN.d])
            S.op("dve", lambda e: e.tensor_tensor(out=YN.t[:], in0=YN.t[:], in1=SG.t[:], op=ALU.mult),
                 reads=[YN.d, SG.d], writes=[YN.d])
            if RET_STAGE < 8:
                return
            for t in range(2):
                S.op("pe", lambda e, t=t: e.transpose(out=bank[4].t[:, t * 128:(t + 1) * 128], in_=YN.t[:, t * 128:(t + 1) * 128],
                                                      identity=ident),
                     reads=[YN.d, CST.d], writes=[bank[4].d], signal=(t == 1))
            yt = YT[(c // 4) % 2]; c4 = c % 4
            S.op("dve", lambda e, yt=yt, c4=c4: e.tensor_tensor(
                out=yt.t[:, :, c4 * 128:(c4 + 1) * 128], in0=bank[4].t[:, 0:256].rearrange("p (t n) -> p t n", t=2),
                in1=RC.t[:, 516:518].unsqueeze(2).broadcast_to([128, 2, 128]), op=ALU.mult),
                reads=[bank[4].d, RC.d], writes=[yt.d])
            if c4 == 3:
                c0 = (c - 3) * 128
                o = T("out"); C.outs.append(o)
                S.dma("pool", lambda e, yt=yt, c0=c0: e.dma_start(out=yT_v[:, :, c0:c0 + 512], in_=yt.t[:]),
                      reads=[yt.d], writes=[o])


        return chunk


def att_rope_table(S):
    t = rope_table(S)
    return np.ascontiguousarray(np.concatenate([t[:, 64:128], t[:, 0:64]], axis=1))


def att_consts(q_norm, k_norm):
    out = np.zeros((128, 1024), np.float32)
    out[:, 0:256] = np.tile(q_norm, 4)[None, :]
    out[:, 256:512] = np.tile(k_norm, 4)[None, :]
    i = np.arange(128)
    out[:, 512:640] = (i[:, None] <= i[None, :])
    out[:, 640:768] = (i[:, None] >= i[None, :])
    out[64, 768:832] = 1.0
    return out


def emit_att(C, NCH, proj, cst_d, ac_d, rope_d, yT):
    if True:
        nc, S, bank = C.nc, C.S, C.bank
        Stok = NCH * 128

        CST = C.sb("cst", [128, 512]); AC = C.sb("ac", [128, 1024]); MSK = C.sb("msk", [128, 256], BF16)
        ident = CST.t[:, 0:128]
        PIN = [C.sb("pin%d" % i, [128, 512]) for i in range(2)]
        RP = [C.sb("rp%d" % i, [128, 128]) for i in range(2)]
        SQ = C.sb("sq", [128, 512]); SS = C.sb("ss", [128, 8]); RSTD = C.sb("rstd", [128, 8]); QN = C.sb("qn", [128, 512])
        TA = C.sb("ta", [128, 8, 32]); TB = C.sb("tb", [128, 8, 32]); QKR = C.sb("qkr", [128, 8, 64], BF16)
        IDB = make_identb(C, CST)
        QT = C.sb("qt", [128, 2, Stok], BF16); KT = C.sb("kt", [128, 2, Stok], BF16)
        ACC = [C.sb("acc%d" % j, [65, Stok]) for j in range(2)]
        VF = [C.sb("vf%d" % i, [128, 2, 64]) for i in range(2)]
        VE = [C.sb("ve%d" % i, [128, 2, 65], BF16) for i in range(3)]
        PT = [[C.sb("pt%d_%d" % (j, i), [128, 256], BF16) for i in range(2)] for j in range(2)]
        RD = C.sb("rd", [64, 512]); YO = [C.sb("yo%d" % i, [64, 2048], BF16) for i in range(2)]

        S.dma("sp", lambda e: e.dma_start(out=CST.t[:], in_=cst_d[:, :]), writes=[CST.d])
        S.dma("sp", lambda e: e.dma_start(out=AC.t[:], in_=ac_d[:, :]), writes=[AC.d])
        S.op("pool", lambda e: e.tensor_copy(out=MSK.t[:], in_=AC.t[:, 512:768]), reads=[AC.d], writes=[MSK.d])
        for i in range(3):
            S.op("pool", lambda e, i=i: e.memset(VE[i].t[:], 1.0), writes=[VE[i].d])

        P1B = [(SQ, SS, RSTD, QN, TA, TB, QKR),
               (C.sb("sq_b", [128, 512]), C.sb("ss_b", [128, 8]), C.sb("rstd_b", [128, 8]), C.sb("qn_b", [128, 512]),
                C.sb("ta_b", [128, 8, 32]), C.sb("tb_b", [128, 8, 32]), C.sb("qkr_b", [128, 8, 64], BF16))]

        def p1(c, S, SQ, SS, RSTD, QN, TA, TB, QKR, bk1):
                pin = PIN[c % 2]; rp = RP[c % 2]
                S.dma("sp", lambda e, pin=pin, c=c: e.dma_start(out=pin.t[:], in_=proj[c * 128:(c + 1) * 128, OAQ:OAQ + 512]),
                      writes=[pin.d])
                S.dma("sp", lambda e, rp=rp, c=c: e.dma_start(out=rp.t[:], in_=rope_d[c * 128:(c + 1) * 128, :]), writes=[rp.d])
                S.op("act", lambda e, pin=pin: e.activation(out=SQ.t[:], in_=pin.t[:], func=AF.Square), reads=[pin.d], writes=[SQ.d])
                S.op("dve", lambda e: e.tensor_reduce(out=SS.t[:], in_=SQ.t[:, :].rearrange("p (h d) -> p h d", h=8),
                                                      axis=mybir.AxisListType.X, op=ALU.add), reads=[SQ.d], writes=[SS.d])
                S.op("act", lambda e: e.activation(out=RSTD.t[:], in_=SS.t[:], func=AF.Sqrt, bias=EPS, scale=1.0 / 64),
                     reads=[SS.d], writes=[RSTD.d])
                S.op("dve", lambda e: e.reciprocal(out=RSTD.t[:], in_=RSTD.t[:]), reads=[RSTD.d], writes=[RSTD.d])
                S.op("dve", lambda e, pin=pin: e.tensor_tensor(out=QN.t[:, :].rearrange("p (h d) -> p h d", h=8),
                                                               in0=pin.t[:, :].rearrange("p (h d) -> p h d", h=8),
                                                               in1=RSTD.t[:, :].unsqueeze(2).broadcast_to([128, 8, 64]), op=ALU.mult),
                     reads=[pin.d, RSTD.d], writes=[QN.d])
                S.op("dve", lambda e: e.tensor_tensor(out=QN.t[:], in0=QN.t[:], in1=AC.t[:, 0:512], op=ALU.mult),
                     reads=[QN.d, AC.d], writes=[QN.d])
                qk = QN.t[:, :].rearrange("p (a h d) -> p a h d", a=2, h=4)

                def tab(rp, off):
                    return rp.t[:, :].rearrange("p (a f) -> p a f", a=2)[:, :, off:off + 32].unsqueeze(2).broadcast_to([128, 2, 4, 32])
                v4 = lambda b: b.t[:, :, :].rearrange("p (a h) d -> p a h d", a=2)
                t1 = qk[:, :, :, 0:32]; t2 = qk[:, :, :, 32:64]
                o1 = QKR.t[:, :, 0:32].rearrange("p (a h) d -> p a h d", a=2)
                o2 = QKR.t[:, :, 32:64].rearrange("p (a h) d -> p a h d", a=2)
                S.op("dve", lambda e, rp=rp, t1=t1: e.tensor_tensor(out=v4(TA), in0=t1, in1=tab(rp, 0), op=ALU.mult),
                     reads=[QN.d, rp.d], writes=[TA.d])
                S.op("dve", lambda e, rp=rp, t2=t2: e.tensor_tensor(out=v4(TB), in0=t2, in1=tab(rp, 32), op=ALU.mult),
                     reads=[QN.d, rp.d], writes=[TB.d])
                S.op("dve", lambda e, o1=o1: e.tensor_tensor(out=o1, in0=v4(TA), in1=v4(TB), op=ALU.subtract),
                     reads=[TA.d, TB.d], writes=[QKR.d])
                S.op("dve", lambda e, rp=rp, t1=t1: e.tensor_tensor(out=v4(TA), in0=t1, in1=tab(rp, 32), op=ALU.mult),
                     reads=[QN.d, rp.d], writes=[TA.d])
                S.op("dve", lambda e, rp=rp, t2=t2: e.tensor_tensor(out=v4(TB), in0=t2, in1=tab(rp, 0), op=ALU.mult),
                     reads=[QN.d, rp.d], writes=[TB.d])
                S.op("dve", lambda e, o2=o2: e.tensor_tensor(out=o2, in0=v4(TA), in1=v4(TB), op=ALU.add),
                     reads=[TA.d, TB.d], writes=[QKR.d])
                for a in range(4):
                    S.op("pe", lambda e, a=a: e.transpose(out=bfv(bk1)[:, a * 128:(a + 1) * 128],
                                                          in_=QKR.t[:, 2 * a:2 * a + 2, :].rearrange("p h d -> p (h d)"), identity=IDB.t[:]),
                         reads=[QKR.d, IDB.d], writes=[bk1.d], signal=(a == 3))
                S.op("act", lambda e, c=c: e.activation(out=QT.t[:, :, c * 128:(c + 1) * 128],
                                                        in_=bfv(bk1)[:, 0:256].rearrange("p (a n) -> p a n", a=2), func=AF.Copy),
                     reads=[bk1.d], writes=[QT.d])
                S.op("act", lambda e, c=c: e.activation(out=KT.t[:, :, c * 128:(c + 1) * 128],
                                                        in_=bfv(bk1)[:, 256:512].rearrange("p (a n) -> p a n", a=2), func=AF.Copy),
                     reads=[bk1.d], writes=[KT.d])

        pend1 = []
        for c in range(NCH):
            r_ = Recorder()
            p1(c, r_, *P1B[c % 2], bank[0] if c % 2 == 0 else bank[7])
            pend1.append(r_.l)
            if len(pend1) == 2 or c == NCH - 1:
                for k in range(max(len(l) for l in pend1)):
                    for l in pend1:
                        if k < len(l):
                            play(S, l[k])
                pend1 = []

        ti = 0
        realS = S
        pend = []

        def flush():
            n = max([len(l) for l in pend] + [0])
            for k in range(n):
                for l in pend:
                    if k < len(l):
                        play(realS, l[k])
            del pend[:]

        for p in range(2):
            for j in range(2):
                S.op("pool", lambda e, j=j: e.memset(ACC[j].t[:], 0.0), writes=[ACC[j].d])
            for d in (1, 4, 16):
                L = Stok // d
                for r in range(d):
                    for blk in range(L // 128):
                        vf = VF[ti % 2]; ve = VE[ti % 3]; vprev = VE[(ti - 1) % 3]
                        t0 = blk * 128 * d + r
                        rec0 = Recorder(); S = rec0
                        S.dma("sp", lambda e, vf=vf, t0=t0, d=d, p=p: e.dma_start(
                            out=vf.t[:], in_=proj[t0:t0 + 127 * d + 1:d, OAV + p * 128:OAV + (p + 1) * 128].rearrange("t (h e) -> t h e", h=2)),
                            writes=[vf.d])
                        S.op("act", lambda e, vf=vf, ve=ve: e.activation(out=ve.t[:, :, 0:64], in_=vf.t[:], func=AF.Copy), reads=[vf.d], writes=[ve.d])
                        tok = slice(t0, t0 + 127 * d + 1, d)
                        tokp = slice(t0 - 128 * d, t0 - d + 1, d)
                        for j in range(2):
                            if j == 1:
                                S = Recorder()
                            pend.append(S.l)
                            pr = slice(j * 64, (j + 1) * 64)
                            sb_ = bank[1 + j + 2 * (ti % 2)]
                            ob = bank[(5 + j) if ti % 2 == 0 else (0 if j == 0 else 7)]
                            pt = PT[j][ti % 2]
                            ncol = 256 if blk > 0 else 128
                            S.op("pe", lambda e, sb_=sb_, pr=pr, p=p, tok=tok: e.matmul(
                                sb_.t[:, 0:128], lhsT=KT.t[pr, p, tok], rhs=QT.t[pr, p, tok], start=True, stop=True),
                                reads=[KT.d, QT.d], writes=[sb_.d], signal=(blk == 0))
                            if blk > 0:
                                S.op("pe", lambda e, sb_=sb_, pr=pr, p=p, tok=tok, tokp=tokp: e.matmul(
                                    sb_.t[:, 128:256], lhsT=KT.t[pr, p, tokp], rhs=QT.t[pr, p, tok], start=True, stop=True),
                                    reads=[KT.d, QT.d], writes=[sb_.d])
                            S.op("act", lambda e, pt=pt, sb_=sb_, ncol=ncol: e.activation(out=pt.t[:, 0:ncol], in_=sb_.t[:, 0:ncol], func=AF.Exp),
                                 reads=[sb_.d], writes=[pt.d])
                            meng = "dve"
                            S.op(meng, lambda e, pt=pt, ncol=ncol: e.tensor_tensor(out=pt.t[:, 0:ncol], in0=pt.t[:, 0:ncol], in1=MSK.t[:, 0:ncol],
                                                                                   op=ALU.mult),
                                 reads=[pt.d, MSK.d], writes=[pt.d])
                            S.op("pe", lambda e, ob=ob, ve=ve, j=j, pt=pt, blk=blk: e.matmul(
                                ob.t[0:65, 0:128], lhsT=ve.t[:, j, :], rhs=pt.t[:, 0:128], start=True, stop=(blk == 0)),
                                reads=[ve.d, pt.d], writes=[ob.d], signal=(blk == 0))
                            if blk > 0:
                                S.op("pe", lambda e, ob=ob, vprev=vprev, j=j, pt=pt: e.matmul(
                                    ob.t[0:65, 0:128], lhsT=vprev.t[:, j, :], rhs=pt.t[:, 128:256], start=False, stop=True),
                                    reads=[vprev.d, pt.d], writes=[ob.d])
                            S.op("dve", lambda e, j=j, ob=ob, tok=tok: e.tensor_tensor(
                                out=ACC[j].t[0:65, tok], in0=ACC[j].t[0:65, tok], in1=ob.t[0:65, 0:128], op=ALU.add),
                                reads=[ACC[j].d, ob.d], writes=[ACC[j].d])
                        ti += 1
                        S = realS
                        if len(pend) >= 4:
                            flush()
            flush()
            for j in range(2):
                h = 2 * p + j
                for n0 in range(0, Stok, 512):
                    yo = YO[(n0 // 2048) % 2]
                    S.op("pe", lambda e, j=j, n0=n0: e.matmul(bank[7].t[0:64, :], lhsT=AC.t[0:65, 768:832], rhs=ACC[j].t[0:65, n0:n0 + 512],
                                                              start=True, stop=True),
                         reads=[AC.d, ACC[j].d], writes=[bank[7].d])
                    S.op("dve", lambda e: e.reciprocal(out=RD.t[:], in_=bank[7].t[0:64, :]), reads=[bank[7].d], writes=[RD.d])
                    S.op("dve", lambda e, j=j, n0=n0, yo=yo: e.tensor_tensor(out=yo.t[:, n0 % 2048:n0 % 2048 + 512], in0=ACC[j].t[0:64, n0:n0 + 512],
                                                                             in1=RD.t[:], op=ALU.mult),
                         reads=[ACC[j].d, RD.d], writes=[yo.d])
                    if (n0 + 512) % 2048 == 0 or n0 + 512 == Stok:
                        nb0 = (n0 // 2048) * 2048; nn = n0 + 512 - nb0
                        o = T("out"); C.outs.append(o)
                        S.dma("pool", lambda e, yo=yo, h=h, nb0=nb0, nn=nn: e.dma_start(out=yT[h * 64:(h + 1) * 64, nb0:nb0 + nn], in_=yo.t[:, 0:nn]),
                              reads=[yo.d], writes=[o])


def make_wconv(C, tiles, per_chunk, wf, wb, nbuf=2):
    S = C.S
    TW_ = wf.shape[1]
    FB = [C.sb("f%d" % i, [128, TW_]) for i in range(nbuf)]
    BB = [C.sb("b%d" % i, [128, TW_], BF16) for i in range(nbuf)]
    cnt = [0]

    def chunk(c):
        for i in tiles[c * per_chunk:(c + 1) * per_chunk]:
            k = cnt[0]; cnt[0] += 1
            f = FB[k % nbuf]; b = BB[k % nbuf]
            S.dma("sp", lambda e, f=f, i=i: e.dma_start(out=f.t[:], in_=wf[i * 128:(i + 1) * 128, :]), writes=[f.d])
            if k % 2 == 0:
                S.op("act", lambda e, f=f, b=b: e.activation(out=b.t[:], in_=f.t[:], func=AF.Copy), reads=[f.d], writes=[b.d])
            else:
                S.op("dve", lambda e, f=f, b=b: e.tensor_copy(out=b.t[:], in_=f.t[:]), reads=[f.d], writes=[b.d])
            o = T("out"); C.outs.append(o)
            S.dma("pool", lambda e, b=b, i=i: e.dma_start(out=wb[i * 128:(i + 1) * 128, :], in_=b.t[:]), reads=[b.d], writes=[o])
    return chunk


def emit_wconv(C, tiles, wf, wb):
    ch = make_wconv(C, tiles, len(tiles), wf, wb, nbuf=3)
    ch(0)


def emit_norm_prep(C, xt_ap, xt_dep, XN, SQ, SS, RSTD):
    S = C.S
    S.op("act", lambda e: e.activation(out=SQ.t[:], in_=xt_ap, func=AF.Square, accum_out=SS.t[:]), reads=[xt_dep], writes=[SQ.d, SS.d])
    S.op("act", lambda e: e.activation(out=RSTD.t[:], in_=SS.t[:], func=AF.Sqrt, bias=EPS, scale=1.0 / D), reads=[SS.d], writes=[RSTD.d])
    S.op("dve", lambda e: e.reciprocal(out=RSTD.t[:], in_=RSTD.t[:]), reads=[RSTD.d], writes=[RSTD.d])
    S.op("act", lambda e: e.activation(out=XN.t[:], in_=xt_ap, func=AF.Copy, scale=RSTD.t[:, 0:1]),
         reads=[xt_dep, RSTD.d], writes=[XN.d])


def emit_norm_tr(C, XN, GAIN, ident, CST, HNT, col0, banks):
    S, bank = C.S, C.bank
    for g in range(4):
        bk = bank[banks[g % len(banks)]]
        for q in range(4):
            k = 4 * g + q
            S.op("pe", lambda e, bk=bk, q=q, k=k: e.transpose(out=bfv(bk)[:, q * 128:(q + 1) * 128], in_=XN.t[:, k * 128:(k + 1) * 128],
                                                              identity=ident.t[:]),
                 reads=[XN.d, ident.d], writes=[bk.d], signal=(q == 3))
        S.op("dve", lambda e, bk=bk, g=g: e.tensor_tensor(out=HNT.t[:, 4 * g:4 * g + 4, col0:col0 + 128],
                                                          in0=bfv(bk)[:, 0:512].rearrange("p (q n) -> p q n", q=4),
                                                          in1=GAIN.t[:, 4 * g:4 * g + 4].unsqueeze(2).broadcast_to([128, 4, 128]), op=ALU.mult),
             reads=[bk.d, GAIN.d], writes=[HNT.d])


def emit_norm_transpose(C, xt_ap, xt_dep, XN, SQ, SS, RSTD, GAIN, ident, CST, HNT, col0, banks):
    emit_norm_prep(C, xt_ap, xt_dep, XN, SQ, SS, RSTD)
    emit_norm_tr(C, XN, GAIN, ident, CST, HNT, col0, banks)


def emit_inproj(C, NCH, x, cst_d, g_d, w_d, proj):
    if True:
        nc, S, bank = C.nc, C.S, C.bank
        Stok = NCH * 128
        CST = C.sb("cst", [128, 512]); GAIN = C.sb("gain", [128, 16])
        ident = CST.t[:, 0:128]
        WB = C.sb("wb", [128, 16, NPROJ], BF16)
        XT = [C.sb("xt%d" % i, [128, D]) for i in range(3)]
        XN = C.sb("xn", [128, D], BF16); SQ = C.sb("sq", [128, D]); SS = C.sb("ss", [128, 1]); RSTD = C.sb("rstd", [128, 1])
        HNT = [C.sb("hnt%d" % i, [128, 16, 128], BF16) for i in range(2)]
        OUT = [C.sb("out%d" % i, [128, NPROJ]) for i in range(2)]
        S.dma("sp", lambda e: e.dma_start(out=CST.t[:], in_=cst_d[:, :]), writes=[CST.d])
        S.dma("sp", lambda e: e.dma_start(out=GAIN.t[:], in_=g_d[:, :]), writes=[GAIN.d])
        ident = make_identb(C, CST)
        wv = w_d.rearrange("(k p) n -> p k n", p=128)
        for k in range(16):
            S.dma("act" if k % 2 else "sp", lambda e, k=k: e.dma_start(out=WB.t[:, k, :], in_=wv[:, k, :]), writes=[WB.d])
        ncols = [(i * 512, min(512, NPROJ - i * 512)) for i in range(6)]

        def load_x(c):
            xt = XT[c % 3]
            S.dma("sp", lambda e, xt=xt, c=c: e.dma_start(out=xt.t[:], in_=x[c * 128:(c + 1) * 128, :]), writes=[xt.d])

        def mm_groups(c, groups):
            hnt = HNT[c % 2]; out = OUT[c % 2]
            for i in groups:
                n0, nw = ncols[i]
                bk = bank[2 + (c * 6 + i) % 6]
                for k in range(16):
                    S.op("pe", lambda e, bk=bk, k=k, n0=n0, nw=nw, hnt=hnt: e.matmul(bk.t[:, 0:nw], lhsT=hnt.t[:, k, :], rhs=WB.t[:, k, n0:n0 + nw],
                                                                                  start=(k == 0), stop=(k == 15)),
                         reads=[hnt.d, WB.d], writes=[bk.d], signal=(k == 15))
                if i % 2 == 0:
                    S.op("act", lambda e, bk=bk, n0=n0, nw=nw, out=out: e.activation(out=out.t[:, n0:n0 + nw], in_=bk.t[:, 0:nw], func=AF.Copy),
                         reads=[bk.d], writes=[out.d])
                else:
                    S.op("dve", lambda e, bk=bk, n0=n0, nw=nw, out=out: e.tensor_copy(out=out.t[:, n0:n0 + nw], in_=bk.t[:, 0:nw]),
                         reads=[bk.d], writes=[out.d])

        load_x(0)
        if NCH > 1:
            load_x(1)
        emit_norm_prep(C, XT[0].t[:], XT[0].d, XN, SQ, SS, RSTD)
        emit_norm_tr(C, XN, GAIN, ident, CST, HNT[0], 0, (0, 1))
        for c in range(NCH):
            if c + 2 < NCH:
                load_x(c + 2)
            if c + 1 < NCH and EXP != "noprep":
                emit_norm_prep(C, XT[(c + 1) % 3].t[:], XT[(c + 1) % 3].d, XN, SQ, SS, RSTD)
            mm_groups(c, (0, 1, 2))
            if c + 1 < NCH and EXP != "notr":
                emit_norm_tr(C, XN, GAIN, ident, CST, HNT[(c + 1) % 2], 0, (0, 1))
            mm_groups(c, (3, 4, 5))
            out = OUT[c % 2]
            o = T("out"); C.outs.append(o)
            S.dma("pool", lambda e, out=out, c=c: e.dma_start(out=proj[c * 128:(c + 1) * 128, :], in_=out.t[:]), reads=[out.d], writes=[o])


def emit_outffn(C, NT, x, yT, cst_d, g_d, wo_d, wg_d, wu_d, wd_d, xo, gidx_d=None):
    if True:
        nc, S, bank = C.nc, C.S, C.bank
        NBLK = NT // 512
        NF = DFF // 128
        CST = C.sb("cst", [128, 512]); GAIN = C.sb("gain", [128, 16])
        ident = CST.t[:, 0:128]
        X1 = C.sb("x1", [128, 4, D])
        YT = C.sb("yt", [128, 16, 512], BF16)
        HNT = C.sb("hnt", [128, 16, 512], BF16)
        HT = C.sb("ht", [128, NF, 512], BF16)
        XN = C.sb("xn", [128, D], BF16); SQ = C.sb("sq", [128, D]); SS = C.sb("ss", [128, 1]); RSTD = C.sb("rstd", [128, 1])
        WS = [C.sb("ws%d" % i, [128, 1024], BF16) for i in range(4)]
        WG = [C.sb("wg%d" % i, [128, 16, 128], BF16) for i in range(3)]
        WU = [C.sb("wu%d" % i, [128, 16, 128], BF16) for i in range(3)]
        SG = [C.sb("sg%d" % i, [128, 512]) for i in range(2)]
        S.dma("sp", lambda e: e.dma_start(out=CST.t[:], in_=cst_d[:, :]), writes=[CST.d])
        S.dma("sp", lambda e: e.dma_start(out=GAIN.t[:], in_=g_d[:, :]), writes=[GAIN.d])
        ident = make_identb(C, CST)
        yT_v = yT.rearrange("(k p) n -> p k n", p=128)
        if gidx_d is not None:
            IDX = C.sb("gidx", [128, NBLK * 20], mybir.dt.uint32)
            S.dma("sp", lambda e: e.dma_start(out=IDX.t[:], in_=gidx_d[:, :]), writes=[IDX.d])
            yT_rows = yT.rearrange("f (b n) -> (f b) n", n=512)
        wsi = [0]

        def gemm_res(lhs, lhs_dep, KCH, wdram):
            for half in range(2):
                for k in range(KCH):
                    ws = WS[wsi[0] % 4]; q = "sp" if wsi[0] % 2 == 0 else "act"; wsi[0] += 1
                    S.dma(q, lambda e, ws=ws, k=k, half=half: e.dma_start(out=ws.t[:], in_=wdram[k * 128:(k + 1) * 128, half * 1024:(half + 1) * 1024]),
                          writes=[ws.d])
                    for t in range(4):
                        for nn in range(2):
                            bk = bank[t * 2 + nn]
                            S.op("pe", lambda e, bk=bk, k=k, t=t, nn=nn, ws=ws: e.matmul(bk.t[:, :], lhsT=lhs(k, t), rhs=ws.t[:, nn * 512:(nn + 1) * 512],
                                                                                      start=(k == 0), stop=(k == KCH - 1)),
                                 reads=[lhs_dep, ws.d], writes=[bk.d], signal=(k == KCH - 1 or (t == 3 and nn == 1)))
                for t in range(4):
                    for nn in range(2):
                        bk = bank[t * 2 + nn]; c0 = half * 1024 + nn * 512
                        S.op("dve", lambda e, bk=bk, t=t, c0=c0: e.tensor_tensor(out=X1.t[:, t, c0:c0 + 512], in0=X1.t[:, t, c0:c0 + 512],
                                                                                 in1=bk.t[:, :], op=ALU.add),
                             reads=[X1.d, bk.d], writes=[X1.d])

        for b in range(NBLK):
            t0 = b * 512
            if gidx_d is None:
                S.dma("sp", lambda e, t0=t0: e.dma_start(out=X1.t[:], in_=x[t0:t0 + 512, :].rearrange("(t p) n -> p t n", p=128)), writes=[X1.d])
                S.dma("act", lambda e, t0=t0: e.dma_start(out=YT.t[:], in_=yT_v[:, :, t0:t0 + 512]), writes=[YT.d])
            else:
                for t in range(4):
                    col = b * 20 + t
                    S.dma("pool", lambda e, t=t, col=col: e.indirect_dma_start(
                        out=X1.t[:, t, :], out_offset=None, in_=x[:, :],
                        in_offset=bass.IndirectOffsetOnAxis(ap=IDX.t[:, col:col + 1], axis=0)), reads=[IDX.d], writes=[X1.d])
                for k in range(16):
                    col = b * 20 + 4 + k
                    S.dma("pool", lambda e, k=k, col=col: e.indirect_dma_start(
                        out=YT.t[:, k, :], out_offset=None, in_=yT_rows[:, :],
                        in_offset=bass.IndirectOffsetOnAxis(ap=IDX.t[:, col:col + 1], axis=0)), reads=[IDX.d], writes=[YT.d])
            gemm_res(lambda k, t: YT.t[:, k, t * 128:(t + 1) * 128], YT.d, 16, wo_d)
            for t in range(4):
                emit_norm_transpose(C, X1.t[:, t, :], X1.d, XN, SQ, SS, RSTD, GAIN, ident, CST, HNT, t * 128, (0, 1, 2, 3))
            for f in range(NF):
                wg = WG[f % 3]; wu = WU[f % 3]; sg = SG[f % 2]
                S.dma("sp", lambda e, wg=wg, f=f: e.dma_start(out=wg.t[:], in_=wg_d[f].rearrange("p (k c) -> p k c", k=16)), writes=[wg.d])
                S.dma("act", lambda e, wu=wu, f=f: e.dma_start(out=wu.t[:], in_=wu_d[f].rearrange("p (k c) -> p k c", k=16)), writes=[wu.d])
                ba = bank[4 + (f % 2) * 2]; bb = bank[5 + (f % 2) * 2]
                for k in range(16):
                    S.op("pe", lambda e, ba=ba, wg=wg, k=k: e.matmul(ba.t[:, :], lhsT=wg.t[:, k, :], rhs=HNT.t[:, k, :], start=(k == 0), stop=(k == 15)),
                         reads=[wg.d, HNT.d], writes=[ba.d], signal=(k == 15))
                for k in range(16):
                    S.op("pe", lambda e, bb=bb, wu=wu, k=k: e.matmul(bb.t[:, :], lhsT=wu.t[:, k, :], rhs=HNT.t[:, k, :], start=(k == 0), stop=(k == 15)),
                         reads=[wu.d, HNT.d], writes=[bb.d], signal=(k == 15))
                S.op("act", lambda e, ba=ba, sg=sg: e.activation(out=sg.t[:], in_=ba.t[:, :], func=AF.Silu), reads=[ba.d], writes=[sg.d])
                S.op("dve", lambda e, bb=bb, sg=sg, f=f: e.tensor_tensor(out=HT.t[:, f, :], in0=bb.t[:, :], in1=sg.t[:], op=ALU.mult),
                     reads=[bb.d, sg.d], writes=[HT.d])
            gemm_res(lambda k, t: HT.t[:, k, t * 128:(t + 1) * 128], HT.d, NF, wd_d)
            o = T("out"); C.outs.append(o)
            S.dma("pool", lambda e, t0=t0: e.dma_start(out=xo[t0:t0 + 512, :].rearrange("(t p) n -> p t n", p=128), in_=X1.t[:]),
                  reads=[X1.d], writes=[o])


def gate_layout(w):
    return np.ascontiguousarray(w.reshape(16, 128, DFF // 128, 128).transpose(2, 1, 0, 3)).reshape(DFF // 128, 128, 2048)


NL = 2
W_SHAPES = [(D, NPROJ), (D, NPROJ), (D, D), (DFF // 128, 128, 2048), (DFF // 128, 128, 2048), (DFF, D)]
W_SIZES = [int(np.prod(sh)) for sh in W_SHAPES]
LW = sum(W_SIZES) // 128
TW = 3392


def build_fused(NCH=SEQ // 128):
    st = ExitStack()
    with st:
        C = Ctx(st)
        nc, S = C.nc, C.S
        Stok = NCH * 128
        x_in = C.din("x", [Stok, D])
        cst_d = C.din("cst", [128, 512])
        rope_d = C.din("rope", [Stok, 128])
        arope_d = C.din("arope", [Stok, 128])
        sprm_d = C.din("ssd_prm", [NL * 2 * 128, 64])
        retc_d = C.din("ret_c", [NL * 2 * 128, 1024])
        attc_d = C.din("att_c", [NL * 128, 1024])
        gains_d = C.din("gains", [NL * 2 * 128, 16])
        NTILE = LW * 128 // (128 * TW)
        wf = [C.din("wf%d" % l, [NTILE * 128, TW]) for l in range(NL)]
        out = C.dout("out", [Stok // 2, D])
        gidx_d = C.din("gidx", [128, (Stok // 1024) * 20], mybir.dt.uint32)
        wbs = [C.scratch("wb%d" % l, [NTILE * 128, TW], BF16) for l in range(NL)]
        projs = [C.scratch("proj_s%d" % h, [Stok, NPROJ]) for h in range(2)]
        yT = C.scratch("yT_s", [D, Stok], BF16)
        xs = C.scratch("xs", [Stok, D])
        wvs = []
        for l in range(NL):
            wbf = wbs[l].rearrange("p n -> (p n)")
            wv, off = [], 0
            for sh, n in zip(W_SHAPES, W_SIZES):
                v = wbf[off:off + n]
                if len(sh) == 2:
                    v = v.rearrange("(r c) -> r c", c=sh[1])
                else:
                    v = v.rearrange("(f p c) -> f p c", p=sh[1], c=sh[2])
                wv.append(v); off += n
            wvs.append(wv)
        n_in = -(-(2 * W_SIZES[0]) // (128 * TW))
        C.run_phase(lambda: emit_wconv(C, list(range(n_in)), wf[0], wbs[0]))
        for l in range(NL):
            xl = x_in if l == 0 else xs
            xo = out if l == NL - 1 else xs
            wv = wvs[l]
            g_mix = gains_d[(l * 2) * 128:(l * 2 + 1) * 128, :]
            for h in range(2):
                C.run_phase(lambda: emit_inproj(C, NCH, xl, cst_d, g_mix, wv[h], projs[h]))
            rr = [(l * 2 + h) * 128 for h in range(2)]
            C.run_phase(lambda: emit_multi(C, NCH, [
                (lambda h=h: make_ssd(C, NCH, projs[h], cst_d, sprm_d[rr[h]:rr[h] + 128, :], yT[h * 512:(h + 1) * 512, :])) for h in range(2)],
                [[0, 1, 2, 3, 0, 1, 3, 2], [4, 5, 6, 7, 4, 5, 7, 6]],
                extra=([lambda: make_wconv(C, list(range(n_in, NTILE)), -(-(NTILE - n_in) // NCH), wf[0], wbs[0])] if l == 0 else [])))
            for h in range(2):
                C.run_phase(lambda: emit_att(C, NCH, projs[h], cst_d, attc_d[l * 128:(l + 1) * 128, :], arope_d,
                                             yT[1024 + h * 256:1024 + (h + 1) * 256, :]))
            C.run_phase(lambda: emit_multi(C, NCH, [
                (lambda h=h: make_ret(C, NCH, projs[h], cst_d, retc_d[rr[h]:rr[h] + 128, :], rope_d,
                                      yT[1536 + h * 256:1536 + (h + 1) * 256, :])) for h in range(2)],
                [[0, 1, 3, 1, 2, 2, 0, 0], [4, 5, 7, 5, 6, 6, 4, 4]],
                extra=([lambda: make_wconv(C, list(range(NTILE)), -(-NTILE // NCH), wf[l + 1], wbs[l + 1])] if l + 1 < NL else [])))
            g_ffn = gains_d[(l * 2 + 1) * 128:(l * 2 + 2) * 128, :]
            if l == NL - 1:
                C.run_phase(lambda: emit_outffn(C, Stok // 2, xl, yT, cst_d, g_ffn, wv[2], wv[3], wv[4], wv[5], out, gidx_d=gidx_d))
            else:
                C.run_phase(lambda: emit_outffn(C, Stok, xl, yT, cst_d, g_ffn, wv[2], wv[3], wv[4], wv[5], xo))
        return nc


def fused_inputs(inputs, S_=SEQ):
    x = np.ascontiguousarray(inputs["x"], dtype=np.float32)
    common = {"cst": const_mats(), "rope": rope_table(S_), "arope": att_rope_table(S_)}
    sprm = np.concatenate([ssd_params(inputs["conv_w"][l], inputs["conv_b"][l], inputs["dt_bias"][l], inputs["a_log"][l],
                                      inputs["d_skip"][l], inputs["ssd_norm"][l], h) for l in range(NL) for h in range(2)], axis=0)
    retc = np.concatenate([ret_consts(inputs["ret_norm"][l], h) for l in range(NL) for h in range(2)], axis=0)
    attc = np.concatenate([att_consts(inputs["q_norm"][l], inputs["k_norm"][l]) for l in range(NL)], axis=0)
    gains = np.concatenate([np.ascontiguousarray(inputs[k][l].reshape(16, 128).T) for l in range(NL) for k in ("ln_mix", "ln_ffn")], axis=0)
    common.update({"ssd_prm": sprm, "ret_c": retc, "att_c": attc, "gains": gains.astype(np.float32)})
    for l in range(NL):
        w_in = inputs["w_in"][l]
        lay = [w_in[:, half_cols(0)], w_in[:, half_cols(1)], inputs["w_out"][l], gate_layout(inputs["w_gate"][l]),
               gate_layout(inputs["w_up"][l]), inputs["w_down"][l]]
        common["wf%d" % l] = np.concatenate([np.asarray(a, np.float32).reshape(-1) for a in lay]).reshape(-1, TW)
    maps = []
    nbt = S_ // 512; nbh = nbt // 2
    p = np.arange(128, dtype=np.int64)
    for c in range(NCORES):
        m = dict(common); m["x"] = np.ascontiguousarray(x[c // 2, :S_])
        half = c % 2
        g = np.zeros((128, nbh * 20), np.int64)
        for b in range(nbh):
            for t in range(4):
                g[:, b * 20 + t] = half * (S_ // 2) + b * 512 + t * 128 + p
            for k in range(16):
                g[:, b * 20 + 4 + k] = (k * 128 + p) * nbt + (half * nbh + b)
        m["gidx"] = g.astype(np.uint32)
        maps.append(m)
    return maps


NCORES = 8


def kernel(**inputs):
    inputs = {k: np.asarray(v) for k, v in inputs.items()}
    nc = build_fused()
    res = run_bass_kernel_spmd(nc, fused_inputs(inputs), core_ids=list(range(NCORES))).results
    return np.ascontiguousarray(np.stack([np.concatenate([np.asarray(res[2 * b]["out"]), np.asarray(res[2 * b + 1]["out"])], axis=0)
                                          for b in range(NB)]), dtype=np.float32)
```
